# Optimizing a Trainium2 kernel written in Bass

```python
import math
import jax, jax.numpy as jnp
from jax import lax
import numpy as np


D_MODEL = 1024
BATCH = 4
SEQ = 8192
DEPTH = 2

A_HEADS = 4
A_HEAD_DIM = 64
A_V_DIM = 2 * A_HEAD_DIM
A_WIDTH = A_HEADS * A_V_DIM
A_COLS = 3 * A_WIDTH
R_HEAD = 64
R_WIDTH = D_MODEL - A_WIDTH
R_HEADS = R_WIDTH // R_HEAD
N_DIR = 2
DECAY_LORA = 64
ICL_LORA = 64
GATE_LORA = 128
R_COLS = 3 * R_WIDTH + N_DIR * (DECAY_LORA + ICL_LORA) + GATE_LORA
R_SPLITS = (R_WIDTH, 2 * R_WIDTH, 3 * R_WIDTH,
            3 * R_WIDTH + N_DIR * DECAY_LORA,
            3 * R_WIDTH + N_DIR * (DECAY_LORA + ICL_LORA))
IN_COLS = A_COLS + R_COLS
D_FF = 4 * D_MODEL
ROPE_THETA = 10000.0
Q_BLOCK = 128
NORM_EPS = 1e-6
SUBLN_EPS = 1e-5
GN_EPS = 64e-5
N_MOD = 6

kernel_name = 'hybrid_diffattn_rwkv7_adaln_encoder'


def rmsnorm(x, g, eps=NORM_EPS):
    x32 = x.astype(jnp.float32)
    y = x32 * lax.rsqrt(jnp.mean(x32 * x32, axis=-1, keepdims=True) + eps)
    return (y * g.astype(jnp.float32)).astype(x.dtype)


def rope_tables(seq, dim):
    inv = 1.0 / (ROPE_THETA ** (jnp.arange(0, dim, 2, dtype=jnp.float32) / dim))
    ang = jnp.arange(seq, dtype=jnp.float32)[:, None] * inv[None, :]
    return jnp.cos(ang), jnp.sin(ang)


def apply_rope(t, cos, sin):
    t32 = t.astype(jnp.float32)
    half = t.shape[-1] // 2
    t1, t2 = t32[..., :half], t32[..., half:]
    c = cos[None, :, None, None, :]
    s = sin[None, :, None, None, :]
    return jnp.concatenate([t1 * c - t2 * s, t1 * s + t2 * c], axis=-1).astype(t.dtype)


def diff_attention(zq, zk, zv, lam_q1, lam_k1, lam_q2, lam_k2, subln_w, lam_init):
    f32 = jnp.float32
    B, S, _ = zq.shape
    q = zq.reshape(B, S, A_HEADS, 2, A_HEAD_DIM)
    k = zk.reshape(B, S, A_HEADS, 2, A_HEAD_DIM)
    v = zv.reshape(B, S, A_HEADS, A_V_DIM)
    cos, sin = rope_tables(S, A_HEAD_DIM)
    q = apply_rope(q, cos, sin)
    k = apply_rope(k, cos, sin)
    lam = (jnp.exp(jnp.sum(lam_q1.astype(f32) * lam_k1.astype(f32)))
           - jnp.exp(jnp.sum(lam_q2.astype(f32) * lam_k2.astype(f32))) + lam_init)
    n_blocks = S // Q_BLOCK
    qb = q.reshape(B, n_blocks, Q_BLOCK, A_HEADS, 2, A_HEAD_DIM).swapaxes(0, 1)
    scale = A_HEAD_DIM ** -0.5

    def block(q_blk):
        s = jnp.einsum('bqhcd,bkhcd->bhcqk', q_blk, k).astype(f32) * scale
        p = jax.nn.softmax(s, axis=-1)
        w = p[:, :, 0] - lam * p[:, :, 1]
        return jnp.einsum('bhqk,bkhe->bqhe', w.astype(v.dtype), v)

    o = lax.map(block, qb)
    o = o.swapaxes(0, 1).reshape(B, S, A_HEADS, A_V_DIM).astype(f32)
    o = o * lax.rsqrt(jnp.mean(o * o, axis=-1, keepdims=True) + SUBLN_EPS)
    o = o * subln_w.astype(f32) * (1.0 - lam_init)
    return o.reshape(B, S, A_WIDTH).astype(zq.dtype)


def centred_shift_mix(z, mu):
    zp = jnp.pad(z, ((0, 0), (1, 1), (0, 0)))
    nb = 0.5 * (zp[:, :-2] + zp[:, 2:])
    return z + mu.astype(z.dtype) * (nb - z)


def rwkv7_step(state, inp):
    r_t, w_t, k_t, v_t, kk_t, a_t = inp
    sa = jnp.einsum('dbhvk,dbhk->dbhv', state, kk_t)
    state = (state * w_t[..., None, :]
             - sa[..., :, None] * (kk_t * a_t)[..., None, :]
             + v_t[..., :, None] * k_t[..., None, :])
    y = jnp.einsum('dbhvk,dbhk->dbhv', state, r_t)
    return state, y


def rwkv7_bidirectional(z, mu, w0, w2, a0, a2, g2, k_k, k_a, r_k, lnx_w, lnx_b):
    f32 = jnp.float32
    B, S, _ = z.shape
    z = centred_shift_mix(z, mu)
    r, k, v, zd, za, zg = jnp.split(z.astype(f32), R_SPLITS, axis=-1)
    zd = zd.reshape(B, S, N_DIR, DECAY_LORA)
    za = za.reshape(B, S, N_DIR, ICL_LORA)
    wl = w0.astype(f32) + jnp.einsum('bsdr,drc->bsdc', jnp.tanh(zd), w2.astype(f32))
    wl = -jax.nn.softplus(-wl) - 0.5
    decay = jnp.exp(-jnp.exp(wl))
    a = jax.nn.sigmoid(a0.astype(f32) + jnp.einsum('bsdr,drc->bsdc', za, a2.astype(f32)))
    g = jnp.einsum('bsr,rc->bsc', jax.nn.sigmoid(zg), g2.astype(f32))
    heads = lambda t: t.reshape(t.shape[:-1] + (R_HEADS, R_HEAD))
    kk = heads(k * k_k.astype(f32))
    kk = kk / jnp.maximum(jnp.sqrt(jnp.sum(kk * kk, axis=-1, keepdims=True)), 1e-12)
    kd = k[:, :, None, :] * (1.0 + (a - 1.0) * k_a.astype(f32))
    rh, vh = heads(r), heads(v)
    decay_h, a_h, kd_h = heads(decay), heads(a), heads(kd)

    def dir_shared(t):
        return jnp.stack([t, jnp.flip(t, 1)], 0).transpose(2, 0, 1, 3, 4)

    def dir_own(t):
        return jnp.stack([t[:, :, 0], jnp.flip(t[:, :, 1], 1)], 0).transpose(2, 0, 1, 3, 4)

    xs = (dir_shared(rh), dir_own(decay_h), dir_own(kd_h),
          dir_shared(vh), dir_shared(kk), dir_own(a_h))
    state0 = jnp.zeros((N_DIR, B, R_HEADS, R_HEAD, R_HEAD), f32)
    _, ys = lax.scan(rwkv7_step, state0, xs)
    y = (ys[:, 0] + jnp.flip(ys[:, 1], 0)).transpose(1, 0, 2, 3)
    mean = jnp.mean(y, axis=-1, keepdims=True)
    var = jnp.mean(jnp.square(y - mean), axis=-1, keepdims=True)
    y = ((y - mean) * lax.rsqrt(var + GN_EPS)).reshape(B, S, R_WIDTH)
    y = y * lnx_w.astype(f32) + lnx_b.astype(f32)
    bonus = jnp.sum(rh[:, :, None] * kd_h * r_k.astype(f32), axis=-1, keepdims=True) * vh[:, :, None]
    y = y + jnp.sum(bonus, axis=2).reshape(B, S, R_WIDTH)
    return (y * g).astype(z.dtype)


def setup_inputs(seed: int = 0) -> dict:
    key = jax.random.key(seed)
    ks = iter(jax.random.split(key, 32))
    L, D, f32 = DEPTH, D_MODEL, jnp.float32

    def nrm(shape, scale):
        return jax.random.normal(next(ks), shape, f32) * scale

    x = nrm((BATCH, SEQ, D), 1.0)
    c = nrm((BATCH, D), 1.0)
    w_ada = nrm((L, D, N_MOD * D), 0.5 * D ** -0.5)
    b_ada = nrm((L, N_MOD * D), 0.02)
    norm1 = 1.0 + nrm((L, D), 0.02)
    norm2 = 1.0 + nrm((L, D), 0.02)
    w_in = nrm((L, D, IN_COLS), D ** -0.5)
    w_out = nrm((L, D, D), D ** -0.5)
    lam_q1 = nrm((L, A_HEAD_DIM), 0.1)
    lam_k1 = nrm((L, A_HEAD_DIM), 0.1)
    lam_q2 = nrm((L, A_HEAD_DIM), 0.1)
    lam_k2 = nrm((L, A_HEAD_DIM), 0.1)
    subln_w = 1.0 + nrm((L, A_V_DIM), 0.02)
    tshift_mu = jax.random.uniform(next(ks), (L, R_COLS), f32)
    ramp = (jnp.arange(R_WIDTH, dtype=f32) / (R_WIDTH - 1)) ** 0.9
    decay_w0 = (-6.0 + 5.0 * ramp + 0.5)[None, None, :] + nrm((L, N_DIR, R_WIDTH), 0.3)
    decay_w2 = nrm((L, N_DIR, DECAY_LORA, R_WIDTH), 0.5 * DECAY_LORA ** -0.5)
    icl_a0 = nrm((L, N_DIR, R_WIDTH), 0.3)
    icl_a2 = nrm((L, N_DIR, ICL_LORA, R_WIDTH), ICL_LORA ** -0.5)
    gate_g2 = nrm((L, GATE_LORA, R_WIDTH), GATE_LORA ** -0.5)
    k_k = 0.85 + nrm((L, R_WIDTH), 0.05)
    k_a = 1.0 + nrm((L, R_WIDTH), 0.05)
    r_k = nrm((L, R_HEADS, R_HEAD), 0.1)
    lnx_w = 1.0 + nrm((L, R_WIDTH), 0.02)
    lnx_b = nrm((L, R_WIDTH), 0.02)
    w_up = nrm((L, D, D_FF), D ** -0.5)
    w_down = nrm((L, D_FF, D), D_FF ** -0.5)
    norm_f = 1.0 + nrm((D,), 0.02)
    return {'x': x, 'c': c, 'w_ada': w_ada, 'b_ada': b_ada, 'norm1': norm1, 'norm2': norm2,
            'w_in': w_in, 'w_out': w_out, 'lam_q1': lam_q1, 'lam_k1': lam_k1,
            'lam_q2': lam_q2, 'lam_k2': lam_k2, 'subln_w': subln_w, 'tshift_mu': tshift_mu,
            'decay_w0': decay_w0, 'decay_w2': decay_w2, 'icl_a0': icl_a0, 'icl_a2': icl_a2,
            'gate_g2': gate_g2, 'k_k': k_k, 'k_a': k_a, 'r_k': r_k, 'lnx_w': lnx_w,
            'lnx_b': lnx_b, 'w_up': w_up, 'w_down': w_down, 'norm_f': norm_f}


def reference(x, c, w_ada, b_ada, norm1, norm2, w_in, w_out, lam_q1, lam_k1, lam_q2, lam_k2,
              subln_w, tshift_mu, decay_w0, decay_w2, icl_a0, icl_a2, gate_g2, k_k, k_a, r_k,
              lnx_w, lnx_b, w_up, w_down, norm_f):
    for l in range(DEPTH):
        lam_init = 0.8 - 0.6 * math.exp(-0.3 * l)
        mod = jax.nn.silu(c) @ w_ada[l] + b_ada[l]
        sh1, sc1, gt1, sh2, sc2, gt2 = [m[:, None, :] for m in jnp.split(mod, N_MOD, axis=-1)]
        h = rmsnorm(x, norm1[l]) * (1.0 + sc1) + sh1
        proj = jnp.einsum('bsd,dc->bsc', h, w_in[l])
        zq = proj[..., :A_WIDTH]
        zk = proj[..., A_WIDTH:2 * A_WIDTH]
        zv = proj[..., 2 * A_WIDTH:A_COLS]
        zr = proj[..., A_COLS:]
        att = diff_attention(zq, zk, zv, lam_q1[l], lam_k1[l], lam_q2[l], lam_k2[l],
                             subln_w[l], lam_init)
        rec = rwkv7_bidirectional(zr, tshift_mu[l], decay_w0[l], decay_w2[l], icl_a0[l],
                                  icl_a2[l], gate_g2[l], k_k[l], k_a[l], r_k[l],
                                  lnx_w[l], lnx_b[l])
        mix = jnp.einsum('bsd,de->bse', jnp.concatenate([att, rec], axis=-1), w_out[l])
        x = x + gt1 * mix
        h = rmsnorm(x, norm2[l]) * (1.0 + sc2) + sh2
        u = jnp.square(jax.nn.relu(jnp.einsum('bsd,df->bsf', h, w_up[l])))
        x = x + gt2 * jnp.einsum('bsf,fd->bsd', u, w_down[l])
    return rmsnorm(x, norm_f)
```

```python
import numpy as np
import concourse.bass as bass
import concourse.mybir as mybir

F32 = mybir.dt.float32
BF16 = mybir.dt.bfloat16
I32 = mybir.dt.int32
AF = mybir.ActivationFunctionType
ALU = mybir.AluOpType
AX = mybir.AxisListType

EPOCH = 20000
NDMASEM = 24


class Prog:
    ENGS = ("pe", "dve", "act", "pool", "sp")

    def __init__(self, nc):
        self.nc = nc
        self.ins = []
        self.lastw = {}
        self.readers = {}
        self.ndma = 0
        self.fence_deps = set()
        self.fence_pending = set()
        self.last_eng = {}
        self.dma_since = []

    def fence(self):
        self.fence_deps = set(self.last_eng.values()) | set(self.dma_since)
        self.fence_pending = set(self.ENGS)
        self.dma_since = []

    def _add(self, eng, fn, reads, writes, dma):
        i = len(self.ins)
        excl = [k for k in reads if (isinstance(k, tuple) and k[0] == "ps") or k == "pbt"]
        writes = list(writes) + [k for k in excl if k not in writes]
        deps = set()
        for k in list(reads) + list(writes):
            if k in self.lastw:
                deps.add(self.lastw[k])
        for k in writes:
            for r in self.readers.get(k, ()):
                deps.add(r)
        if eng in self.fence_pending:
            deps |= self.fence_deps
            self.fence_pending.discard(eng)
        deps.discard(i)
        if dma:
            self.dma_since.append(i)
        else:
            self.last_eng[eng] = i
        for k in writes:
            self.lastw[k] = i
            self.readers[k] = []
        for k in reads:
            self.readers.setdefault(k, []).append(i)
        self.ins.append(dict(eng=eng, fn=fn, deps=deps, dma=dma))
        return i

    def op(self, eng, fn, reads=(), writes=()):
        return self._add(eng, fn, reads, writes, False)

    def dma(self, eng, fn, reads=(), writes=()):
        return self._add(eng, fn, reads, writes, True)

    def coll(self, fn, reads=(), writes=()):
        i = self._add("pool", fn, reads, writes, True)
        self.ins[i]["own"] = True
        return i

    def mm(self, out, lhsT, rhs, start=True, stop=True, reads=(), writes=(), **kw):
        return self.op("pe", lambda e: e.matmul(out, lhsT, rhs, start=start, stop=stop, **kw), reads, writes)

    def transpose(self, out, in_, ident, reads=(), writes=()):
        return self.op("pe", lambda e: e.transpose(out, in_, ident), reads, writes)

    def act(self, out, in_, func, reads=(), writes=(), eng="act", **kw):
        return self.op(eng, lambda e: e.activation(out, in_, func, **kw), reads, writes)

    def tt(self, eng, out, in0, in1, op, reads=(), writes=()):
        return self.op(eng, lambda e: e.tensor_tensor(out, in0, in1, op), reads, writes)

    def ts(self, eng, out, in0, s1, s2, op0, op1=None, reads=(), writes=(), **kw):
        if op1 is None:
            return self.op(eng, lambda e: e.tensor_scalar(out, in0, s1, None, op0, **kw), reads, writes)
        return self.op(eng, lambda e: e.tensor_scalar(out, in0, s1, s2, op0, op1, **kw), reads, writes)

    def stt(self, out, in0, scalar, in1, op0, op1, reads=(), writes=(), **kw):
        return self.op("dve", lambda e: e.scalar_tensor_tensor(out, in0, scalar, in1, op0, op1, **kw), reads, writes)

    def copy(self, eng, out, in_, reads=(), writes=()):
        if eng == "act":
            return self.op(eng, lambda e: e.copy(out, in_), reads, writes)
        return self.op(eng, lambda e: e.tensor_copy(out, in_), reads, writes)

    def memset(self, eng, ap, val, writes=()):
        return self.op(eng, lambda e: e.memset(ap, val), (), writes)

    def load(self, eng, out, in_, reads=(), writes=(), **kw):
        return self.dma(eng, lambda e: e.dma_start(out=out, in_=in_, **kw), reads, writes)

    def emit(self, final_wait_keys=()):
        nc = self.nc
        ins = self.ins
        fin_deps = set()
        for k in final_wait_keys:
            if k in self.lastw:
                fin_deps.add(self.lastw[k])
        needed = set()
        for r in ins:
            for d in r["deps"]:
                if ins[d]["eng"] == "pe" and r["eng"] == "pe" and not ins[d]["dma"] and not r["dma"]:
                    continue
                needed.add(d)
        needed |= fin_deps
        cnt = {e: 0 for e in self.ENGS}
        dmacnt = [0] * NDMASEM
        ndma = 0
        nown = 0
        for i, r in enumerate(ins):
            if r["dma"] and r.get("own"):
                r["dsem"] = NDMASEM + nown
                nown += 1
                r["dprev"] = 0
                r["dtarget"] = 1
            elif r["dma"]:
                s = ndma % NDMASEM
                ndma += 1
                r["dsem"] = s
                r["dprev"] = dmacnt[s]
                dmacnt[s] += 16
                r["dtarget"] = dmacnt[s]
            elif i in needed:
                cnt[r["eng"]] += 1
                r["ms"] = cnt[r["eng"]]
        nep = {e: (cnt[e] // EPOCH) + 1 for e in self.ENGS}
        import contextlib
        st = contextlib.ExitStack()
        csem = {e: [st.enter_context(nc.semaphore(f"c_{e}_{k}")) for k in range(nep[e])] for e in self.ENGS if e != "sp"}
        dsem = [st.enter_context(nc.semaphore(f"d_{k}")) for k in range(NDMASEM + nown)]
        per = {e: [] for e in self.ENGS}
        for i, r in enumerate(ins):
            per[r["eng"]].append(i)
        block = st.enter_context(nc.Block())

        def waits_for(i, known):
            r = ins[i]
            out = []
            deps = set(r["deps"])
            for d in sorted(deps):
                dr = ins[d]
                if dr["dma"]:
                    key = ("d", dr["dsem"])
                    val = dr["dtarget"]
                    sem = dsem[dr["dsem"]]
                else:
                    if dr["eng"] == "pe" and r["eng"] == "pe" and not r["dma"]:
                        continue
                    m = dr["ms"]
                    ep = (m - 1) // EPOCH
                    key = ("c", dr["eng"], ep)
                    val = m - ep * EPOCH
                    sem = csem[dr["eng"]][ep]
                    if any(k2[0] == "c" and k2[1] == dr["eng"] and k2[2] > ep for k2 in known):
                        continue
                if known.get(key, 0) >= val:
                    continue
                known[key] = val
                out.append((sem, val))
            if r["dma"]:
                key = ("d", r["dsem"])
                if r["dprev"] > 0 and known.get(key, 0) < r["dprev"]:
                    known[key] = r["dprev"]
                    out.append((dsem[r["dsem"]], r["dprev"]))
            return out

        def body(engname):
            def f(eng):
                known = {}
                for i in per[engname]:
                    r = ins[i]
                    for sem, val in waits_for(i, known):
                        eng.wait_ge(sem, val)
                    inst = r["fn"](eng)
                    if r["dma"] and r.get("own"):
                        inst.then_inc(dsem[r["dsem"]])
                    elif r["dma"]:
                        inst.then_inc(dsem[r["dsem"]], 16)
                    elif "ms" in r:
                        ep = (r["ms"] - 1) // EPOCH
                        inst.then_inc(csem[engname][ep], 1)
                if engname == "sp":
                    for d in sorted(fin_deps):
                        dr = ins[d]
                        if dr["dma"]:
                            eng.wait_ge(dsem[dr["dsem"]], dr["dtarget"])
                        else:
                            ep = (dr["ms"] - 1) // EPOCH
                            eng.wait_ge(csem[dr["eng"]][ep], dr["ms"] - ep * EPOCH)
            return f

        block.tensor(body("pe"))
        block.vector(body("dve"))
        block.scalar(body("act"))
        block.gpsimd(body("pool"))
        block.sync(body("sp"))
        st.close()
        return {e: len(per[e]) for e in self.ENGS}

import contextlib
from concourse.bass_utils import run_bass_kernel_spmd
import ml_dtypes

D = 1024
DECAY_C = 0.6065306597126334


class Ctx:
    def __init__(self, nc):
        self.nc = nc
        self.st = contextlib.ExitStack()
        self.P = Prog(nc)
        self.alt = 0
        self.cur = self.st
        self.pfx = ""
        self.banks = None
        self.pbt = None

    def alloc_psum(self):
        self.banks = [self.ps(f"bank{i}", [128, 512]) for i in range(7)]
        self.pbt = self.ps("pbt", [128, 1024], BF16)

    @contextlib.contextmanager
    def scope(self, pfx):
        self.P.fence()
        old = (self.cur, self.pfx)
        with contextlib.ExitStack() as stk:
            self.cur, self.pfx = stk, pfx
            yield
            self.P.fence()
        self.cur, self.pfx = old

    def sb(self, name, shape, dt=F32):
        return self.cur.enter_context(self.nc.sbuf_tensor(self.pfx + name, shape, dt))

    def ps(self, name, shape, dt=F32):
        return self.st.enter_context(self.nc.psum_tensor(name, shape, dt))

    def evac_eng(self):
        self.alt ^= 1
        return "dve" if self.alt else "act"


def dram(nc, name, shape, dt, kind):
    return nc.dram_tensor(name, shape, dt, kind=kind).ap()


def bk(i):
    return [("ps", i)]


class Chunked:
    def __init__(self, aps, W):
        self.aps, self.W = aps, W

    def sl(self, r0, r1, t0, t1):
        k = t0 // self.W
        assert (t1 - 1) // self.W == k
        return self.aps[k][r0:r1, t0 - k * self.W:t1 - k * self.W]

    def tile_pkt(self, t0, t1):
        k = t0 // self.W
        assert (t1 - 1) // self.W == k
        return self.aps[k].rearrange("(k p) t -> p k t", p=128)[:, :, t0 - k * self.W:t1 - k * self.W]


def as_chunked(x, S):
    return x if isinstance(x, Chunked) else Chunked([x], S)


def emit_mod(cx, wada, bada, sc, noc, wa, pm, mod, tag):
    P = cx.P
    wv = wada.rearrange("(k p) c -> p k c", p=128)
    for oc in range(noc):
        b = oc % 2
        P.load("sp", wa[:, b, :, :], wv[:, :, oc * 128:(oc + 1) * 128],
               writes=[(tag + "wa", b)])
        for k in range(8):
            P.mm(pm[:, oc, :], wa[:, b, k, :], sc[:, k, :], start=(k == 0), stop=(k == 7),
                 reads=[(tag + "wa", b), tag + "sc"], writes=bk(6))
    P.tt("dve", mod[:, 0:noc], pm[:, 0:noc, 0], bada, ALU.add, reads=bk(6) + [tag + "bada"], writes=[tag + "mod"])


def emit_rstd(cx, xt, nk, ncol, sqb, onesD, psr_ap, psr_keys, rs, rstd, xkey, tag, eps):
    P = cx.P
    for k in range(nk):
        b = k % 2
        P.act(sqb[:, b, :], xt[:, k, :], AF.Square, reads=[xkey], writes=[(tag + "sq", b)])
        P.mm(psr_ap, onesD[:], sqb[:, b, :], start=(k == 0), stop=(k == nk - 1),
             reads=[(tag + "sq", b), "onesD"], writes=psr_keys)
    P.act(rs[:], psr_ap, AF.Sqrt, bias=eps, reads=psr_keys, writes=[tag + "rs"])
    P.op("dve", lambda e: e.reciprocal(rstd[:], rs[:]), reads=[tag + "rs"], writes=[tag + "rstd"])


def emit_A(cx, d, T, noc, pfx):
    nc = cx.nc
    P = cx.P
    xT, ccol, wada, bada, n1, win, ZT = (d[k] for k in ("xT", "ccol", "wada", "bada", "n1", "win", "ZT"))
    NT = T // 512
    gs = 9 if noc == 27 else 5
    npc = 3 if noc == 27 else 2
    pw = noc * 128 // npc
    with cx.scope(pfx):
        banks = cx.banks
        pm = banks[6][:, 0:64].rearrange("p (o t) -> p o t", t=2)
        sc = cx.sb("sc", [128, 8, 2]); wa = cx.sb("wa", [128, 2, 8, 128]); mod = cx.sb("mod", [128, 16])
        badas = cx.sb("badas", [128, 16]); n1s = cx.sb("n1s", [128, 8]); gp = cx.sb("gp", [128, 8])
        onesD = cx.sb("onesD", [128, 128]); Wb = cx.sb("Wb", [128, 8, noc * 128], BF16)
        wtmp = cx.sb("wtmp", [128, 2, pw]); xt = cx.sb("xt", [128, 2, 8, 512])
        sqb = cx.sb("sqb", [128, 2, 512]); rs = cx.sb("rs", [128, 512]); rstd = cx.sb("rstd", [128, 512])
        tmp = cx.sb("tmp", [128, 2, 512]); hb = cx.sb("hb", [128, 8, 512], BF16)
        zst = cx.sb("zst", [128, 2, gs, 512], BF16)
        P.load("sp", sc[:], ccol, writes=["Asc"])
        P.load("sp", badas[:], bada, writes=["Abada"])
        P.load("sp", n1s[:], n1, writes=["n1s"])
        P.memset("pool", onesD[:], 1.0 / D, writes=["onesD"])
        P.act(sc[:], sc[:], AF.Silu, reads=["Asc"], writes=["Asc"])
        emit_mod(cx, wada, badas[:], sc, 16, wa, pm, mod, "A")
        P.stt(gp[:], mod[:, 8:16], 1.0, n1s[:], ALU.add, ALU.mult, reads=["Amod", "n1s"], writes=["gp"])
        wv = win.rearrange("(k p) c -> p k c", p=128)
        i = 0
        for k in range(8):
            for pc in range(npc):
                b = i % 2
                P.load("sp", wtmp[:, b, :], wv[:, k, pc * pw:(pc + 1) * pw], writes=[("wtmp", b)])
                P.copy(cx.evac_eng(), Wb[:, k, pc * pw:(pc + 1) * pw], wtmp[:, b, :], reads=[("wtmp", b)], writes=["Wb"])
                i += 1
        xv = xT.rearrange("(k p) t -> p k t", p=128)
        zv = ZT.rearrange("(o p) t -> p o t", p=128)
        bi = 0
        for it in range(NT):
            xb = it % 2
            cols = slice(it * 512, (it + 1) * 512)
            P.load("sp", xt[:, xb, :, :], xv[:, :, cols], writes=[("xt", xb)])
            emit_rstd(cx, xt[:, xb], 8, 512, sqb, onesD, banks[6][:], bk(6), rs, rstd, ("xt", xb), "A", 1e-6)
            for k in range(8):
                b = k % 2
                P.tt("dve", tmp[:, b, :], xt[:, xb, k, :], rstd[:], ALU.mult, reads=[("xt", xb), "Arstd"], writes=[("tmp", b)])
                P.ts("pool", hb[:, k, :], tmp[:, b, :], gp[:, k:k + 1], mod[:, k:k + 1], ALU.mult, ALU.add,
                     reads=[("tmp", b), "gp", "Amod"], writes=["hb"])
            for oc in range(noc):
                bnk = bi % 6
                bi += 1
                for k in range(8):
                    P.mm(banks[bnk][:], Wb[:, k, oc * 128:(oc + 1) * 128], hb[:, k, :], start=(k == 0), stop=(k == 7),
                         reads=["Wb", "hb"], writes=bk(bnk))
                g = (it * 3 + oc // gs) % 2
                P.copy(cx.evac_eng(), zst[:, g, oc % gs, :], banks[bnk][:], reads=bk(bnk), writes=[("zst", g)])
                if oc % gs == gs - 1:
                    P.load("sp", zv[:, oc - gs + 1:oc + 1, cols], zst[:, g, :, :], reads=[("zst", g)], writes=["ZT"])


def build_A(T, noc=27):
    nc = bass.Bass("TRN2", target_bir_lowering=False)
    d = dict(xT=dram(nc, "xT", [D, T], F32, "ExternalInput"), ccol=dram(nc, "ccol", [128, 8, 2], F32, "ExternalInput"),
             wada=dram(nc, "wada", [D, 2048], F32, "ExternalInput"), bada=dram(nc, "bada", [128, 16], F32, "ExternalInput"),
             n1=dram(nc, "n1", [128, 8], F32, "ExternalInput"), win=dram(nc, "win", [D, noc * 128], F32, "ExternalInput"),
             ZT=dram(nc, "ZT", [noc * 128, T], BF16, "ExternalOutput"))
    cx = Ctx(nc)
    with cx.st:
        cx.alloc_psum()
        emit_A(cx, d, T, noc, "A_")
        cnt = cx.P.emit(final_wait_keys=["ZT"])
    return nc, cnt


def emit_C(cx, d, T, pfx, final=True):
    nc = cx.nc
    P = cx.P
    TC = 256
    xT, cat, ccol, wada, bada, n2, nf, wout, wup, wdn, XTo, wupB = (d[k] for k in (
        "xT", "cat", "ccol", "wada", "bada", "n2", "nf", "wout", "wup", "wdn", "XTo", "wupB"))
    OUTT = d.get("OUTT")
    NT = T // TC
    with cx.scope(pfx):
        banks = cx.banks
        pm = banks[6][:, 0:64].rearrange("p (o t) -> p o t", t=2)
        sc = cx.sb("sc", [128, 8, 2]); wa = cx.sb("wa", [128, 2, 8, 128]); mod = cx.sb("mod", [128, 32])
        badas = cx.sb("badas", [128, 32]); n2s = cx.sb("n2s", [128, 8]); nfs = cx.sb("nfs", [128, 8]); gp = cx.sb("gp", [128, 8])
        onesD = cx.sb("onesD", [128, 128])
        Wo = cx.sb("Wo", [128, 8, 1024], BF16); Wd = cx.sb("Wd", [128, 32, 1024], BF16)
        wups = cx.sb("wups", [128, 2, 8, 1024], BF16)
        wtmp = cx.sb("wtmp", [128, 2, 1024]); wcb = cx.sb("wcb", [128, 2, 1024], BF16)
        xt = cx.sb("xt", [128, 8, TC]); ct = cx.sb("ct", [128, 8, TC], BF16)
        sqb = cx.sb("sqb", [128, 2, TC]); rs = cx.sb("rs", [128, TC]); rstd = cx.sb("rstd", [128, TC])
        tmp = cx.sb("tmp", [128, 2, TC]); hb = cx.sb("hb", [128, 8, TC], BF16)
        u = cx.sb("u", [128, 32, TC], BF16); ost = cx.sb("ost", [128, 8, TC])
        P.load("sp", sc[:], ccol, writes=["Csc"])
        P.load("sp", badas[:], bada, writes=["Cbada"])
        P.load("sp", n2s[:], n2, writes=["n2s"])
        P.load("sp", nfs[:], nf, writes=["nfs"])
        P.memset("pool", onesD[:], 1.0 / D, writes=["onesD"])
        P.act(sc[:], sc[:], AF.Silu, reads=["Csc"], writes=["Csc"])
        emit_mod(cx, wada, badas[:], sc, 32, wa, pm, mod, "C")
        P.stt(gp[:], mod[:, 16:24], 1.0, n2s[:], ALU.add, ALU.mult, reads=["Cmod", "n2s"], writes=["gp"])
        i = 0
        wov = wout.rearrange("(k p) c -> p k c", p=128)
        for k in range(8):
            b = i % 2
            P.load("sp", wtmp[:, b, :], wov[:, k, :], writes=[("wtmp", b)])
            P.copy(cx.evac_eng(), Wo[:, k, :], wtmp[:, b, :], reads=[("wtmp", b)], writes=["Wo"])
            i += 1
        wdv = wdn.rearrange("(k p) c -> p k c", p=128)
        for k in range(32):
            b = i % 2
            P.load("sp", wtmp[:, b, :], wdv[:, k, :], writes=[("wtmp", b)])
            P.copy(cx.evac_eng(), Wd[:, k, :], wtmp[:, b, :], reads=[("wtmp", b)], writes=["Wd"])
            i += 1
        wuv = wup.rearrange("(k p) c -> p k c", p=128)
        wubv = wupB.rearrange("(k p) c -> p k c", p=128)
        for k in range(8):
            for pc in range(4):
                b = i % 2
                P.load("sp", wtmp[:, b, :], wuv[:, k, pc * 1024:(pc + 1) * 1024], writes=[("wtmp", b)])
                P.copy(cx.evac_eng(), wcb[:, b, :], wtmp[:, b, :], reads=[("wtmp", b)], writes=[("wcb", b)])
                P.load("sp", wubv[:, k, pc * 1024:(pc + 1) * 1024], wcb[:, b, :], reads=[("wcb", b)], writes=["wupB"])
                i += 1
        xv = xT.rearrange("(k p) t -> p k t", p=128)
        catc = as_chunked(cat, T)
        xov = XTo.rearrange("(k p) t -> p k t", p=128)
        ouv = OUTT.rearrange("(k p) t -> p k t", p=128) if final else None
        bi = 0
        wi = 0
        for it in range(NT):
            cols = slice(it * TC, (it + 1) * TC)
            P.load("sp", xt[:], xv[:, :, cols], reads=[("XT", it)], writes=["xt"])
            P.load("sp", ct[:], catc.tile_pkt(it * TC, (it + 1) * TC), writes=["ct"])
            for oc in range(8):
                bnk = bi % 6; bi += 1
                pa = banks[bnk][:, 0:TC]
                for k in range(8):
                    P.mm(pa, Wo[:, k, oc * 128:(oc + 1) * 128], ct[:, k, :], start=(k == 0), stop=(k == 7),
                         reads=["Wo", "ct"], writes=bk(bnk))
                P.stt(xt[:, oc, :], pa, mod[:, oc:oc + 1], xt[:, oc, :], ALU.mult, ALU.add,
                      reads=bk(bnk) + ["Cmod", "xt"], writes=["xt"])
            emit_rstd(cx, xt, 8, TC, sqb, onesD, banks[6][:, 0:TC], bk(6), rs, rstd, "xt", "C", 1e-6)
            for k in range(8):
                b = k % 2
                P.tt("dve", tmp[:, b, :], xt[:, k, :], rstd[:], ALU.mult, reads=["xt", "Crstd"], writes=[("tmp", b)])
                P.ts("pool", hb[:, k, :], tmp[:, b, :], gp[:, k:k + 1], mod[:, 8 + k:9 + k], ALU.mult, ALU.add,
                     reads=[("tmp", b), "gp", "Cmod"], writes=["hb"])
            for pc in range(4):
                wb_ = wi % 2; wi += 1
                P.load("sp", wups[:, wb_, :, :], wubv[:, :, pc * 1024:(pc + 1) * 1024], reads=["wupB"], writes=[("wups", wb_)])
                for f in range(8):
                    fc = pc * 8 + f
                    bnk = bi % 6; bi += 1
                    pa = banks[bnk][:, 0:TC]
                    for k in range(8):
                        P.mm(pa, wups[:, wb_, k, f * 128:(f + 1) * 128], hb[:, k, :], start=(k == 0), stop=(k == 7),
                             reads=[("wups", wb_), "hb"], writes=bk(bnk))
                    b = fc % 2
                    P.act(tmp[:, b, :], pa, AF.Relu, reads=bk(bnk), writes=[("tmp", b)])
                    P.tt("pool" if fc % 2 else "dve", u[:, fc, :], tmp[:, b, :], tmp[:, b, :], ALU.mult, reads=[("tmp", b)], writes=["u"])
            for oc in range(8):
                bnk = bi % 6; bi += 1
                pa = banks[bnk][:, 0:TC]
                for fc in range(32):
                    P.mm(pa, Wd[:, fc, oc * 128:(oc + 1) * 128], u[:, fc, :], start=(fc == 0), stop=(fc == 31),
                         reads=["Wd", "u"], writes=bk(bnk))
                P.stt(xt[:, oc, :], pa, mod[:, 24 + oc:25 + oc], xt[:, oc, :], ALU.mult, ALU.add,
                      reads=bk(bnk) + ["Cmod", "xt"], writes=["xt"])
            P.load("sp", xov[:, :, cols], xt[:], reads=["xt"], writes=["XTo", ("XT", it)])
            if not final:
                continue
            emit_rstd(cx, xt, 8, TC, sqb, onesD, banks[6][:, 0:TC], bk(6), rs, rstd, "xt", "C", 1e-6)
            for k in range(8):
                P.stt(ost[:, k, :], xt[:, k, :], nfs[:, k:k + 1], rstd[:], ALU.mult, ALU.mult,
                      reads=["xt", "nfs", "Crstd"], writes=["ost"])
            P.load("sp", ouv[:, :, cols], ost[:], reads=["ost"], writes=["OUTT"])


def build_C(T):
    nc = bass.Bass("TRN2", target_bir_lowering=False)
    I = "ExternalInput"
    d = dict(xT=dram(nc, "xT", [D, T], F32, I), cat=dram(nc, "cat", [D, T], BF16, I), ccol=dram(nc, "ccol", [128, 8, 2], F32, I),
             wada=dram(nc, "wada", [D, 4096], F32, I), bada=dram(nc, "bada", [128, 32], F32, I), n2=dram(nc, "n2", [128, 8], F32, I),
             nf=dram(nc, "nf", [128, 8], F32, I), wout=dram(nc, "wout", [D, D], F32, I), wup=dram(nc, "wup", [D, 4096], F32, I),
             wdn=dram(nc, "wdn", [4096, D], F32, I), XTo=dram(nc, "XTo", [D, T], F32, "ExternalOutput"),
             OUTT=dram(nc, "OUTT", [D, T], F32, "ExternalOutput"), wupB=dram(nc, "wupB", [D, 4096], BF16, "Internal"))
    cx = Ctx(nc)
    with cx.st:
        cx.alloc_psum()
        emit_C(cx, d, T, "C_", final=True)
        cnt = cx.P.emit(final_wait_keys=["XTo", "OUTT"])
    return nc, cnt


def emit_B1(cx, d, S, pfx):
    nc = cx.nc
    P = cx.P
    FEAT, cosT, sinT, prot, identd, lamv, laminit, sw, ATT = (d[k] for k in (
        "FEAT", "cosT", "sinT", "prot", "ident", "lamv", "laminit", "sw", "ATT"))
    NT = S // 512
    NK = S // 128
    with cx.scope(pfx):
        banks = cx.banks
        pbt = cx.pbt
        QT = cx.sb("QT", [128, 2, S], BF16); KT = cx.sb("KT", [128, 2, S], BF16)
        Vtm = cx.sb("Vtm", [128, 2, NK, 129], BF16)
        raw = cx.sb("raw", [128, 2, 512], BF16); cs_ = cx.sb("cs", [128, 2, 512]); sn_ = cx.sb("sn", [128, 2, 512])
        t1 = cx.sb("t1", [128, 2, 512]); t2 = cx.sb("t2", [128, 2, 512])
        protb = cx.sb("protb", [128, 128], BF16); identb = cx.sb("identb", [128, 128], BF16)
        lamvs = cx.sb("lamvs", [128, 4, 64]); lt = cx.sb("lt", [128, 2, 64]); ls = cx.sb("ls", [128, 2]); le = cx.sb("le", [128, 2])
        lis = cx.sb("lis", [128, 1]); nlam = cx.sb("nlam", [128, 1]); oml = cx.sb("oml", [128, 1])
        sws = cx.sb("sws", [128, 128])
        pT = cx.sb("pT", [128, 3, 512], BF16)
        Osb = cx.sb("Osb", [128, 2, 4, 129])
        rs2 = cx.sb("rs2", [128, 2]); nl1 = cx.sb("nl1", [128, 1]); of = cx.sb("of", [128, 128]); of2 = cx.sb("of2", [128, 128])
        junk = cx.sb("junk", [128, 128]); ssq = cx.sb("ssq", [128, 1]); rr = cx.sb("rr", [128, 1]); rinv = cx.sb("rinv", [128, 1])
        attb = cx.sb("attb", [128, 128], BF16); attT = cx.sb("attT", [128, 2, 512], BF16)
        P.load("sp", protb[:], prot, writes=["protb"])
        P.load("sp", identb[:], identd, writes=["identb"])
        P.load("sp", lamvs[:], lamv, writes=["lamvs"])
        P.load("sp", lis[:], laminit, writes=["lis"])
        P.load("sp", sws[:], sw, writes=["sws"])
        P.tt("dve", lt[:, 0, :], lamvs[:, 0, :], lamvs[:, 1, :], ALU.mult, reads=["lamvs"], writes=["lt"])
        P.tt("dve", lt[:, 1, :], lamvs[:, 2, :], lamvs[:, 3, :], ALU.mult, reads=["lamvs"], writes=["lt"])
        P.op("dve", lambda e: e.tensor_reduce(ls[:], lt[:], AX.X, ALU.add), reads=["lt"], writes=["ls"])
        P.act(le[:], ls[:], AF.Exp, reads=["ls"], writes=["le"])
        P.tt("dve", nlam[:], le[:, 1:2], le[:, 0:1], ALU.subtract, reads=["le"], writes=["nlam"])
        P.tt("dve", nlam[:], nlam[:], lis[:], ALU.subtract, reads=["nlam", "lis"], writes=["nlam"])
        P.ts("dve", oml[:], lis[:], -1.0, 1.0, ALU.mult, ALU.add, reads=["lis"], writes=["oml"])
        P.ts("dve", sws[:], sws[:], oml[:, 0:1], None, ALU.mult, reads=["sws", "oml"], writes=["sws"])
        P.memset("pool", Vtm[:], 1.0, writes=["Vtm"])
        ri = 0
        for it in range(NT):
            cols = slice(it * 512, (it + 1) * 512)
            cb = it % 2
            P.load("sp", cs_[:, cb, :], cosT[:, cols], writes=[("cs", cb)])
            P.load("sp", sn_[:, cb, :], sinT[:, cols], writes=[("sn", cb)])
            for h in range(2):
                for wq, (dst, row0) in enumerate(((QT, h * 128), (KT, 256 + h * 128))):
                    b = ri % 2; ri += 1
                    P.load("sp", raw[:, b, :], FEAT[row0:row0 + 128, cols], writes=[("raw", b)])
                    P.mm(banks[6][:], protb[:], raw[:, b, :], reads=["protb", ("raw", b)], writes=bk(6))
                    P.tt("dve", t1[:, b, :], banks[6][:], sn_[:, cb, :], ALU.mult, reads=bk(6) + [("sn", cb)], writes=[("t1", b)])
                    P.tt("pool", t2[:, b, :], raw[:, b, :], cs_[:, cb, :], ALU.mult, reads=[("raw", b), ("cs", cb)], writes=[("t2", b)])
                    P.tt("dve", dst[:, h, cols], t1[:, b, :], t2[:, b, :], ALU.add, reads=[("t1", b), ("t2", b)],
                         writes=["QT" if wq == 0 else "KT"])
                b = ri % 2; ri += 1
                P.load("sp", raw[:, b, :], FEAT[512 + h * 128:640 + h * 128, cols], writes=[("raw", b)])
                for j in range(4):
                    kt = it * 4 + j
                    sl = (kt % 8)
                    P.transpose(pbt[:, sl * 128:(sl + 1) * 128], raw[:, b, j * 128:(j + 1) * 128], identb[:],
                                reads=[("raw", b), "identb"], writes=["pbt"])
                    P.copy(cx.evac_eng(), Vtm[:, h, kt, 0:128], pbt[:, sl * 128:(sl + 1) * 128], reads=["pbt"], writes=["Vtm"])
        si = 0
        pi = 0
        ti = 0
        for h in range(2):
            for qt in range(NT):
                qcols = slice(qt * 512, (qt + 1) * 512)
                for c in range(2):
                    cs = slice(c * 64, (c + 1) * 64)
                    for kt in range(NK):
                        sb_ = si % 2; si += 1
                        pb = pi % 3; pi += 1
                        P.mm(banks[sb_][:], KT[cs, h, kt * 128:(kt + 1) * 128], QT[cs, h, qcols], reads=["KT", "QT"], writes=bk(sb_))
                        P.act(pT[:, pb, :], banks[sb_][:], AF.Exp, scale=0.125, reads=bk(sb_), writes=[("pT", pb)])
                        for qs in range(4):
                            P.mm(banks[2 + qs][:, 0:129], pT[:, pb, qs * 128:(qs + 1) * 128], Vtm[:, h, kt, :],
                                 start=(kt == 0), stop=(kt == NK - 1), reads=[("pT", pb), "Vtm"], writes=bk(2 + qs))
                    for qs in range(4):
                        P.copy(cx.evac_eng(), Osb[:, c, qs, :], banks[2 + qs][:, 0:129], reads=bk(2 + qs), writes=[("Osb", c)])
                ab = (h * NT + qt) % 2
                for qs in range(4):
                    P.op("dve", lambda e, qs=qs: e.reciprocal(rs2[:], Osb[:, :, qs, 128]), reads=[("Osb", 0), ("Osb", 1)], writes=["rs2"])
                    P.tt("dve", nl1[:], rs2[:, 1:2], nlam[:], ALU.mult, reads=["rs2", "nlam"], writes=["nl1"])
                    P.ts("dve", of[:], Osb[:, 0, qs, 0:128], rs2[:, 0:1], None, ALU.mult, reads=[("Osb", 0), "rs2"], writes=["of"])
                    P.stt(of2[:], Osb[:, 1, qs, 0:128], nl1[:, 0:1], of[:], ALU.mult, ALU.add, reads=[("Osb", 1), "nl1", "of"], writes=["of2"])
                    P.act(junk[:], of2[:], AF.Square, accum_out=ssq[:], reads=["of2"], writes=["junk", "ssq"])
                    P.act(rr[:], ssq[:], AF.Sqrt, scale=1.0 / 128, bias=1e-5, reads=["ssq"], writes=["rr"])
                    P.op("dve", lambda e: e.reciprocal(rinv[:], rr[:]), reads=["rr"], writes=["rinv"])
                    P.stt(attb[:], of2[:], rinv[:, 0:1], sws[:], ALU.mult, ALU.mult, reads=["of2", "rinv", "sws"], writes=["attb"])
                    sl = ti % 8; ti += 1
                    P.transpose(pbt[:, sl * 128:(sl + 1) * 128], attb[:], identb[:], reads=["attb", "identb"], writes=["pbt"])
                    P.copy(cx.evac_eng(), attT[:, ab, qs * 128:(qs + 1) * 128], pbt[:, sl * 128:(sl + 1) * 128],
                           reads=["pbt"], writes=[("attT", ab)])
                P.load("sp", as_chunked(ATT, S).sl(h * 128, (h + 1) * 128, qt * 512, (qt + 1) * 512), attT[:, ab, :],
                       reads=[("attT", ab)], writes=["ATT"])


def build_B1(S):
    nc = bass.Bass("TRN2", target_bir_lowering=False)
    I = "ExternalInput"
    d = dict(FEAT=dram(nc, "FEAT", [1920, S], BF16, I), cosT=dram(nc, "cosT", [128, S], F32, I), sinT=dram(nc, "sinT", [128, S], F32, I),
             prot=dram(nc, "prot", [128, 128], BF16, I), ident=dram(nc, "ident", [128, 128], BF16, I),
             lamv=dram(nc, "lamv", [128, 4, 64], F32, I), laminit=dram(nc, "laminit", [128, 1], F32, I),
             sw=dram(nc, "sw", [128, 128], F32, I), ATT=dram(nc, "ATT", [256, S], BF16, "ExternalOutput"))
    cx = Ctx(nc)
    with cx.st:
        cx.alloc_psum()
        emit_B1(cx, d, S, "B1_")
        cnt = cx.P.emit(final_wait_keys=["ATT"])
    return nc, cnt


def emit_B2(cx, d, S, pfx, dbg=0):
    nc = cx.nc
    P = cx.P
    FEAT, mu9, w2d, a2d, g2d, vec, m1d, m2d, cst, REC, YF = (d[k] for k in (
        "FEAT", "mu9", "w2d", "a2d", "g2d", "vec", "m1d", "m2d", "cst", "REC", "YF"))
    NT = S // 512
    C_ = DECAY_C
    with cx.scope(pfx):
        banks = cx.banks
        names = ["nb0", "nb1", "zo0", "zo1", "zr", "zk", "zv", "zd", "za", "zg", "sgm", "aa", "aa0", "kkr", "sq", "nrm",
                 "kkn", "tk", "tk0", "kd", "bb", "csf", "cse", "E1", "E2", "E3", "bt", "kt", "Bh", "Kh", "msk",
                 "ytile", "yf", "yc", "tmpb", "yn"]
        W = {n: cx.sb("w_" + n, [128, 512]) for n in names}
        ar = cx.sb("ar", [128, 2, 512])
        raw = {n: cx.sb("raw_" + n, [128, 514], BF16) for n in ["r", "k", "v", "d", "a", "g"]}
        mus = cx.sb("mus", [128, 9]); omm = cx.sb("omm", [128, 9]); hm = cx.sb("hm", [128, 9])
        w2s = cx.sb("w2s", [128, 256]); a2s = cx.sb("a2s", [128, 256]); g2s = cx.sb("g2s", [128, 256])
        vecs = cx.sb("vecs", [128, 20]); omka = cx.sb("omka", [128, 2])
        m1s = cx.sb("m1s", [128, 2, 256]); m2s = cx.sb("m2s", [128, 2, 128]); csts = cx.sb("csts", [128, 448])
        wtot = cx.sb("wtot", [128, 8]); iwtot = cx.sb("iwtot", [128, 8])
        X = [[cx.sb(f"X{h}{p}", [128, 128]) for p in range(2)] for h in range(2)]
        Vpad = [cx.sb(f"Vpad{h}", [128, 128]) for h in range(2)]
        Bhtm = cx.sb("Bhtm", [128, 128]); Khtm = cx.sb("Khtm", [128, 128])
        LRs = [cx.sb(f"LRs{h}", [128, 256]) for h in range(2)]; KRs = [cx.sb(f"KRs{h}", [128, 256]) for h in range(2)]
        P0s = [cx.sb(f"P0s{h}", [128, 128]) for h in range(2)]
        LP = [[cx.sb(f"LP{h}{p}", [128, 256]) for p in range(2)] for h in range(2)]
        RpT = cx.sb("RpT", [128, 128]); ST = cx.sb("ST", [128, 64]); MTs = cx.sb("MTs", [128, 64]); Nns = cx.sb("Nns", [128, 64])
        recb = cx.sb("recb", [128, 2, 512], BF16)
        blk1 = csts[:, 0:128]; blkm = csts[:, 128:256]; identf = csts[:, 256:384]; ident64 = csts[:, 384:448]
        ctr = dict(f=0, h=0, q=0)

        def full():
            i = ctr["f"] % 2; ctr["f"] += 1
            return banks[i][:], bk(i)

        def half():
            b = 2 + ctr["q"] % 5; ctr["q"] += 1
            return banks[b][:, 0:256], bk(b)

        def quarter():
            b = 2 + ctr["q"] % 5; ctr["q"] += 1
            return banks[b][:, 0:128], bk(b)

        for (t_, d_, k_) in ((mus, mu9, "mus"), (w2s, w2d, "w2s"), (a2s, a2d, "a2s"), (g2s, g2d, "g2s"), (vecs, vec, "vecs"), (csts, cst, "csts")):
            P.load("sp", t_[:], d_, writes=[k_])
        for dr in range(2):
            P.load("sp", m1s[:, dr, :], m1d[dr], writes=["m1s"])
            P.load("sp", m2s[:, dr, :], m2d[dr], writes=["m2s"])
        P.ts("dve", omm[:], mus[:], -1.0, 1.0, ALU.mult, ALU.add, reads=["mus"], writes=["omm"])
        P.ts("dve", hm[:], mus[:], 0.5, None, ALU.mult, reads=["mus"], writes=["hm"])
        P.ts("dve", omka[:], vecs[:, 10:12], -1.0, 1.0, ALU.mult, ALU.add, reads=["vecs"], writes=["omka"])
        P.memset("pool", W["msk"][:], 1.0, writes=["msk"])
        P.memset("pool", W["msk"][:].rearrange("p (c t) -> p c t", t=64)[:, :, 0:1], 0.0, writes=["msk"])
        for h in range(2):
            P.memset("pool", Vpad[h][:], 0.0, writes=[("Vpad", h)])
        mi = [0]

        def mix(nm, col, dst):
            i = mi[0] % 2; mi[0] += 1
            rb = raw[nm]
            nb, zo = W[f"nb{i}"], W[f"zo{i}"]
            P.tt("pool", nb[:], rb[:, 0:512], rb[:, 2:514], ALU.add, reads=[("raw", nm)], writes=[f"nb{i}"])
            P.act(zo[:], rb[:, 1:513], AF.Identity, scale=omm[:, col:col + 1], reads=[("raw", nm), "omm"], writes=[f"zo{i}"])
            P.stt(W[dst][:], nb[:], hm[:, col:col + 1], zo[:], ALU.mult, ALU.add, reads=[f"nb{i}", f"zo{i}", "hm"], writes=[dst])

        def tt(eng, o, a, b, op):
            P.tt(eng, W[o][:], W[a][:], W[b][:], op, reads=[a, b], writes=[o])

        for hp in range(2):
            hcols = slice(hp * 128, (hp + 1) * 128)
            rows = dict(r=768 + hp * 128, k=1024 + hp * 128, v=1280 + hp * 128, d=1536, a=1664, g=1792)
            mucol = dict(r=hp, k=2 + hp, v=4 + hp, d=6, a=7, g=8)
            for dr in range(2):
                drs = slice(dr * 64, (dr + 1) * 64)
                P.memset("pool", ST[:], 0.0, writes=[("ST", 0), ("ST", 1)])
                tiles = range(NT) if dr == 0 else range(NT - 1, -1, -1)
                for it in tiles:
                    t0 = it * 512
                    cols = slice(t0, t0 + 512)
                    for nm in ["r", "k", "v", "d", "a"] + (["g"] if dr == 1 else []):
                        rb = raw[nm]
                        lo, hi = max(t0 - 1, 0), min(t0 + 513, S)
                        dlo = lo - (t0 - 1)
                        if t0 == 0:
                            P.memset("pool", rb[:, 0:1], 0.0, writes=[("raw", nm)])
                        if t0 + 512 == S:
                            P.memset("pool", rb[:, 513:514], 0.0, writes=[("raw", nm)])
                        P.load("sp", rb[:, dlo:dlo + (hi - lo)], FEAT[rows[nm]:rows[nm] + 128, lo:hi], writes=[("raw", nm)])
                    mix("r", mucol["r"], "zr"); mix("k", mucol["k"], "zk"); mix("v", mucol["v"], "zv")
                    mix("d", mucol["d"], "zd"); mix("a", mucol["a"], "za")
                    P.act(W["zd"][:], W["zd"][:], AF.Tanh, reads=["zd"], writes=["zd"])
                    pa, pk = full()
                    P.mm(pa, w2s[drs, hcols], W["zd"][drs, :], reads=["w2s", "zd"], writes=pk)
                    P.act(W["sgm"][:], pa, AF.Sigmoid, bias=vecs[:, dr * 2 + hp:dr * 2 + hp + 1], reads=pk + ["vecs"], writes=["sgm"])
                    pa, pk = full()
                    P.mm(pa, a2s[drs, hcols], W["za"][drs, :], reads=["a2s", "za"], writes=pk)
                    P.act(W["aa"][:], pa, AF.Sigmoid, bias=vecs[:, 4 + dr * 2 + hp:5 + dr * 2 + hp], reads=pk + ["vecs"], writes=["aa"])
                    P.ts("dve", W["kkr"][:], W["zk"][:], vecs[:, 8 + hp:9 + hp], None, ALU.mult, reads=["zk", "vecs"], writes=["kkr"])
                    tt("pool", "sq", "kkr", "kkr", ALU.mult)
                    pa, pk = full()
                    P.mm(pa, blk1, W["sq"][:], reads=["csts", "sq"], writes=pk)
                    P.act(W["nrm"][:], pa, AF.Sqrt, reads=pk, writes=["nrm"])
                    P.ts("dve", W["nrm"][:], W["nrm"][:], 1e-12, None, ALU.max, reads=["nrm"], writes=["nrm"])
                    P.op("dve", lambda e: e.reciprocal(W["nrm"][:], W["nrm"][:]), reads=["nrm"], writes=["nrm"])
                    tt("dve", "kkn", "kkr", "nrm", ALU.mult)
                    P.ts("pool", W["tk"][:], W["aa"][:], vecs[:, 10 + hp:11 + hp], omka[:, hp:hp + 1], ALU.mult, ALU.add,
                         reads=["aa", "vecs", "omka"], writes=["tk"])
                    tt("pool", "kd", "tk", "zk", ALU.mult)
                    tt("pool", "bb", "kkn", "aa", ALU.mult)
                    P.op("dve", lambda e: e.tensor_tensor_scan(W["csf"][:], W["msk"][:], W["sgm"][:], 0.0, ALU.mult, ALU.add),
                         reads=["msk", "sgm"], writes=["csf"])
                    tt("pool", "cse", "csf", "sgm", ALU.subtract)
                    cends = W["csf"][:].rearrange("p (c t) -> p c t", t=64)[:, :, 63]
                    P.act(wtot[:], cends, AF.Exp, scale=-C_, reads=["csf"], writes=["wtot"])
                    wtb = wtot[:].unsqueeze(2).to_broadcast([128, 8, 64])
                    v3 = lambda ap: ap.rearrange("p (c t) -> p c t", t=64)
                    at_, rt_ = ar[:, 0, :], ar[:, 1, :]
                    if dr == 0:
                        P.act(W["E1"][:], W["cse"][:], AF.Exp, scale=-C_, reads=["cse"], writes=["E1"])
                        P.act(W["E2"][:], W["csf"][:], AF.Exp, scale=-C_, reads=["csf"], writes=["E2"])
                        P.act(W["E3"][:], W["csf"][:], AF.Exp, scale=C_, reads=["csf"], writes=["E3"])
                        P.stt(at_, W["kkn"][:], -1.0, W["E1"][:], ALU.mult, ALU.mult, reads=["kkn", "E1"], writes=["ar"])
                        P.tt("pool", rt_, W["zr"][:], W["E2"][:], ALU.mult, reads=["zr", "E2"], writes=["ar"])
                        tt("dve", "bt", "bb", "E3", ALU.mult)
                        tt("pool", "kt", "kd", "E3", ALU.mult)
                        P.tt("dve", v3(W["Bh"][:]), v3(W["bt"][:]), wtb, ALU.mult, reads=["bt", "wtot"], writes=["Bh"])
                        P.tt("pool", v3(W["Kh"][:]), v3(W["kt"][:]), wtb, ALU.mult, reads=["kt", "wtot"], writes=["Kh"])
                    else:
                        P.act(iwtot[:], cends, AF.Exp, scale=C_, reads=["csf"], writes=["iwtot"])
                        iwb = iwtot[:].unsqueeze(2).to_broadcast([128, 8, 64])
                        P.act(W["E1"][:], W["csf"][:], AF.Exp, scale=C_, reads=["csf"], writes=["E1"])
                        P.act(W["E2"][:], W["cse"][:], AF.Exp, scale=C_, reads=["cse"], writes=["E2"])
                        P.act(W["E3"][:], W["cse"][:], AF.Exp, scale=-C_, reads=["cse"], writes=["E3"])
                        P.stt(at_, W["kkn"][:], -1.0, W["E1"][:], ALU.mult, ALU.mult, reads=["kkn", "E1"], writes=["ar"])
                        P.tt("dve", v3(at_), v3(at_), wtb, ALU.mult, reads=["ar", "wtot"], writes=["ar"])
                        P.tt("pool", rt_, W["zr"][:], W["E2"][:], ALU.mult, reads=["zr", "E2", "ar"], writes=["ar"])
                        P.tt("pool", v3(rt_), v3(rt_), wtb, ALU.mult, reads=["ar", "wtot"], writes=["ar"])
                        tt("dve", "Bh", "bb", "E3", ALU.mult)
                        tt("pool", "Kh", "kd", "E3", ALU.mult)
                        P.tt("dve", v3(W["bt"][:]), v3(W["Bh"][:]), iwb, ALU.mult, reads=["Bh", "iwtot"], writes=["bt"])
                        P.tt("pool", v3(W["kt"][:]), v3(W["Kh"][:]), iwb, ALU.mult, reads=["Kh", "iwtot"], writes=["kt"])
                    blocks = range(4) if dr == 0 else range(3, -1, -1)
                    if dbg == 1:
                        blocks = []
                        P.memset("pool", W["ytile"][:], 0.0, writes=["ytile"])
                    for bidx in blocks:
                        cb = slice(bidx * 128, (bidx + 1) * 128)
                        qa, qk = quarter()
                        P.mm(qa, ar[:, 0, cb], identf, reads=["ar", "csts"], writes=qk)
                        for hh in range(2):
                            P.copy(cx.evac_eng(), X[hh][0][:, 0:64], qa[:, hh * 64:(hh + 1) * 64], reads=qk, writes=[("X", hh, 0)])
                        qa, qk = quarter()
                        P.mm(qa, W["zv"][:, cb], identf, reads=["zv", "csts"], writes=qk)
                        for hh in range(2):
                            P.copy(cx.evac_eng(), Vpad[hh][:, 64:128], qa[:, hh * 64:(hh + 1) * 64], reads=qk, writes=[("Vpad", hh)])
                        qa, qk = quarter()
                        P.mm(qa, W["Bh"][:, cb], identf, reads=["Bh", "csts"], writes=qk)
                        P.copy(cx.evac_eng(), Bhtm[:], qa, reads=qk, writes=["Bhtm"])
                        qa, qk = quarter()
                        P.mm(qa, W["Kh"][:, cb], identf, reads=["Kh", "csts"], writes=qk)
                        P.copy(cx.evac_eng(), Khtm[:], qa, reads=qk, writes=["Khtm"])
                        if dbg == 2:
                            P.memset("pool", W["ytile"][:], 0.0, writes=["ytile"])
                            continue
                        for hh in range(2):
                            hs = slice(hh * 64, (hh + 1) * 64)
                            ha, hk = half()
                            P.mm(ha, W["bt"][hs, cb], ar[hs, :, cb], reads=["bt", "ar"], writes=hk)
                            P.tt("dve", LRs[hh][:], ha, m1s[:, dr, :], ALU.mult, reads=hk + ["m1s"], writes=[("LRs", hh)])
                            ha, hk = half()
                            P.mm(ha, W["kt"][hs, cb], ar[hs, :, cb], reads=["kt", "ar"], writes=hk)
                            P.tt("dve", KRs[hh][:], ha, m1s[:, dr, :], ALU.mult, reads=hk + ["m1s"], writes=[("KRs", hh)])
                            qa, qk = quarter()
                            P.mm(qa, ar[hs, 0, cb], W["bt"][hs, cb], reads=["bt", "ar"], writes=qk)
                            P.tt("dve", P0s[hh][:], qa, m2s[:, dr, :], ALU.mult, reads=qk + ["m2s"], writes=[("P0s", hh)])
                            if dbg == 3:
                                P.memset("pool", W["ytile"][:], 0.0, writes=["ytile"])
                                continue
                            qa, qk = quarter()
                            P.mm(qa[:, 0:64], KRs[hh][:, 0:128], Vpad[hh][:, 64:128], reads=[("KRs", hh), ("Vpad", hh)], writes=qk)
                            P.copy(cx.evac_eng(), X[hh][0][:, 64:128], qa[:, 0:64], reads=qk, writes=[("X", hh, 0)])
                            L_ap, P_ap = LRs[hh][:, 0:128], P0s[hh][:]
                            Lk, Pk = [("LRs", hh)], [("P0s", hh)]
                            xc = 0
                            for lvl in range(6):
                                qa, qk = quarter()
                                P.mm(qa, L_ap, X[hh][xc][:], reads=Lk + [("X", hh, xc)], writes=qk)
                                P.tt("dve", X[hh][1 - xc][:], qa, X[hh][xc][:], ALU.add, reads=qk + [("X", hh, xc)], writes=[("X", hh, 1 - xc)])
                                xc = 1 - xc
                                if lvl < 5:
                                    par = lvl % 2
                                    qa, qk = quarter()
                                    P.mm(qa, P_ap, L_ap, reads=Lk + Pk, writes=qk)
                                    P.copy("act", LP[hh][par][:, 0:128], qa, reads=qk, writes=[("LP", hh, par)])
                                    qa, qk = quarter()
                                    P.mm(qa, L_ap, P_ap, reads=Lk + Pk, writes=qk)
                                    P.copy("act", LP[hh][par][:, 128:256], qa, reads=qk, writes=[("LP", hh, par)])
                                    L_ap, P_ap = LP[hh][par][:, 0:128], LP[hh][par][:, 128:256]
                                    Lk = Pk = [("LP", hh, par)]
                            if dbg == 4:
                                P.memset("pool", W["ytile"][:], 0.0, writes=["ytile"])
                                continue
                            Xf = X[hh][xc]
                            Xfk = [("X", hh, xc)]
                            ga, gk = quarter()
                            P.mm(ga, Xf[:], LRs[hh][:, 128:256], start=True, stop=False, reads=Xfk + [("LRs", hh)], writes=gk)
                            P.mm(ga, Vpad[hh][:], KRs[hh][:, 128:256], start=False, stop=True, reads=[("Vpad", hh), ("KRs", hh)], writes=gk)
                            P.tt("dve", RpT[hs, :], ga[0:64, :], ar[hs, 1, cb], ALU.add, reads=gk + ["ar"], writes=[("RpT", hh)])
                            P.copy("act", W["ytile"][hs, cb], ga[64:128, :], reads=gk, writes=["ytile"])
                            for c in ((0, 1) if dr == 0 else (1, 0)):
                                if dbg == 5:
                                    continue
                                cc = slice(c * 64, (c + 1) * 64)
                                ycols = slice(bidx * 128 + c * 64, bidx * 128 + (c + 1) * 64)
                                qa, qk = quarter()
                                P.mm(qa[hs, 0:64], ST[hs, :], RpT[hs, cc], reads=[("ST", hh), ("RpT", hh)], writes=qk)
                                P.tt("dve", W["ytile"][hs, ycols], qa[hs, 0:64], W["ytile"][hs, ycols], ALU.add, reads=qk + ["ytile"], writes=["ytile"])
                                qa, qk = quarter()
                                P.mm(qa[hs, 0:64], Xf[cc, 0:64], Bhtm[cc, hs], reads=Xfk + ["Bhtm"], writes=qk)
                                P.mm(qa[hs, 64:128], Bhtm[cc, hs], Xf[cc, 64:128], start=True, stop=False, reads=Xfk + ["Bhtm"], writes=qk)
                                P.mm(qa[hs, 64:128], Khtm[cc, hs], Vpad[hh][cc, 64:128], start=False, stop=True, reads=["Khtm", ("Vpad", hh)], writes=qk)
                                wcol = bidx * 2 + c
                                P.stt(MTs[hs, :], ident64[hs, :], wtot[hs, wcol:wcol + 1], qa[hs, 0:64], ALU.mult, ALU.add,
                                      reads=qk + ["csts", "wtot"], writes=[("MTs", hh)])
                                P.copy("act", Nns[hs, :], qa[hs, 64:128], reads=qk, writes=[("Nns", hh)])
                                qa, qk = quarter()
                                P.mm(qa[hs, 0:64], MTs[hs, :], ST[hs, :], reads=[("MTs", hh), ("ST", hh)], writes=qk)
                                P.tt("dve", ST[hs, :], qa[hs, 0:64], Nns[hs, :], ALU.add, reads=qk + [("Nns", hh)], writes=[("ST", hh)])
                    if dr == 0:
                        P.load("sp", YF[hp * 128:(hp + 1) * 128, cols], W["ytile"][:], reads=["ytile"], writes=["YF"])
                    else:
                        P.load("sp", W["yf"][:], YF[hp * 128:(hp + 1) * 128, cols], reads=["YF"], writes=["yf"])
                        tt("pool", "yf", "ytile", "yf", ALU.add)
                        pa, pk = full()
                        P.mm(pa, blkm, W["yf"][:], reads=["csts", "yf"], writes=pk)
                        P.tt("dve", W["yc"][:], W["yf"][:], pa, ALU.subtract, reads=pk + ["yf"], writes=["yc"])
                        tt("pool", "sq", "yc", "yc", ALU.mult)
                        pa, pk = full()
                        P.mm(pa, blkm, W["sq"][:], reads=["csts", "sq"], writes=pk)
                        P.act(W["nrm"][:], pa, AF.Sqrt, bias=64e-5, reads=pk, writes=["nrm"])
                        P.op("dve", lambda e: e.reciprocal(W["nrm"][:], W["nrm"][:]), reads=["nrm"], writes=["nrm"])
                        tt("pool", "yn", "yc", "nrm", ALU.mult)
                        P.ts("pool", W["yn"][:], W["yn"][:], vecs[:, 14 + hp:15 + hp], vecs[:, 16 + hp:17 + hp], ALU.mult, ALU.add,
                             reads=["yn", "vecs"], writes=["yn"])
                        pa, pk = full()
                        P.mm(pa, a2s[0:64, hcols], W["za"][0:64, :], reads=["a2s", "za"], writes=pk)
                        P.act(W["aa0"][:], pa, AF.Sigmoid, bias=vecs[:, 4 + hp:5 + hp], reads=pk + ["vecs"], writes=["aa0"])
                        P.ts("pool", W["tk0"][:], W["aa0"][:], vecs[:, 10 + hp:11 + hp], omka[:, hp:hp + 1], ALU.mult, ALU.add,
                             reads=["aa0", "vecs", "omka"], writes=["tk0"])
                        tt("pool", "tk0", "tk0", "tk", ALU.add)
                        tt("pool", "tk0", "tk0", "zk", ALU.mult)
                        P.stt(W["tmpb"][:], W["zr"][:], vecs[:, 12 + hp:13 + hp], W["tk0"][:], ALU.mult, ALU.mult,
                              reads=["zr", "vecs", "tk0"], writes=["tmpb"])
                        pa, pk = full()
                        P.mm(pa, blk1, W["tmpb"][:], reads=["csts", "tmpb"], writes=pk)
                        P.tt("dve", W["tmpb"][:], pa, W["zv"][:], ALU.mult, reads=pk + ["zv", "tmpb"], writes=["tmpb"])
                        tt("pool", "yn", "yn", "tmpb", ALU.add)
                        mix("g", mucol["g"], "zg")
                        P.act(W["zg"][:], W["zg"][:], AF.Sigmoid, reads=["zg"], writes=["zg"])
                        pa, pk = full()
                        P.mm(pa, g2s[:, hcols], W["zg"][:], reads=["g2s", "zg"], writes=pk)
                        rb_ = it % 2
                        P.tt("dve", recb[:, rb_, :], pa, W["yn"][:], ALU.mult, reads=pk + ["yn"], writes=[("recb", rb_)])
                        P.load("sp", as_chunked(REC, S).sl(hp * 128, (hp + 1) * 128, t0, t0 + 512), recb[:, rb_, :],
                               reads=[("recb", rb_)], writes=["REC"])


def build_B2(S, dbg=0):
    nc = bass.Bass("TRN2", target_bir_lowering=False)
    I = "ExternalInput"
    d = dict(FEAT=dram(nc, "FEAT", [1920, S], BF16, I), mu9=dram(nc, "mu9", [128, 9], F32, I), w2d=dram(nc, "w2d", [128, 256], F32, I),
             a2d=dram(nc, "a2d", [128, 256], F32, I), g2d=dram(nc, "g2d", [128, 256], F32, I), vec=dram(nc, "vec", [128, 20], F32, I),
             m1d=dram(nc, "m1d", [2, 128, 256], F32, I), m2d=dram(nc, "m2d", [2, 128, 128], F32, I), cst=dram(nc, "cst", [128, 448], F32, I),
             REC=dram(nc, "REC", [256, S], BF16, "ExternalOutput"), YF=dram(nc, "YF", [256, S], F32, "Internal"))
    cx = Ctx(nc)
    with cx.st:
        cx.alloc_psum()
        emit_B2(cx, d, S, "B2_", dbg)
        cnt = cx.P.emit(final_wait_keys=["REC"])
    return nc, cnt


bf = ml_dtypes.bfloat16
def rope_tables(S):
    inv = (1.0 / (np.float32(10000.0) ** (np.arange(0, 64, 2, dtype=np.float32) / np.float32(64)))).astype(np.float32)
    ang = (np.arange(S, dtype=np.float32)[:, None] * inv[None, :]).astype(np.float32)
    cos = np.cos(ang).astype(np.float32); sin = np.sin(ang).astype(np.float32)
    idx = np.arange(128) % 32
    return np.ascontiguousarray(cos[:, idx].T), np.ascontiguousarray(sin[:, idx].T)
def rot_lhsT():
    Pm = np.zeros((128, 128), np.float32)
    for p in range(128):
        if (p % 64) < 32: Pm[p, p + 32] = -1.0
        else: Pm[p, p - 32] = 1.0
    return np.ascontiguousarray(Pm.T)

def colv(v):
    return np.ascontiguousarray(np.asarray(v, np.float32).reshape(-1, 128).T)

def b2_masks():
    t = np.arange(128)[:, None]; j = np.arange(128)[None, :]
    same = (t // 64) == (j // 64)
    m1 = np.zeros((2, 128, 256), np.float32); m2 = np.zeros((2, 128, 128), np.float32)
    for dr in range(2):
        strict = same & ((j < t) if dr == 0 else (j > t))
        incl = same & ((j <= t) if dr == 0 else (j >= t))
        m2[dr] = strict
        m1[dr][:, 0:128] = strict.T
        m1[dr][:, 128:256] = incl.T
    return m1, m2

def b2_consts():
    blk1 = np.kron(np.eye(2), np.ones((64, 64))).astype(np.float32)
    ident64 = np.concatenate([np.eye(64), np.eye(64)], 0).astype(np.float32)
    return np.ascontiguousarray(np.concatenate([blk1, blk1 / 64.0, np.eye(128, dtype=np.float32), ident64], 1).astype(np.float32))

def b2_inputs(g, mu, w0, w2, a0, a2, g2, k_k, k_a, r_k, lnw, lnb):
    ch = slice(g * 256, (g + 1) * 256)
    mu_r, mu_k, mu_v = mu[0:512][ch], mu[512:1024][ch], mu[1024:1536][ch]
    mu9 = np.stack([mu_r[0:128], mu_r[128:256], mu_k[0:128], mu_k[128:256], mu_v[0:128], mu_v[128:256],
                    mu[1536:1664], mu[1664:1792], mu[1792:1920]], 1).astype(np.float32)
    w2d = np.ascontiguousarray(w2[:, :, ch].reshape(128, 256)); a2d = np.ascontiguousarray(a2[:, :, ch].reshape(128, 256))
    g2d = np.ascontiguousarray(g2[:, ch])
    cols = []
    for dr in range(2):
        for hp in range(2):
            cols.append(w0[dr][ch][hp * 128:(hp + 1) * 128])
    for dr in range(2):
        for hp in range(2):
            cols.append(a0[dr][ch][hp * 128:(hp + 1) * 128])
    rk = r_k.reshape(-1)[ch]
    for vv in (k_k[ch], k_a[ch], rk, lnw[ch], lnb[ch]):
        cols.append(vv[0:128]); cols.append(vv[128:256])
    cols.append(np.zeros(128)); cols.append(np.zeros(128))
    vec = np.ascontiguousarray(np.stack(cols, 1).astype(np.float32))
    m1, m2 = b2_masks()
    return dict(mu9=np.ascontiguousarray(mu9), w2d=w2d.astype(np.float32), a2d=a2d.astype(np.float32), g2d=g2d.astype(np.float32),
                vec=vec, m1d=m1, m2d=m2, cst=b2_consts())


_PROGS = {}
_RUN = [None]


def _prog(name, fn, arg):
    key = (name, arg)
    if key not in _PROGS:
        _PROGS[key] = fn(arg)[0]
    return _PROGS[key]


def _launch(nc, in_maps):
    if _RUN[0] is not None:
        return _RUN[0](nc, in_maps)
    res = run_bass_kernel_spmd(nc, in_maps, core_ids=list(range(len(in_maps))))
    return res.results


def _in_perm():
    cols = []
    for g in range(2):
        for base in (0, 512, 1024):
            cols += list(range(base + g * 256, base + (g + 1) * 256))
        for base in (1536, 1536 + 512, 1536 + 1024):
            cols += list(range(base + g * 256, base + (g + 1) * 256))
    cols += list(range(1536 + 1536, 1536 + 1920))
    return np.array(cols)


def kernel_unfused(x, c, w_ada, b_ada, norm1, norm2, w_in, w_out, lam_q1, lam_k1, lam_q2, lam_k2,
           subln_w, tshift_mu, decay_w0, decay_w2, icl_a0, icl_a2, gate_g2, k_k, k_a, r_k,
           lnx_w, lnx_b, w_up, w_down, norm_f):
    f32 = np.float32
    A_ = lambda a: np.ascontiguousarray(np.asarray(a, f32))
    x = np.asarray(x, f32)
    B, S, Dm = x.shape
    T = S // 2
    L = np.asarray(w_in).shape[0]
    ncores = 2 * B
    ncA = _prog("A", build_A, T); ncB1 = _prog("B1", build_B1, S); ncB2 = _prog("B2", build_B2, S); ncC = _prog("C", build_C, T)
    xT = [np.ascontiguousarray(x[cid // 2, (cid % 2) * T:(cid % 2 + 1) * T, :].T) for cid in range(ncores)]
    ccol = [np.ascontiguousarray(np.repeat(colv(np.asarray(c, f32)[b])[:, :, None], 2, axis=2)) for b in range(B)]
    perm = _in_perm()
    cosT, sinT = rope_tables(S)
    prot = rot_lhsT().astype(bf)
    identb = np.eye(128, dtype=f32).astype(bf)
    outT = None
    for l in range(L):
        lam_init = f32(0.8 - 0.6 * np.exp(-0.3 * l))
        wa = np.asarray(w_ada[l], f32); ba = np.asarray(b_ada[l], f32)
        winp = np.ascontiguousarray(np.asarray(w_in[l], f32)[:, perm])
        wadaA = np.ascontiguousarray(wa[:, 0:2048]); badaA = colv(ba[0:2048]); n1 = colv(np.asarray(norm1[l], f32))
        resA = _launch(ncA, [dict(xT=xT[cid], ccol=ccol[cid // 2], wada=wadaA, bada=badaA, n1=n1, win=winp) for cid in range(ncores)])
        ZT = [np.asarray(r["ZT"]) for r in resA]
        feats = []
        for cid in range(ncores):
            b, g = cid // 2, cid % 2
            parts = []
            for j in range(2):
                z = ZT[2 * b + j]
                parts.append(np.concatenate([z[g * 1536:(g + 1) * 1536], z[3072:3456]], axis=0))
            feats.append(np.ascontiguousarray(np.concatenate(parts, axis=1)))
        lam4 = np.stack([np.asarray(v[l], f32) for v in (lam_q1, lam_k1, lam_q2, lam_k2)], 0)
        lamv = np.ascontiguousarray(np.broadcast_to(lam4[None], (128, 4, 64)))
        laminit = np.full((128, 1), lam_init, f32)
        sw = np.ascontiguousarray(np.broadcast_to(np.asarray(subln_w[l], f32)[None], (128, 128)))
        resB1 = _launch(ncB1, [dict(FEAT=feats[cid], cosT=cosT, sinT=sinT, prot=prot, ident=identb, lamv=lamv, laminit=laminit, sw=sw)
                               for cid in range(ncores)])
        b2in = [b2_inputs(g, np.asarray(tshift_mu[l], f32), np.asarray(decay_w0[l], f32), np.asarray(decay_w2[l], f32),
                          np.asarray(icl_a0[l], f32), np.asarray(icl_a2[l], f32), np.asarray(gate_g2[l], f32),
                          np.asarray(k_k[l], f32), np.asarray(k_a[l], f32), np.asarray(r_k[l], f32),
                          np.asarray(lnx_w[l], f32), np.asarray(lnx_b[l], f32)) for g in range(2)]
        resB2 = _launch(ncB2, [dict(FEAT=feats[cid], **b2in[cid % 2]) for cid in range(ncores)])
        ATT = [np.asarray(r["ATT"]) for r in resB1]
        REC = [np.asarray(r["REC"]) for r in resB2]
        wadaC = np.ascontiguousarray(wa[:, 2048:6144]); badaC = colv(ba[2048:6144])
        n2 = colv(np.asarray(norm2[l], f32)); nf = colv(np.asarray(norm_f, f32))
        wo = A_(w_out[l]); wu = A_(w_up[l]); wd = A_(w_down[l])
        mapsC = []
        for cid in range(ncores):
            b, j = cid // 2, cid % 2
            ts_ = slice(j * T, (j + 1) * T)
            cat = np.ascontiguousarray(np.concatenate([ATT[2 * b][:, ts_], ATT[2 * b + 1][:, ts_], REC[2 * b][:, ts_], REC[2 * b + 1][:, ts_]], axis=0))
            mapsC.append(dict(xT=xT[cid], cat=cat, ccol=ccol[b], wada=wadaC, bada=badaC, n2=n2, nf=nf, wout=wo, wup=wu, wdn=wd))
        resC = _launch(ncC, mapsC)
        xT = [np.ascontiguousarray(np.asarray(r["XTo"], f32)) for r in resC]
        outT = [np.asarray(r["OUTT"], f32) for r in resC]
    out = np.empty((B, S, Dm), f32)
    for cid in range(ncores):
        b, j = cid // 2, cid % 2
        out[b, j * T:(j + 1) * T, :] = outT[cid].T
    return out


PAIRS = [[0, 1], [2, 3], [4, 5], [6, 7]]


def build_fused(S, L=2):
    nc = bass.Bass("TRN2", target_bir_lowering=False)
    I = "ExternalInput"
    g = {}
    g["x0"] = dram(nc, "x0", [D, S], F32, I)
    g["ccol"] = dram(nc, "ccol", [128, 8, 2], F32, I)
    for nm, shp, dt in (("cosT", [128, S], F32), ("sinT", [128, S], F32), ("prot", [128, 128], BF16), ("ident", [128, 128], BF16),
                        ("m1d", [2, 128, 256], F32), ("m2d", [2, 128, 128], F32), ("cst", [128, 448], F32), ("nf", [128, 8], F32)):
        g[nm] = dram(nc, nm, shp, dt, I)
    per = []
    for l in range(L):
        p = {}
        for nm, shp, dt in (("wadaA", [D, 2048], F32), ("badaA", [128, 16], F32), ("n1", [128, 8], F32), ("win", [D, 1920], F32),
                            ("lamv", [128, 4, 64], F32), ("laminit", [128, 1], F32), ("sw", [128, 128], F32),
                            ("mu9", [128, 9], F32), ("w2d", [128, 256], F32), ("a2d", [128, 256], F32), ("g2d", [128, 256], F32),
                            ("vec", [128, 20], F32),
                            ("wadaC", [D, 4096], F32), ("badaC", [128, 32], F32), ("n2", [128, 8], F32),
                            ("wout", [D, D], F32), ("wup", [D, 4096], F32), ("wdn", [4096, D], F32)):
            p[nm] = dram(nc, f"{nm}{l}", shp, dt, I)
        per.append(p)
    OUTT = dram(nc, "OUTT", [D, S], F32, "ExternalOutput")
    XTs = dram(nc, "XTs", [D, S], F32, "Internal")
    FEAT = dram(nc, "FEATs", [1920, S], BF16, "Internal")
    YF = dram(nc, "YFs", [256, S], F32, "Internal")
    wupB = dram(nc, "wupBs", [D, 4096], BF16, "Internal")
    CW = min(1024, S)
    NCH = S // CW
    CATg = [nc.dram_tensor(f"CATg{k}", [512, CW], BF16).ap() for k in range(NCH)]
    CATALL = [nc.dram_tensor(f"CATALL{k}", [1024, CW], BF16).ap() for k in range(NCH)]
    catg_att = Chunked([a[0:256, :] for a in CATg], CW)
    catg_rec = Chunked([a[256:512, :] for a in CATg], CW)
    catall = Chunked(CATALL, CW)
    cx = Ctx(nc)
    P = cx.P
    with cx.st:
        cx.alloc_psum()
        for l in range(L):
            p = per[l]
            emit_A(cx, dict(xT=(g["x0"] if l == 0 else XTs), ccol=g["ccol"], wada=p["wadaA"], bada=p["badaA"], n1=p["n1"],
                            win=p["win"], ZT=FEAT), S, 15, f"L{l}A_")
            emit_B1(cx, dict(FEAT=FEAT, cosT=g["cosT"], sinT=g["sinT"], prot=g["prot"], ident=g["ident"], lamv=p["lamv"],
                             laminit=p["laminit"], sw=p["sw"], ATT=catg_att), S, f"L{l}B1_")
            emit_B2(cx, dict(FEAT=FEAT, mu9=p["mu9"], w2d=p["w2d"], a2d=p["a2d"], g2d=p["g2d"], vec=p["vec"], m1d=g["m1d"],
                             m2d=g["m2d"], cst=g["cst"], REC=catg_rec, YF=YF), S, f"L{l}B2_")
            P.fence()
            for k in range(NCH):
                P.coll(lambda e, k=k: e.collective_compute("AllGather", ALU.bypass, PAIRS, ins=[CATg[k].opt()], outs=[CATALL[k].opt()]),
                       reads=["ATT", "REC"], writes=["CATALL"])
            P.fence()
            emit_C(cx, dict(xT=(g["x0"] if l == 0 else XTs), cat=catall, ccol=g["ccol"], wada=p["wadaC"], bada=p["badaC"],
                            n2=p["n2"], nf=g["nf"], wout=p["wout"], wup=p["wup"], wdn=p["wdn"], XTo=XTs, OUTT=OUTT, wupB=wupB),
                   S, f"L{l}C_", final=(l == L - 1))
        cnt = P.emit(final_wait_keys=["OUTT"])
    return nc, cnt


def kernel(x, c, w_ada, b_ada, norm1, norm2, w_in, w_out, lam_q1, lam_k1, lam_q2, lam_k2,
                 subln_w, tshift_mu, decay_w0, decay_w2, icl_a0, icl_a2, gate_g2, k_k, k_a, r_k,
                 lnx_w, lnx_b, w_up, w_down, norm_f):
    f32 = np.float32
    A_ = lambda a: np.ascontiguousarray(np.asarray(a, f32))
    x = np.asarray(x, f32)
    B, S, Dm = x.shape
    L = np.asarray(w_in).shape[0]
    ncores = 2 * B
    ncF = _prog("F", lambda s_: build_fused(s_, L), S)
    cosT, sinT = rope_tables(S)
    m1, m2 = b2_masks()
    shared = dict(cosT=cosT, sinT=sinT, prot=rot_lhsT().astype(bf), ident=np.eye(128, dtype=f32).astype(bf),
                  m1d=m1, m2d=m2, cst=b2_consts(), nf=colv(np.asarray(norm_f, f32)))
    perm = _in_perm()
    wo_rows = np.concatenate([np.arange(0, 256), np.arange(512, 768), np.arange(256, 512), np.arange(768, 1024)])
    lay = []
    for l in range(L):
        wa = np.asarray(w_ada[l], f32); ba = np.asarray(b_ada[l], f32)
        winp = np.asarray(w_in[l], f32)[:, perm]
        lam4 = np.stack([np.asarray(v[l], f32) for v in (lam_q1, lam_k1, lam_q2, lam_k2)], 0)
        com = dict(wadaA=np.ascontiguousarray(wa[:, 0:2048]), badaA=colv(ba[0:2048]), n1=colv(np.asarray(norm1[l], f32)),
                   lamv=np.ascontiguousarray(np.broadcast_to(lam4[None], (128, 4, 64))),
                   laminit=np.full((128, 1), f32(0.8 - 0.6 * np.exp(-0.3 * l)), f32),
                   sw=np.ascontiguousarray(np.broadcast_to(np.asarray(subln_w[l], f32)[None], (128, 128))),
                   wadaC=np.ascontiguousarray(wa[:, 2048:6144]), badaC=colv(ba[2048:6144]), n2=colv(np.asarray(norm2[l], f32)),
                   wout=np.ascontiguousarray(np.asarray(w_out[l], f32)[wo_rows, :]), wup=A_(w_up[l]), wdn=A_(w_down[l]))
        pg = []
        for g_ in range(2):
            dd = dict(com)
            dd["win"] = np.ascontiguousarray(np.concatenate([winp[:, g_ * 1536:(g_ + 1) * 1536], winp[:, 3072:3456]], axis=1))
            b2 = b2_inputs(g_, np.asarray(tshift_mu[l], f32), np.asarray(decay_w0[l], f32), np.asarray(decay_w2[l], f32),
                           np.asarray(icl_a0[l], f32), np.asarray(icl_a2[l], f32), np.asarray(gate_g2[l], f32),
                           np.asarray(k_k[l], f32), np.asarray(k_a[l], f32), np.asarray(r_k[l], f32),
                           np.asarray(lnx_w[l], f32), np.asarray(lnx_b[l], f32))
            for k_ in ("mu9", "w2d", "a2d", "g2d", "vec"):
                dd[k_] = b2[k_]
            pg.append(dd)
        lay.append(pg)
    xTb = [np.ascontiguousarray(x[b].T) for b in range(B)]
    in_maps = []
    for cid in range(ncores):
        b, g_ = cid // 2, cid % 2
        m = dict(shared)
        m["x0"] = xTb[b]
        m["ccol"] = np.ascontiguousarray(np.repeat(colv(np.asarray(c, f32)[b])[:, :, None], 2, axis=2))
        for l in range(L):
            for k_, v_ in lay[l][g_].items():
                m[f"{k_}{l}"] = v_
        in_maps.append(m)
    res = _launch(ncF, in_maps)
    out = np.empty((B, S, Dm), f32)
    T = S // 2
    for cid in range(ncores):
        b, j = cid // 2, cid % 2
        out[b, j * T:(j + 1) * T, :] = np.asarray(res[cid]["OUTT"], f32)[:, j * T:(j + 1) * T].T
    return out
```

```python
import numpy as np
import concourse.bass as bass
import concourse.mybir as mybir

F32 = mybir.dt.float32
BF16 = mybir.dt.bfloat16
I32 = mybir.dt.int32
AF = mybir.ActivationFunctionType
ALU = mybir.AluOpType
AX = mybir.AxisListType

EPOCH = 20000
NDMASEM = 24


class Prog:
    ENGS = ("pe", "dve", "act", "pool", "sp")

    def __init__(self, nc):
        self.nc = nc
        self.ins = []
        self.lastw = {}
        self.readers = {}
        self.ndma = 0
        self.fence_deps = set()
        self.fence_pending = set()
        self.last_eng = {}
        self.dma_since = []

    def fence(self):
        self.fence_deps = set(self.last_eng.values()) | set(self.dma_since)
        self.fence_pending = set(self.ENGS)
        self.dma_since = []

    def _add(self, eng, fn, reads, writes, dma):
        i = len(self.ins)
        excl = [k for k in reads if (isinstance(k, tuple) and k[0] == "ps") or k == "pbt"]
        writes = list(writes) + [k for k in excl if k not in writes]
        deps = set()
        for k in list(reads) + list(writes):
            if k in self.lastw:
                deps.add(self.lastw[k])
        for k in writes:
            for r in self.readers.get(k, ()):
                deps.add(r)
        if eng in self.fence_pending:
            deps |= self.fence_deps
            self.fence_pending.discard(eng)
        deps.discard(i)
        if dma:
            self.dma_since.append(i)
        else:
            self.last_eng[eng] = i
        for k in writes:
            self.lastw[k] = i
            self.readers[k] = []
        for k in reads:
            self.readers.setdefault(k, []).append(i)
        self.ins.append(dict(eng=eng, fn=fn, deps=deps, dma=dma))
        return i

    def op(self, eng, fn, reads=(), writes=()):
        return self._add(eng, fn, reads, writes, False)

    def dma(self, eng, fn, reads=(), writes=()):
        return self._add(eng, fn, reads, writes, True)

    def coll(self, fn, reads=(), writes=()):
        i = self._add("pool", fn, reads, writes, True)
        self.ins[i]["own"] = True
        return i

    def mm(self, out, lhsT, rhs, start=True, stop=True, reads=(), writes=(), **kw):
        return self.op("pe", lambda e: e.matmul(out, lhsT, rhs, start=start, stop=stop, **kw), reads, writes)

    def transpose(self, out, in_, ident, reads=(), writes=()):
        return self.op("pe", lambda e: e.transpose(out, in_, ident), reads, writes)

    def act(self, out, in_, func, reads=(), writes=(), eng="act", **kw):
        return self.op(eng, lambda e: e.activation(out, in_, func, **kw), reads, writes)

    def tt(self, eng, out, in0, in1, op, reads=(), writes=()):
        return self.op(eng, lambda e: e.tensor_tensor(out, in0, in1, op), reads, writes)

    def ts(self, eng, out, in0, s1, s2, op0, op1=None, reads=(), writes=(), **kw):
        if op1 is None:
            return self.op(eng, lambda e: e.tensor_scalar(out, in0, s1, None, op0, **kw), reads, writes)
        return self.op(eng, lambda e: e.tensor_scalar(out, in0, s1, s2, op0, op1, **kw), reads, writes)

    def stt(self, out, in0, scalar, in1, op0, op1, reads=(), writes=(), **kw):
        return self.op("dve", lambda e: e.scalar_tensor_tensor(out, in0, scalar, in1, op0, op1, **kw), reads, writes)

    def copy(self, eng, out, in_, reads=(), writes=()):
        if eng == "act":
            return self.op(eng, lambda e: e.copy(out, in_), reads, writes)
        return self.op(eng, lambda e: e.tensor_copy(out, in_), reads, writes)

    def memset(self, eng, ap, val, writes=()):
        return self.op(eng, lambda e: e.memset(ap, val), (), writes)

    def load(self, eng, out, in_, reads=(), writes=(), **kw):
        return self.dma(eng, lambda e: e.dma_start(out=out, in_=in_, **kw), reads, writes)

    def emit(self, final_wait_keys=()):
        nc = self.nc
        ins = self.ins
        fin_deps = set()
        for k in final_wait_keys:
            if k in self.lastw:
                fin_deps.add(self.lastw[k])
        needed = set()
        for r in ins:
            for d in r["deps"]:
                if ins[d]["eng"] == "pe" and r["eng"] == "pe" and not ins[d]["dma"] and not r["dma"]:
                    continue
                needed.add(d)
        needed |= fin_deps
        cnt = {e: 0 for e in self.ENGS}
        dmacnt = [0] * NDMASEM
        ndma = 0
        nown = 0
        for i, r in enumerate(ins):
            if r["dma"] and r.get("own"):
                r["dsem"] = NDMASEM + nown
                nown += 1
                r["dprev"] = 0
                r["dtarget"] = 1
            elif r["dma"]:
                s = ndma % NDMASEM
                ndma += 1
                r["dsem"] = s
                r["dprev"] = dmacnt[s]
                dmacnt[s] += 16
                r["dtarget"] = dmacnt[s]
            elif i in needed:
                cnt[r["eng"]] += 1
                r["ms"] = cnt[r["eng"]]
        nep = {e: (cnt[e] // EPOCH) + 1 for e in self.ENGS}
        import contextlib
        st = contextlib.ExitStack()
        csem = {e: [st.enter_context(nc.semaphore(f"c_{e}_{k}")) for k in range(nep[e])] for e in self.ENGS if e != "sp"}
        dsem = [st.enter_context(nc.semaphore(f"d_{k}")) for k in range(NDMASEM + nown)]
        per = {e: [] for e in self.ENGS}
        for i, r in enumerate(ins):
            per[r["eng"]].append(i)
        block = st.enter_context(nc.Block())

        def waits_for(i, known):
            r = ins[i]
            out = []
            deps = set(r["deps"])
            for d in sorted(deps):
                dr = ins[d]
                if dr["dma"]:
                    key = ("d", dr["dsem"])
                    val = dr["dtarget"]
                    sem = dsem[dr["dsem"]]
                else:
                    if dr["eng"] == "pe" and r["eng"] == "pe" and not r["dma"]:
                        continue
                    m = dr["ms"]
                    ep = (m - 1) // EPOCH
                    key = ("c", dr["eng"], ep)
                    val = m - ep * EPOCH
                    sem = csem[dr["eng"]][ep]
                    if any(k2[0] == "c" and k2[1] == dr["eng"] and k2[2] > ep for k2 in known):
                        continue
                if known.get(key, 0) >= val:
                    continue
                known[key] = val
                out.append((sem, val))
            if r["dma"]:
                key = ("d", r["dsem"])
                if r["dprev"] > 0 and known.get(key, 0) < r["dprev"]:
                    known[key] = r["dprev"]
                    out.append((dsem[r["dsem"]], r["dprev"]))
            return out

        def body(engname):
            def f(eng):
                known = {}
                for i in per[engname]:
                    r = ins[i]
                    for sem, val in waits_for(i, known):
                        eng.wait_ge(sem, val)
                    inst = r["fn"](eng)
                    if r["dma"] and r.get("own"):
                        inst.then_inc(dsem[r["dsem"]])
                    elif r["dma"]:
                        inst.then_inc(dsem[r["dsem"]], 16)
                    elif "ms" in r:
                        ep = (r["ms"] - 1) // EPOCH
                        inst.then_inc(csem[engname][ep], 1)
                if engname == "sp":
                    for d in sorted(fin_deps):
                        dr = ins[d]
                        if dr["dma"]:
                            eng.wait_ge(dsem[dr["dsem"]], dr["dtarget"])
                        else:
                            ep = (dr["ms"] - 1) // EPOCH
                            eng.wait_ge(csem[dr["eng"]][ep], dr["ms"] - ep * EPOCH)
            return f

        block.tensor(body("pe"))
        block.vector(body("dve"))
        block.scalar(body("act"))
        block.gpsimd(body("pool"))
        block.sync(body("sp"))
        st.close()
        return {e: len(per[e]) for e in self.ENGS}

import contextlib
from concourse.bass_utils import run_bass_kernel_spmd
import ml_dtypes

D = 1024
DECAY_C = 0.6065306597126334


class Ctx:
    def __init__(self, nc):
        self.nc = nc
        self.st = contextlib.ExitStack()
        self.P = Prog(nc)
        self.alt = 0
        self.cur = self.st
        self.pfx = ""
        self.banks = None
        self.pbt = None

    def alloc_psum(self):
        self.banks = [self.ps(f"bank{i}", [128, 512]) for i in range(7)]
        self.pbt = self.ps("pbt", [128, 1024], BF16)

    @contextlib.contextmanager
    def scope(self, pfx):
        self.P.fence()
        old = (self.cur, self.pfx)
        with contextlib.ExitStack() as stk:
            self.cur, self.pfx = stk, pfx
            yield
            self.P.fence()
        self.cur, self.pfx = old

    def sb(self, name, shape, dt=F32):
        return self.cur.enter_context(self.nc.sbuf_tensor(self.pfx + name, shape, dt))

    def ps(self, name, shape, dt=F32):
        return self.st.enter_context(self.nc.psum_tensor(name, shape, dt))

    def evac_eng(self):
        self.alt ^= 1
        return "dve" if self.alt else "act"


def dram(nc, name, shape, dt, kind):
    return nc.dram_tensor(name, shape, dt, kind=kind).ap()


def bk(i):
    return [("ps", i)]


class Chunked:
    def __init__(self, aps, W):
        self.aps, self.W = aps, W

    def sl(self, r0, r1, t0, t1):
        k = t0 // self.W
        assert (t1 - 1) // self.W == k
        return self.aps[k][r0:r1, t0 - k * self.W:t1 - k * self.W]

    def tile_pkt(self, t0, t1):
        k = t0 // self.W
        assert (t1 - 1) // self.W == k
        return self.aps[k].rearrange("(k p) t -> p k t", p=128)[:, :, t0 - k * self.W:t1 - k * self.W]


def as_chunked(x, S):
    return x if isinstance(x, Chunked) else Chunked([x], S)


def emit_mod(cx, wada, bada, sc, noc, wa, pm, mod, tag):
    P = cx.P
    wv = wada.rearrange("(k p) c -> p k c", p=128)
    for oc in range(noc):
        b = oc % 2
        P.load("sp", wa[:, b, :, :], wv[:, :, oc * 128:(oc + 1) * 128],
               writes=[(tag + "wa", b)])
        for k in range(8):
            P.mm(pm[:, oc, :], wa[:, b, k, :], sc[:, k, :], start=(k == 0), stop=(k == 7),
                 reads=[(tag + "wa", b), tag + "sc"], writes=bk(6))
    P.tt("dve", mod[:, 0:noc], pm[:, 0:noc, 0], bada, ALU.add, reads=bk(6) + [tag + "bada"], writes=[tag + "mod"])


def emit_rstd(cx, xt, nk, ncol, sqb, onesD, psr_ap, psr_keys, rs, rstd, xkey, tag, eps):
    P = cx.P
    for k in range(nk):
        b = k % 2
        P.act(sqb[:, b, :], xt[:, k, :], AF.Square, reads=[xkey], writes=[(tag + "sq", b)])
        P.mm(psr_ap, onesD[:], sqb[:, b, :], start=(k == 0), stop=(k == nk - 1),
             reads=[(tag + "sq", b), "onesD"], writes=psr_keys)
    P.act(rs[:], psr_ap, AF.Sqrt, bias=eps, reads=psr_keys, writes=[tag + "rs"])
    P.op("dve", lambda e: e.reciprocal(rstd[:], rs[:]), reads=[tag + "rs"], writes=[tag + "rstd"])


def emit_A(cx, d, T, noc, pfx):
    nc = cx.nc
    P = cx.P
    xT, ccol, wada, bada, n1, win, ZT = (d[k] for k in ("xT", "ccol", "wada", "bada", "n1", "win", "ZT"))
    NT = T // 512
    gs = 9 if noc == 27 else 5
    npc = 3 if noc == 27 else 2
    pw = noc * 128 // npc
    with cx.scope(pfx):
        banks = cx.banks
        pm = banks[6][:, 0:64].rearrange("p (o t) -> p o t", t=2)
        sc = cx.sb("sc", [128, 8, 2]); wa = cx.sb("wa", [128, 2, 8, 128]); mod = cx.sb("mod", [128, 16])
        badas = cx.sb("badas", [128, 16]); n1s = cx.sb("n1s", [128, 8]); gp = cx.sb("gp", [128, 8])
        onesD = cx.sb("onesD", [128, 128]); Wb = cx.sb("Wb", [128, 8, noc * 128], BF16)
        wtmp = cx.sb("wtmp", [128, 2, pw]); xt = cx.sb("xt", [128, 2, 8, 512])
        sqb = cx.sb("sqb", [128, 2, 512]); rs = cx.sb("rs", [128, 512]); rstd = cx.sb("rstd", [128, 512])
        tmp = cx.sb("tmp", [128, 2, 512]); hb = cx.sb("hb", [128, 8, 512], BF16)
        zst = cx.sb("zst", [128, 2, gs, 512], BF16)
        P.load("sp", sc[:], ccol, writes=["Asc"])
        P.load("sp", badas[:], bada, writes=["Abada"])
        P.load("sp", n1s[:], n1, writes=["n1s"])
        P.memset("pool", onesD[:], 1.0 / D, writes=["onesD"])
        P.act(sc[:], sc[:], AF.Silu, reads=["Asc"], writes=["Asc"])
        emit_mod(cx, wada, badas[:], sc, 16, wa, pm, mod, "A")
        P.stt(gp[:], mod[:, 8:16], 1.0, n1s[:], ALU.add, ALU.mult, reads=["Amod", "n1s"], writes=["gp"])
        wv = win.rearrange("(k p) c -> p k c", p=128)
        i = 0
        for k in range(8):
            for pc in range(npc):
                b = i % 2
                P.load("sp", wtmp[:, b, :], wv[:, k, pc * pw:(pc + 1) * pw], writes=[("wtmp", b)])
                P.copy(cx.evac_eng(), Wb[:, k, pc * pw:(pc + 1) * pw], wtmp[:, b, :], reads=[("wtmp", b)], writes=["Wb"])
                i += 1
        xv = xT.rearrange("(k p) t -> p k t", p=128)
        zv = ZT.rearrange("(o p) t -> p o t", p=128)
        bi = 0
        for it in range(NT):
            xb = it % 2
            cols = slice(it * 512, (it + 1) * 512)
            P.load("sp", xt[:, xb, :, :], xv[:, :, cols], writes=[("xt", xb)])
            emit_rstd(cx, xt[:, xb], 8, 512, sqb, onesD, banks[6][:], bk(6), rs, rstd, ("xt", xb), "A", 1e-6)
            for k in range(8):
                b = k % 2
                P.tt("dve", tmp[:, b, :], xt[:, xb, k, :], rstd[:], ALU.mult, reads=[("xt", xb), "Arstd"], writes=[("tmp", b)])
                P.ts("pool", hb[:, k, :], tmp[:, b, :], gp[:, k:k + 1], mod[:, k:k + 1], ALU.mult, ALU.add,
                     reads=[("tmp", b), "gp", "Amod"], writes=["hb"])
            for oc in range(noc):
                bnk = bi % 6
                bi += 1
                for k in range(8):
                    P.mm(banks[bnk][:], Wb[:, k, oc * 128:(oc + 1) * 128], hb[:, k, :], start=(k == 0), stop=(k == 7),
                         reads=["Wb", "hb"], writes=bk(bnk))
                g = (it * 3 + oc // gs) % 2
                P.copy(cx.evac_eng(), zst[:, g, oc % gs, :], banks[bnk][:], reads=bk(bnk), writes=[("zst", g)])
                if oc % gs == gs - 1:
                    P.load("sp", zv[:, oc - gs + 1:oc + 1, cols], zst[:, g, :, :], reads=[("zst", g)], writes=["ZT"])


def build_A(T, noc=27):
    nc = bass.Bass("TRN2", target_bir_lowering=False)
    d = dict(xT=dram(nc, "xT", [D, T], F32, "ExternalInput"), ccol=dram(nc, "ccol", [128, 8, 2], F32, "ExternalInput"),
             wada=dram(nc, "wada", [D, 2048], F32, "ExternalInput"), bada=dram(nc, "bada", [128, 16], F32, "ExternalInput"),
             n1=dram(nc, "n1", [128, 8], F32, "ExternalInput"), win=dram(nc, "win", [D, noc * 128], F32, "ExternalInput"),
             ZT=dram(nc, "ZT", [noc * 128, T], BF16, "ExternalOutput"))
    cx = Ctx(nc)
    with cx.st:
        cx.alloc_psum()
        emit_A(cx, d, T, noc, "A_")
        cnt = cx.P.emit(final_wait_keys=["ZT"])
    return nc, cnt


def emit_C(cx, d, T, pfx, final=True):
    nc = cx.nc
    P = cx.P
    TC = 256
    xT, cat, ccol, wada, bada, n2, nf, wout, wup, wdn, XTo, wupB = (d[k] for k in (
        "xT", "cat", "ccol", "wada", "bada", "n2", "nf", "wout", "wup", "wdn", "XTo", "wupB"))
    OUTT = d.get("OUTT")
    NT = T // TC
    with cx.scope(pfx):
        banks = cx.banks
        pm = banks[6][:, 0:64].rearrange("p (o t) -> p o t", t=2)
        sc = cx.sb("sc", [128, 8, 2]); wa = cx.sb("wa", [128, 2, 8, 128]); mod = cx.sb("mod", [128, 32])
        badas = cx.sb("badas", [128, 32]); n2s = cx.sb("n2s", [128, 8]); nfs = cx.sb("nfs", [128, 8]); gp = cx.sb("gp", [128, 8])
        onesD = cx.sb("onesD", [128, 128])
        Wo = cx.sb("Wo", [128, 8, 1024], BF16); Wd = cx.sb("Wd", [128, 32, 1024], BF16)
        wups = cx.sb("wups", [128, 2, 8, 1024], BF16)
        wtmp = cx.sb("wtmp", [128, 2, 1024]); wcb = cx.sb("wcb", [128, 2, 1024], BF16)
        xt = cx.sb("xt", [128, 8, TC]); ct = cx.sb("ct", [128, 8, TC], BF16)
        sqb = cx.sb("sqb", [128, 2, TC]); rs = cx.sb("rs", [128, TC]); rstd = cx.sb("rstd", [128, TC])
        tmp = cx.sb("tmp", [128, 2, TC]); hb = cx.sb("hb", [128, 8, TC], BF16)
        u = cx.sb("u", [128, 32, TC], BF16); ost = cx.sb("ost", [128, 8, TC])
        P.load("sp", sc[:], ccol, writes=["Csc"])
        P.load("sp", badas[:], bada, writes=["Cbada"])
        P.load("sp", n2s[:], n2, writes=["n2s"])
        P.load("sp", nfs[:], nf, writes=["nfs"])
        P.memset("pool", onesD[:], 1.0 / D, writes=["onesD"])
        P.act(sc[:], sc[:], AF.Silu, reads=["Csc"], writes=["Csc"])
        emit_mod(cx, wada, badas[:], sc, 32, wa, pm, mod, "C")
        P.stt(gp[:], mod[:, 16:24], 1.0, n2s[:], ALU.add, ALU.mult, reads=["Cmod", "n2s"], writes=["gp"])
        i = 0
        wov = wout.rearrange("(k p) c -> p k c", p=128)
        for k in range(8):
            b = i % 2
            P.load("sp", wtmp[:, b, :], wov[:, k, :], writes=[("wtmp", b)])
            P.copy(cx.evac_eng(), Wo[:, k, :], wtmp[:, b, :], reads=[("wtmp", b)], writes=["Wo"])
            i += 1
        wdv = wdn.rearrange("(k p) c -> p k c", p=128)
        for k in range(32):
            b = i % 2
            P.load("sp", wtmp[:, b, :], wdv[:, k, :], writes=[("wtmp", b)])
            P.copy(cx.evac_eng(), Wd[:, k, :], wtmp[:, b, :], reads=[("wtmp", b)], writes=["Wd"])
            i += 1
        wuv = wup.rearrange("(k p) c -> p k c", p=128)
        wubv = wupB.rearrange("(k p) c -> p k c", p=128)
        for k in range(8):
            for pc in range(4):
                b = i % 2
                P.load("sp", wtmp[:, b, :], wuv[:, k, pc * 1024:(pc + 1) * 1024], writes=[("wtmp", b)])
                P.copy(cx.evac_eng(), wcb[:, b, :], wtmp[:, b, :], reads=[("wtmp", b)], writes=[("wcb", b)])
                P.load("sp", wubv[:, k, pc * 1024:(pc + 1) * 1024], wcb[:, b, :], reads=[("wcb", b)], writes=["wupB"])
                i += 1
        xv = xT.rearrange("(k p) t -> p k t", p=128)
        catc = as_chunked(cat, T)
        xov = XTo.rearrange("(k p) t -> p k t", p=128)
        ouv = OUTT.rearrange("(k p) t -> p k t", p=128) if final else None
        bi = 0
        wi = 0
        for it in range(NT):
            cols = slice(it * TC, (it + 1) * TC)
            P.load("sp", xt[:], xv[:, :, cols], reads=[("XT", it)], writes=["xt"])
            P.load("sp", ct[:], catc.tile_pkt(it * TC, (it + 1) * TC), writes=["ct"])
            for oc in range(8):
                bnk = bi % 6; bi += 1
                pa = banks[bnk][:, 0:TC]
                for k in range(8):
                    P.mm(pa, Wo[:, k, oc * 128:(oc + 1) * 128], ct[:, k, :], start=(k == 0), stop=(k == 7),
                         reads=["Wo", "ct"], writes=bk(bnk))
                P.stt(xt[:, oc, :], pa, mod[:, oc:oc + 1], xt[:, oc, :], ALU.mult, ALU.add,
                      reads=bk(bnk) + ["Cmod", "xt"], writes=["xt"])
            emit_rstd(cx, xt, 8, TC, sqb, onesD, banks[6][:, 0:TC], bk(6), rs, rstd, "xt", "C", 1e-6)
            for k in range(8):
                b = k % 2
                P.tt("dve", tmp[:, b, :], xt[:, k, :], rstd[:], ALU.mult, reads=["xt", "Crstd"], writes=[("tmp", b)])
                P.ts("pool", hb[:, k, :], tmp[:, b, :], gp[:, k:k + 1], mod[:, 8 + k:9 + k], ALU.mult, ALU.add,
                     reads=[("tmp", b), "gp", "Cmod"], writes=["hb"])
            for pc in range(4):
                wb_ = wi % 2; wi += 1
                P.load("sp", wups[:, wb_, :, :], wubv[:, :, pc * 1024:(pc + 1) * 1024], reads=["wupB"], writes=[("wups", wb_)])
                for f in range(8):
                    fc = pc * 8 + f
                    bnk = bi % 6; bi += 1
                    pa = banks[bnk][:, 0:TC]
                    for k in range(8):
                        P.mm(pa, wups[:, wb_, k, f * 128:(f + 1) * 128], hb[:, k, :], start=(k == 0), stop=(k == 7),
                             reads=[("wups", wb_), "hb"], writes=bk(bnk))
                    b = fc % 2
                    P.act(tmp[:, b, :], pa, AF.Relu, reads=bk(bnk), writes=[("tmp", b)])
                    P.tt("pool" if fc % 2 else "dve", u[:, fc, :], tmp[:, b, :], tmp[:, b, :], ALU.mult, reads=[("tmp", b)], writes=["u"])
            for oc in range(8):
                bnk = bi % 6; bi += 1
                pa = banks[bnk][:, 0:TC]
                for fc in range(32):
                    P.mm(pa, Wd[:, fc, oc * 128:(oc + 1) * 128], u[:, fc, :], start=(fc == 0), stop=(fc == 31),
                         reads=["Wd", "u"], writes=bk(bnk))
                P.stt(xt[:, oc, :], pa, mod[:, 24 + oc:25 + oc], xt[:, oc, :], ALU.mult, ALU.add,
                      reads=bk(bnk) + ["Cmod", "xt"], writes=["xt"])
            P.load("sp", xov[:, :, cols], xt[:], reads=["xt"], writes=["XTo", ("XT", it)])
            if not final:
                continue
            emit_rstd(cx, xt, 8, TC, sqb, onesD, banks[6][:, 0:TC], bk(6), rs, rstd, "xt", "C", 1e-6)
            for k in range(8):
                P.stt(ost[:, k, :], xt[:, k, :], nfs[:, k:k + 1], rstd[:], ALU.mult, ALU.mult,
                      reads=["xt", "nfs", "Crstd"], writes=["ost"])
            P.load("sp", ouv[:, :, cols], ost[:], reads=["ost"], writes=["OUTT"])


def build_C(T):
    nc = bass.Bass("TRN2", target_bir_lowering=False)
    I = "ExternalInput"
    d = dict(xT=dram(nc, "xT", [D, T], F32, I), cat=dram(nc, "cat", [D, T], BF16, I), ccol=dram(nc, "ccol", [128, 8, 2], F32, I),
             wada=dram(nc, "wada", [D, 4096], F32, I), bada=dram(nc, "bada", [128, 32], F32, I), n2=dram(nc, "n2", [128, 8], F32, I),
             nf=dram(nc, "nf", [128, 8], F32, I), wout=dram(nc, "wout", [D, D], F32, I), wup=dram(nc, "wup", [D, 4096], F32, I),
             wdn=dram(nc, "wdn", [4096, D], F32, I), XTo=dram(nc, "XTo", [D, T], F32, "ExternalOutput"),
             OUTT=dram(nc, "OUTT", [D, T], F32, "ExternalOutput"), wupB=dram(nc, "wupB", [D, 4096], BF16, "Internal"))
    cx = Ctx(nc)
    with cx.st:
        cx.alloc_psum()
        emit_C(cx, d, T, "C_", final=True)
        cnt = cx.P.emit(final_wait_keys=["XTo", "OUTT"])
    return nc, cnt


def emit_B1(cx, d, S, pfx):
    nc = cx.nc
    P = cx.P
    FEAT, cosT, sinT, prot, identd, lamv, laminit, sw, ATT = (d[k] for k in (
        "FEAT", "cosT", "sinT", "prot", "ident", "lamv", "laminit", "sw", "ATT"))
    NT = S // 512
    NK = S // 128
    with cx.scope(pfx):
        banks = cx.banks
        pbt = cx.pbt
        QT = cx.sb("QT", [128, 2, S], BF16); KT = cx.sb("KT", [128, 2, S], BF16)
        Vtm = cx.sb("Vtm", [128, 2, NK, 129], BF16)
        raw = cx.sb("raw", [128, 2, 512], BF16); cs_ = cx.sb("cs", [128, 2, 512]); sn_ = cx.sb("sn", [128, 2, 512])
        t1 = cx.sb("t1", [128, 2, 512]); t2 = cx.sb("t2", [128, 2, 512])
        protb = cx.sb("protb", [128, 128], BF16); identb = cx.sb("identb", [128, 128], BF16)
        lamvs = cx.sb("lamvs", [128, 4, 64]); lt = cx.sb("lt", [128, 2, 64]); ls = cx.sb("ls", [128, 2]); le = cx.sb("le", [128, 2])
        lis = cx.sb("lis", [128, 1]); nlam = cx.sb("nlam", [128, 1]); oml = cx.sb("oml", [128, 1])
        sws = cx.sb("sws", [128, 128])
        pT = cx.sb("pT", [128, 3, 512], BF16)
        Osb = cx.sb("Osb", [128, 2, 4, 129])
        rs2 = cx.sb("rs2", [128, 2]); nl1 = cx.sb("nl1", [128, 1]); of = cx.sb("of", [128, 128]); of2 = cx.sb("of2", [128, 128])
        junk = cx.sb("junk", [128, 128]); ssq = cx.sb("ssq", [128, 1]); rr = cx.sb("rr", [128, 1]); rinv = cx.sb("rinv", [128, 1])
        attb = cx.sb("attb", [128, 128], BF16); attT = cx.sb("attT", [128, 2, 512], BF16)
        P.load("sp", protb[:], prot, writes=["protb"])
        P.load("sp", identb[:], identd, writes=["identb"])
        P.load("sp", lamvs[:], lamv, writes=["lamvs"])
        P.load("sp", lis[:], laminit, writes=["lis"])
        P.load("sp", sws[:], sw, writes=["sws"])
        P.tt("dve", lt[:, 0, :], lamvs[:, 0, :], lamvs[:, 1, :], ALU.mult, reads=["lamvs"], writes=["lt"])
        P.tt("dve", lt[:, 1, :], lamvs[:, 2, :], lamvs[:, 3, :], ALU.mult, reads=["lamvs"], writes=["lt"])
        P.op("dve", lambda e: e.tensor_reduce(ls[:], lt[:], AX.X, ALU.add), reads=["lt"], writes=["ls"])
        P.act(le[:], ls[:], AF.Exp, reads=["ls"], writes=["le"])
        P.tt("dve", nlam[:], le[:, 1:2], le[:, 0:1], ALU.subtract, reads=["le"], writes=["nlam"])
        P.tt("dve", nlam[:], nlam[:], lis[:], ALU.subtract, reads=["nlam", "lis"], writes=["nlam"])
        P.ts("dve", oml[:], lis[:], -1.0, 1.0, ALU.mult, ALU.add, reads=["lis"], writes=["oml"])
        P.ts("dve", sws[:], sws[:], oml[:, 0:1], None, ALU.mult, reads=["sws", "oml"], writes=["sws"])
        P.memset("pool", Vtm[:], 1.0, writes=["Vtm"])
        ri = 0
        for it in range(NT):
            cols = slice(it * 512, (it + 1) * 512)
            cb = it % 2
            P.load("sp", cs_[:, cb, :], cosT[:, cols], writes=[("cs", cb)])
            P.load("sp", sn_[:, cb, :], sinT[:, cols], writes=[("sn", cb)])
            for h in range(2):
                for wq, (dst, row0) in enumerate(((QT, h * 128), (KT, 256 + h * 128))):
                    b = ri % 2; ri += 1
                    P.load("sp", raw[:, b, :], FEAT[row0:row0 + 128, cols], writes=[("raw", b)])
                    P.mm(banks[6][:], protb[:], raw[:, b, :], reads=["protb", ("raw", b)], writes=bk(6))
                    P.tt("dve", t1[:, b, :], banks[6][:], sn_[:, cb, :], ALU.mult, reads=bk(6) + [("sn", cb)], writes=[("t1", b)])
                    P.tt("pool", t2[:, b, :], raw[:, b, :], cs_[:, cb, :], ALU.mult, reads=[("raw", b), ("cs", cb)], writes=[("t2", b)])
                    P.tt("dve", dst[:, h, cols], t1[:, b, :], t2[:, b, :], ALU.add, reads=[("t1", b), ("t2", b)],
                         writes=["QT" if wq == 0 else "KT"])
                b = ri % 2; ri += 1
                P.load("sp", raw[:, b, :], FEAT[512 + h * 128:640 + h * 128, cols], writes=[("raw", b)])
                for j in range(4):
                    kt = it * 4 + j
                    sl = (kt % 8)
                    P.transpose(pbt[:, sl * 128:(sl + 1) * 128], raw[:, b, j * 128:(j + 1) * 128], identb[:],
                                reads=[("raw", b), "identb"], writes=["pbt"])
                    P.copy(cx.evac_eng(), Vtm[:, h, kt, 0:128], pbt[:, sl * 128:(sl + 1) * 128], reads=["pbt"], writes=["Vtm"])
        steps = [(h, qt, c, kt) for h in range(2) for qt in range(NT) for c in range(2) for kt in range(NK)]
        ti = 0

        def emit_qk(i):
            h, qt, c, kt = steps[i]
            cs = slice(c * 64, (c + 1) * 64)
            P.mm(banks[i % 2][:], KT[cs, h, kt * 128:(kt + 1) * 128], QT[cs, h, qt * 512:(qt + 1) * 512],
                 reads=["KT", "QT"], writes=bk(i % 2))

        emit_qk(0)
        for i, (h, qt, c, kt) in enumerate(steps):
            qcols = slice(qt * 512, (qt + 1) * 512)
            if i + 1 < len(steps):
                emit_qk(i + 1)
            pb = i % 3
            P.act(pT[:, pb, :], banks[i % 2][:], AF.Exp, scale=0.125, reads=bk(i % 2), writes=[("pT", pb)])
            for qs in range(4):
                P.mm(banks[2 + qs][:, 0:129], pT[:, pb, qs * 128:(qs + 1) * 128], Vtm[:, h, kt, :],
                     start=(kt == 0), stop=(kt == NK - 1), reads=[("pT", pb), "Vtm"], writes=bk(2 + qs))
            if kt != NK - 1:
                continue
            for qs in range(4):
                P.copy("dve" if qs % 2 == 0 else "act", Osb[:, c, qs, :], banks[2 + qs][:, 0:129], reads=bk(2 + qs), writes=[("Osb", c)])
            if c != 1:
                continue
            ab = (h * NT + qt) % 2
            for qs in range(4):
                P.op("dve", lambda e, qs=qs: e.reciprocal(rs2[:], Osb[:, :, qs, 128]), reads=[("Osb", 0), ("Osb", 1)], writes=["rs2"])
                P.tt("dve", nl1[:], rs2[:, 1:2], nlam[:], ALU.mult, reads=["rs2", "nlam"], writes=["nl1"])
                P.ts("dve", of[:], Osb[:, 0, qs, 0:128], rs2[:, 0:1], None, ALU.mult, reads=[("Osb", 0), "rs2"], writes=["of"])
                P.stt(of2[:], Osb[:, 1, qs, 0:128], nl1[:, 0:1], of[:], ALU.mult, ALU.add, reads=[("Osb", 1), "nl1", "of"], writes=["of2"])
                P.act(junk[:], of2[:], AF.Square, accum_out=ssq[:], reads=["of2"], writes=["junk", "ssq"])
                P.act(rr[:], ssq[:], AF.Sqrt, scale=1.0 / 128, bias=1e-5, reads=["ssq"], writes=["rr"])
                P.op("dve", lambda e: e.reciprocal(rinv[:], rr[:]), reads=["rr"], writes=["rinv"])
                P.stt(attb[:], of2[:], rinv[:, 0:1], sws[:], ALU.mult, ALU.mult, reads=["of2", "rinv", "sws"], writes=["attb"])
                sl = ti % 8; ti += 1
                P.transpose(pbt[:, sl * 128:(sl + 1) * 128], attb[:], identb[:], reads=["attb", "identb"], writes=["pbt"])
                P.copy("dve", attT[:, ab, qs * 128:(qs + 1) * 128], pbt[:, sl * 128:(sl + 1) * 128],
                       reads=["pbt"], writes=[("attT", ab)])
            P.load("sp", as_chunked(ATT, S).sl(h * 128, (h + 1) * 128, qt * 512, (qt + 1) * 512), attT[:, ab, :],
                   reads=[("attT", ab)], writes=["ATT"])


def build_B1(S):
    nc = bass.Bass("TRN2", target_bir_lowering=False)
    I = "ExternalInput"
    d = dict(FEAT=dram(nc, "FEAT", [1920, S], BF16, I), cosT=dram(nc, "cosT", [128, S], F32, I), sinT=dram(nc, "sinT", [128, S], F32, I),
             prot=dram(nc, "prot", [128, 128], BF16, I), ident=dram(nc, "ident", [128, 128], BF16, I),
             lamv=dram(nc, "lamv", [128, 4, 64], F32, I), laminit=dram(nc, "laminit", [128, 1], F32, I),
             sw=dram(nc, "sw", [128, 128], F32, I), ATT=dram(nc, "ATT", [256, S], BF16, "ExternalOutput"))
    cx = Ctx(nc)
    with cx.st:
        cx.alloc_psum()
        emit_B1(cx, d, S, "B1_")
        cnt = cx.P.emit(final_wait_keys=["ATT"])
    return nc, cnt


def emit_B2(cx, d, S, pfx, dbg=0):
    nc = cx.nc
    P = cx.P
    FEAT, mu9, w2d, a2d, g2d, vec, m1d, m2d, cst, REC, YF = (d[k] for k in (
        "FEAT", "mu9", "w2d", "a2d", "g2d", "vec", "m1d", "m2d", "cst", "REC", "YF"))
    NT = S // 512
    C_ = DECAY_C
    with cx.scope(pfx):
        banks = cx.banks
        names = ["nb0", "nb1", "zo0", "zo1", "zr", "zk", "zv", "zd", "za", "zg", "sgm", "aa", "aa0", "kkr", "sq", "nrm",
                 "kkn", "tk", "tk0", "kd", "bb", "csf", "cse", "E1", "E2", "E3", "bt", "kt", "Bh", "Kh", "msk",
                 "ytile", "yf", "yc", "tmpb", "yn"]
        W = {n: cx.sb("w_" + n, [128, 512]) for n in names}
        ar = cx.sb("ar", [128, 2, 512])
        raw = {n: cx.sb("raw_" + n, [128, 514], BF16) for n in ["r", "k", "v", "d", "a", "g"]}
        mus = cx.sb("mus", [128, 9]); omm = cx.sb("omm", [128, 9]); hm = cx.sb("hm", [128, 9])
        w2s = cx.sb("w2s", [128, 256]); a2s = cx.sb("a2s", [128, 256]); g2s = cx.sb("g2s", [128, 256])
        vecs = cx.sb("vecs", [128, 20]); omka = cx.sb("omka", [128, 2])
        m1s = cx.sb("m1s", [128, 2, 256]); m2s = cx.sb("m2s", [128, 2, 128]); csts = cx.sb("csts", [128, 448])
        wtot = cx.sb("wtot", [128, 8]); iwtot = cx.sb("iwtot", [128, 8])
        X = [[cx.sb(f"X{h}{p}", [128, 128]) for p in range(2)] for h in range(2)]
        Vpad = [cx.sb(f"Vpad{h}", [128, 128]) for h in range(2)]
        Bhtm = cx.sb("Bhtm", [128, 128]); Khtm = cx.sb("Khtm", [128, 128])
        LRs = [cx.sb(f"LRs{h}", [128, 256]) for h in range(2)]; KRs = [cx.sb(f"KRs{h}", [128, 256]) for h in range(2)]
        P0s = [cx.sb(f"P0s{h}", [128, 128]) for h in range(2)]
        LP = [[cx.sb(f"LP{h}{p}", [128, 256]) for p in range(2)] for h in range(2)]
        RpT = cx.sb("RpT", [128, 128]); ST = cx.sb("ST", [128, 64]); MTs = cx.sb("MTs", [128, 64]); Nns = cx.sb("Nns", [128, 64])
        recb = cx.sb("recb", [128, 2, 512], BF16)
        blk1 = csts[:, 0:128]; blkm = csts[:, 128:256]; identf = csts[:, 256:384]; ident64 = csts[:, 384:448]
        ctr = dict(f=0, h=0, q=0)

        def full():
            i = ctr["f"] % 2; ctr["f"] += 1
            return banks[i][:], bk(i)

        def half():
            b = 2 + ctr["q"] % 5; ctr["q"] += 1
            return banks[b][:, 0:256], bk(b)

        def quarter():
            b = 2 + ctr["q"] % 5; ctr["q"] += 1
            return banks[b][:, 0:128], bk(b)

        for (t_, d_, k_) in ((mus, mu9, "mus"), (w2s, w2d, "w2s"), (a2s, a2d, "a2s"), (g2s, g2d, "g2s"), (vecs, vec, "vecs"), (csts, cst, "csts")):
            P.load("sp", t_[:], d_, writes=[k_])
        for dr in range(2):
            P.load("sp", m1s[:, dr, :], m1d[dr], writes=["m1s"])
            P.load("sp", m2s[:, dr, :], m2d[dr], writes=["m2s"])
        P.ts("dve", omm[:], mus[:], -1.0, 1.0, ALU.mult, ALU.add, reads=["mus"], writes=["omm"])
        P.ts("dve", hm[:], mus[:], 0.5, None, ALU.mult, reads=["mus"], writes=["hm"])
        P.ts("dve", omka[:], vecs[:, 10:12], -1.0, 1.0, ALU.mult, ALU.add, reads=["vecs"], writes=["omka"])
        P.memset("pool", W["msk"][:], 1.0, writes=["msk"])
        P.memset("pool", W["msk"][:].rearrange("p (c t) -> p c t", t=64)[:, :, 0:1], 0.0, writes=["msk"])
        for h in range(2):
            P.memset("pool", Vpad[h][:], 0.0, writes=[("Vpad", h)])
        mi = [0]

        def mix(nm, col, dst):
            i = mi[0] % 2; mi[0] += 1
            rb = raw[nm]
            nb, zo = W[f"nb{i}"], W[f"zo{i}"]
            P.tt("pool", nb[:], rb[:, 0:512], rb[:, 2:514], ALU.add, reads=[("raw", nm)], writes=[f"nb{i}"])
            P.act(zo[:], rb[:, 1:513], AF.Identity, scale=omm[:, col:col + 1], reads=[("raw", nm), "omm"], writes=[f"zo{i}"])
            P.stt(W[dst][:], nb[:], hm[:, col:col + 1], zo[:], ALU.mult, ALU.add, reads=[f"nb{i}", f"zo{i}", "hm"], writes=[dst])

        def tt(eng, o, a, b, op):
            P.tt(eng, W[o][:], W[a][:], W[b][:], op, reads=[a, b], writes=[o])

        for hp in range(2):
            hcols = slice(hp * 128, (hp + 1) * 128)
            rows = dict(r=768 + hp * 128, k=1024 + hp * 128, v=1280 + hp * 128, d=1536, a=1664, g=1792)
            mucol = dict(r=hp, k=2 + hp, v=4 + hp, d=6, a=7, g=8)
            for dr in range(2):
                drs = slice(dr * 64, (dr + 1) * 64)
                P.memset("pool", ST[:], 0.0, writes=[("ST", 0), ("ST", 1)])
                tiles = range(NT) if dr == 0 else range(NT - 1, -1, -1)
                for it in tiles:
                    t0 = it * 512
                    cols = slice(t0, t0 + 512)
                    for nm in ["r", "k", "v", "d", "a"] + (["g"] if dr == 1 else []):
                        rb = raw[nm]
                        lo, hi = max(t0 - 1, 0), min(t0 + 513, S)
                        dlo = lo - (t0 - 1)
                        if t0 == 0:
                            P.memset("pool", rb[:, 0:1], 0.0, writes=[("raw", nm)])
                        if t0 + 512 == S:
                            P.memset("pool", rb[:, 513:514], 0.0, writes=[("raw", nm)])
                        P.load("sp", rb[:, dlo:dlo + (hi - lo)], FEAT[rows[nm]:rows[nm] + 128, lo:hi], writes=[("raw", nm)])
                    mix("r", mucol["r"], "zr"); mix("k", mucol["k"], "zk"); mix("v", mucol["v"], "zv")
                    mix("d", mucol["d"], "zd"); mix("a", mucol["a"], "za")
                    P.act(W["zd"][:], W["zd"][:], AF.Tanh, reads=["zd"], writes=["zd"])
                    pa, pk = full()
                    P.mm(pa, w2s[drs, hcols], W["zd"][drs, :], reads=["w2s", "zd"], writes=pk)
                    P.act(W["sgm"][:], pa, AF.Sigmoid, bias=vecs[:, dr * 2 + hp:dr * 2 + hp + 1], reads=pk + ["vecs"], writes=["sgm"])
                    pa, pk = full()
                    P.mm(pa, a2s[drs, hcols], W["za"][drs, :], reads=["a2s", "za"], writes=pk)
                    P.act(W["aa"][:], pa, AF.Sigmoid, bias=vecs[:, 4 + dr * 2 + hp:5 + dr * 2 + hp], reads=pk + ["vecs"], writes=["aa"])
                    P.ts("dve", W["kkr"][:], W["zk"][:], vecs[:, 8 + hp:9 + hp], None, ALU.mult, reads=["zk", "vecs"], writes=["kkr"])
                    tt("pool", "sq", "kkr", "kkr", ALU.mult)
                    pa, pk = full()
                    P.mm(pa, blk1, W["sq"][:], reads=["csts", "sq"], writes=pk)
                    P.act(W["nrm"][:], pa, AF.Sqrt, reads=pk, writes=["nrm"])
                    P.ts("dve", W["nrm"][:], W["nrm"][:], 1e-12, None, ALU.max, reads=["nrm"], writes=["nrm"])
                    P.op("dve", lambda e: e.reciprocal(W["nrm"][:], W["nrm"][:]), reads=["nrm"], writes=["nrm"])
                    tt("dve", "kkn", "kkr", "nrm", ALU.mult)
                    P.ts("pool", W["tk"][:], W["aa"][:], vecs[:, 10 + hp:11 + hp], omka[:, hp:hp + 1], ALU.mult, ALU.add,
                         reads=["aa", "vecs", "omka"], writes=["tk"])
                    tt("pool", "kd", "tk", "zk", ALU.mult)
                    tt("pool", "bb", "kkn", "aa", ALU.mult)
                    P.op("dve", lambda e: e.tensor_tensor_scan(W["csf"][:], W["msk"][:], W["sgm"][:], 0.0, ALU.mult, ALU.add),
                         reads=["msk", "sgm"], writes=["csf"])
                    tt("pool", "cse", "csf", "sgm", ALU.subtract)
                    cends = W["csf"][:].rearrange("p (c t) -> p c t", t=64)[:, :, 63]
                    P.act(wtot[:], cends, AF.Exp, scale=-C_, reads=["csf"], writes=["wtot"])
                    wtb = wtot[:].unsqueeze(2).to_broadcast([128, 8, 64])
                    v3 = lambda ap: ap.rearrange("p (c t) -> p c t", t=64)
                    at_, rt_ = ar[:, 0, :], ar[:, 1, :]
                    if dr == 0:
                        P.act(W["E1"][:], W["cse"][:], AF.Exp, scale=-C_, reads=["cse"], writes=["E1"])
                        P.act(W["E2"][:], W["csf"][:], AF.Exp, scale=-C_, reads=["csf"], writes=["E2"])
                        P.act(W["E3"][:], W["csf"][:], AF.Exp, scale=C_, reads=["csf"], writes=["E3"])
                        P.stt(at_, W["kkn"][:], -1.0, W["E1"][:], ALU.mult, ALU.mult, reads=["kkn", "E1"], writes=["ar"])
                        P.tt("pool", rt_, W["zr"][:], W["E2"][:], ALU.mult, reads=["zr", "E2"], writes=["ar"])
                        tt("dve", "bt", "bb", "E3", ALU.mult)
                        tt("pool", "kt", "kd", "E3", ALU.mult)
                        P.tt("dve", v3(W["Bh"][:]), v3(W["bt"][:]), wtb, ALU.mult, reads=["bt", "wtot"], writes=["Bh"])
                        P.tt("pool", v3(W["Kh"][:]), v3(W["kt"][:]), wtb, ALU.mult, reads=["kt", "wtot"], writes=["Kh"])
                    else:
                        P.act(iwtot[:], cends, AF.Exp, scale=C_, reads=["csf"], writes=["iwtot"])
                        iwb = iwtot[:].unsqueeze(2).to_broadcast([128, 8, 64])
                        P.act(W["E1"][:], W["csf"][:], AF.Exp, scale=C_, reads=["csf"], writes=["E1"])
                        P.act(W["E2"][:], W["cse"][:], AF.Exp, scale=C_, reads=["cse"], writes=["E2"])
                        P.act(W["E3"][:], W["cse"][:], AF.Exp, scale=-C_, reads=["cse"], writes=["E3"])
                        P.stt(at_, W["kkn"][:], -1.0, W["E1"][:], ALU.mult, ALU.mult, reads=["kkn", "E1"], writes=["ar"])
                        P.tt("dve", v3(at_), v3(at_), wtb, ALU.mult, reads=["ar", "wtot"], writes=["ar"])
                        P.tt("pool", rt_, W["zr"][:], W["E2"][:], ALU.mult, reads=["zr", "E2", "ar"], writes=["ar"])
                        P.tt("pool", v3(rt_), v3(rt_), wtb, ALU.mult, reads=["ar", "wtot"], writes=["ar"])
                        tt("dve", "Bh", "bb", "E3", ALU.mult)
                        tt("pool", "Kh", "kd", "E3", ALU.mult)
                        P.tt("dve", v3(W["bt"][:]), v3(W["Bh"][:]), iwb, ALU.mult, reads=["Bh", "iwtot"], writes=["bt"])
                        P.tt("pool", v3(W["kt"][:]), v3(W["Kh"][:]), iwb, ALU.mult, reads=["Kh", "iwtot"], writes=["kt"])
                    blocks = range(4) if dr == 0 else range(3, -1, -1)
                    for bidx in blocks:
                        cb = slice(bidx * 128, (bidx + 1) * 128)
                        qa, qk = quarter()
                        P.mm(qa, ar[:, 0, cb], identf, reads=["ar", "csts"], writes=qk)
                        for hh in range(2):
                            P.copy(cx.evac_eng(), X[hh][0][:, 0:64], qa[:, hh * 64:(hh + 1) * 64], reads=qk, writes=[("X", hh, 0)])
                        qa, qk = quarter()
                        P.mm(qa, W["zv"][:, cb], identf, reads=["zv", "csts"], writes=qk)
                        for hh in range(2):
                            P.copy(cx.evac_eng(), Vpad[hh][:, 64:128], qa[:, hh * 64:(hh + 1) * 64], reads=qk, writes=[("Vpad", hh)])
                        qa, qk = quarter()
                        P.mm(qa, W["Bh"][:, cb], identf, reads=["Bh", "csts"], writes=qk)
                        P.copy(cx.evac_eng(), Bhtm[:], qa, reads=qk, writes=["Bhtm"])
                        qa, qk = quarter()
                        P.mm(qa, W["Kh"][:, cb], identf, reads=["Kh", "csts"], writes=qk)
                        P.copy(cx.evac_eng(), Khtm[:], qa, reads=qk, writes=["Khtm"])
                        HH = (0, 1)
                        hsl = [slice(hh * 64, (hh + 1) * 64) for hh in HH]
                        pend = []
                        for hh in HH:
                            ha, hk = half()
                            P.mm(ha, W["bt"][hsl[hh], cb], ar[hsl[hh], :, cb], reads=["bt", "ar"], writes=hk)
                            pend.append((ha, hk))
                        for hh in HH:
                            ha, hk = pend[hh]
                            P.tt("dve", LRs[hh][:], ha, m1s[:, dr, :], ALU.mult, reads=hk + ["m1s"], writes=[("LRs", hh)])
                        pend = []
                        for hh in HH:
                            ha, hk = half()
                            P.mm(ha, W["kt"][hsl[hh], cb], ar[hsl[hh], :, cb], reads=["kt", "ar"], writes=hk)
                            pend.append((ha, hk))
                        for hh in HH:
                            ha, hk = pend[hh]
                            P.tt("dve", KRs[hh][:], ha, m1s[:, dr, :], ALU.mult, reads=hk + ["m1s"], writes=[("KRs", hh)])
                        pend = []
                        for hh in HH:
                            qa, qk = quarter()
                            P.mm(qa, ar[hsl[hh], 0, cb], W["bt"][hsl[hh], cb], reads=["bt", "ar"], writes=qk)
                            pend.append((qa, qk))
                        for hh in HH:
                            qa, qk = pend[hh]
                            P.tt("dve", P0s[hh][:], qa, m2s[:, dr, :], ALU.mult, reads=qk + ["m2s"], writes=[("P0s", hh)])
                        pend = []
                        for hh in HH:
                            qa, qk = quarter()
                            P.mm(qa[:, 0:64], KRs[hh][:, 0:128], Vpad[hh][:, 64:128], reads=[("KRs", hh), ("Vpad", hh)], writes=qk)
                            pend.append((qa, qk))
                        for hh in HH:
                            qa, qk = pend[hh]
                            P.copy("act", X[hh][0][:, 64:128], qa[:, 0:64], reads=qk, writes=[("X", hh, 0)])
                        L_ap = [LRs[hh][:, 0:128] for hh in HH]
                        P_ap = [P0s[hh][:] for hh in HH]
                        Lk = [[("LRs", hh)] for hh in HH]
                        Pk = [[("P0s", hh)] for hh in HH]
                        xc = 0
                        for lvl in range(6):
                            pend = []
                            for hh in HH:
                                qa, qk = quarter()
                                P.mm(qa, L_ap[hh], X[hh][xc][:], reads=Lk[hh] + [("X", hh, xc)], writes=qk)
                                pend.append((qa, qk))
                            for hh in HH:
                                qa, qk = pend[hh]
                                P.tt("dve", X[hh][1 - xc][:], qa, X[hh][xc][:], ALU.add, reads=qk + [("X", hh, xc)], writes=[("X", hh, 1 - xc)])
                            xc = 1 - xc
                            if lvl < 5:
                                par = lvl % 2
                                pend = []
                                for hh in HH:
                                    qa, qk = quarter()
                                    P.mm(qa, P_ap[hh], L_ap[hh], reads=Lk[hh] + Pk[hh], writes=qk)
                                    pend.append((qa, qk))
                                for hh in HH:
                                    qa, qk = pend[hh]
                                    P.copy("act", LP[hh][par][:, 0:128], qa, reads=qk, writes=[("LPa", hh, par)])
                                pend = []
                                for hh in HH:
                                    qa, qk = quarter()
                                    P.mm(qa, L_ap[hh], P_ap[hh], reads=Lk[hh] + Pk[hh], writes=qk)
                                    pend.append((qa, qk))
                                for hh in HH:
                                    qa, qk = pend[hh]
                                    P.copy("act", LP[hh][par][:, 128:256], qa, reads=qk, writes=[("LPb", hh, par)])
                                for hh in HH:
                                    L_ap[hh], P_ap[hh] = LP[hh][par][:, 0:128], LP[hh][par][:, 128:256]
                                    Lk[hh] = [("LPa", hh, par)]
                                    Pk[hh] = [("LPb", hh, par)]
                        Xf = [X[hh][xc] for hh in HH]
                        Xfk = [[("X", hh, xc)] for hh in HH]
                        pend = []
                        for hh in HH:
                            ga, gk = quarter()
                            P.mm(ga, Xf[hh][:], LRs[hh][:, 128:256], start=True, stop=False, reads=Xfk[hh] + [("LRs", hh)], writes=gk)
                            P.mm(ga, Vpad[hh][:], KRs[hh][:, 128:256], start=False, stop=True, reads=[("Vpad", hh), ("KRs", hh)], writes=gk)
                            pend.append((ga, gk))
                        for hh in HH:
                            ga, gk = pend[hh]
                            P.tt("dve", RpT[hsl[hh], :], ga[0:64, :], ar[hsl[hh], 1, cb], ALU.add, reads=gk + ["ar"], writes=[("RpT", hh)])
                            P.copy("act", W["ytile"][hsl[hh], cb], ga[64:128, :], reads=gk, writes=[("ytile", hh)])
                        for c in ((0, 1) if dr == 0 else (1, 0)):
                            cc = slice(c * 64, (c + 1) * 64)
                            ycols = slice(bidx * 128 + c * 64, bidx * 128 + (c + 1) * 64)
                            wcol = bidx * 2 + c
                            pend = []
                            for hh in HH:
                                hs = hsl[hh]
                                qa, qk = quarter()
                                P.mm(qa[hs, 0:64], ST[hs, :], RpT[hs, cc], reads=[("ST", hh), ("RpT", hh)], writes=qk)
                                pend.append((qa, qk))
                            for hh in HH:
                                hs = hsl[hh]
                                qa, qk = pend[hh]
                                P.tt("dve", W["ytile"][hs, ycols], qa[hs, 0:64], W["ytile"][hs, ycols], ALU.add,
                                     reads=qk + [("ytile", hh)], writes=[("ytile", hh)])
                            pend = []
                            for hh in HH:
                                hs = hsl[hh]
                                qa, qk = quarter()
                                P.mm(qa[hs, 0:64], Xf[hh][cc, 0:64], Bhtm[cc, hs], reads=Xfk[hh] + ["Bhtm"], writes=qk)
                                P.mm(qa[hs, 64:128], Bhtm[cc, hs], Xf[hh][cc, 64:128], start=True, stop=False, reads=Xfk[hh] + ["Bhtm"], writes=qk)
                                P.mm(qa[hs, 64:128], Khtm[cc, hs], Vpad[hh][cc, 64:128], start=False, stop=True, reads=["Khtm", ("Vpad", hh)], writes=qk)
                                pend.append((qa, qk))
                            for hh in HH:
                                hs = hsl[hh]
                                qa, qk = pend[hh]
                                P.stt(MTs[hs, :], ident64[hs, :], wtot[hs, wcol:wcol + 1], qa[hs, 0:64], ALU.mult, ALU.add,
                                      reads=qk + ["csts", "wtot"], writes=[("MTs", hh)])
                                P.copy("act", Nns[hs, :], qa[hs, 64:128], reads=qk, writes=[("Nns", hh)])
                            pend = []
                            for hh in HH:
                                hs = hsl[hh]
                                qa, qk = quarter()
                                P.mm(qa[hs, 0:64], MTs[hs, :], ST[hs, :], reads=[("MTs", hh), ("ST", hh)], writes=qk)
                                pend.append((qa, qk))
                            for hh in HH:
                                hs = hsl[hh]
                                qa, qk = pend[hh]
                                P.tt("dve", ST[hs, :], qa[hs, 0:64], Nns[hs, :], ALU.add, reads=qk + [("Nns", hh)], writes=[("ST", hh)])
                    if dr == 0:
                        P.load("sp", YF[hp * 128:(hp + 1) * 128, cols], W["ytile"][:], reads=[("ytile", 0), ("ytile", 1)], writes=["YF"])
                    else:
                        P.load("sp", W["yf"][:], YF[hp * 128:(hp + 1) * 128, cols], reads=["YF"], writes=["yf"])
                        P.tt("pool", W["yf"][:], W["ytile"][:], W["yf"][:], ALU.add, reads=[("ytile", 0), ("ytile", 1), "yf"], writes=["yf"])
                        pa, pk = full()
                        P.mm(pa, blkm, W["yf"][:], reads=["csts", "yf"], writes=pk)
                        P.tt("dve", W["yc"][:], W["yf"][:], pa, ALU.subtract, reads=pk + ["yf"], writes=["yc"])
                        tt("pool", "sq", "yc", "yc", ALU.mult)
                        pa, pk = full()
                        P.mm(pa, blkm, W["sq"][:], reads=["csts", "sq"], writes=pk)
                        P.act(W["nrm"][:], pa, AF.Sqrt, bias=64e-5, reads=pk, writes=["nrm"])
                        P.op("dve", lambda e: e.reciprocal(W["nrm"][:], W["nrm"][:]), reads=["nrm"], writes=["nrm"])
                        tt("pool", "yn", "yc", "nrm", ALU.mult)
                        P.ts("pool", W["yn"][:], W["yn"][:], vecs[:, 14 + hp:15 + hp], vecs[:, 16 + hp:17 + hp], ALU.mult, ALU.add,
                             reads=["yn", "vecs"], writes=["yn"])
                        pa, pk = full()
                        P.mm(pa, a2s[0:64, hcols], W["za"][0:64, :], reads=["a2s", "za"], writes=pk)
                        P.act(W["aa0"][:], pa, AF.Sigmoid, bias=vecs[:, 4 + hp:5 + hp], reads=pk + ["vecs"], writes=["aa0"])
                        P.ts("pool", W["tk0"][:], W["aa0"][:], vecs[:, 10 + hp:11 + hp], omka[:, hp:hp + 1], ALU.mult, ALU.add,
                             reads=["aa0", "vecs", "omka"], writes=["tk0"])
                        tt("pool", "tk0", "tk0", "tk", ALU.add)
                        tt("pool", "tk0", "tk0", "zk", ALU.mult)
                        P.stt(W["tmpb"][:], W["zr"][:], vecs[:, 12 + hp:13 + hp], W["tk0"][:], ALU.mult, ALU.mult,
                              reads=["zr", "vecs", "tk0"], writes=["tmpb"])
                        pa, pk = full()
                        P.mm(pa, blk1, W["tmpb"][:], reads=["csts", "tmpb"], writes=pk)
                        P.tt("dve", W["tmpb"][:], pa, W["zv"][:], ALU.mult, reads=pk + ["zv", "tmpb"], writes=["tmpb"])
                        tt("pool", "yn", "yn", "tmpb", ALU.add)
                        mix("g", mucol["g"], "zg")
                        P.act(W["zg"][:], W["zg"][:], AF.Sigmoid, reads=["zg"], writes=["zg"])
                        pa, pk = full()
                        P.mm(pa, g2s[:, hcols], W["zg"][:], reads=["g2s", "zg"], writes=pk)
                        rb_ = it % 2
                        P.tt("dve", recb[:, rb_, :], pa, W["yn"][:], ALU.mult, reads=pk + ["yn"], writes=[("recb", rb_)])
                        P.load("sp", as_chunked(REC, S).sl(hp * 128, (hp + 1) * 128, t0, t0 + 512), recb[:, rb_, :],
                               reads=[("recb", rb_)], writes=["REC"])


def build_B2(S, dbg=0):
    nc = bass.Bass("TRN2", target_bir_lowering=False)
    I = "ExternalInput"
    d = dict(FEAT=dram(nc, "FEAT", [1920, S], BF16, I), mu9=dram(nc, "mu9", [128, 9], F32, I), w2d=dram(nc, "w2d", [128, 256], F32, I),
             a2d=dram(nc, "a2d", [128, 256], F32, I), g2d=dram(nc, "g2d", [128, 256], F32, I), vec=dram(nc, "vec", [128, 20], F32, I),
             m1d=dram(nc, "m1d", [2, 128, 256], F32, I), m2d=dram(nc, "m2d", [2, 128, 128], F32, I), cst=dram(nc, "cst", [128, 448], F32, I),
             REC=dram(nc, "REC", [256, S], BF16, "ExternalOutput"), YF=dram(nc, "YF", [256, S], F32, "Internal"))
    cx = Ctx(nc)
    with cx.st:
        cx.alloc_psum()
        emit_B2(cx, d, S, "B2_", dbg)
        cnt = cx.P.emit(final_wait_keys=["REC"])
    return nc, cnt


bf = ml_dtypes.bfloat16
def rope_tables(S):
    inv = (1.0 / (np.float32(10000.0) ** (np.arange(0, 64, 2, dtype=np.float32) / np.float32(64)))).astype(np.float32)
    ang = (np.arange(S, dtype=np.float32)[:, None] * inv[None, :]).astype(np.float32)
    cos = np.cos(ang).astype(np.float32); sin = np.sin(ang).astype(np.float32)
    idx = np.arange(128) % 32
    return np.ascontiguousarray(cos[:, idx].T), np.ascontiguousarray(sin[:, idx].T)
def rot_lhsT():
    Pm = np.zeros((128, 128), np.float32)
    for p in range(128):
        if (p % 64) < 32: Pm[p, p + 32] = -1.0
        else: Pm[p, p - 32] = 1.0
    return np.ascontiguousarray(Pm.T)

def colv(v):
    return np.ascontiguousarray(np.asarray(v, np.float32).reshape(-1, 128).T)

def b2_masks():
    t = np.arange(128)[:, None]; j = np.arange(128)[None, :]
    same = (t // 64) == (j // 64)
    m1 = np.zeros((2, 128, 256), np.float32); m2 = np.zeros((2, 128, 128), np.float32)
    for dr in range(2):
        strict = same & ((j < t) if dr == 0 else (j > t))
        incl = same & ((j <= t) if dr == 0 else (j >= t))
        m2[dr] = strict
        m1[dr][:, 0:128] = strict.T
        m1[dr][:, 128:256] = incl.T
    return m1, m2

def b2_consts():
    blk1 = np.kron(np.eye(2), np.ones((64, 64))).astype(np.float32)
    ident64 = np.concatenate([np.eye(64), np.eye(64)], 0).astype(np.float32)
    return np.ascontiguousarray(np.concatenate([blk1, blk1 / 64.0, np.eye(128, dtype=np.float32), ident64], 1).astype(np.float32))

def b2_inputs(g, mu, w0, w2, a0, a2, g2, k_k, k_a, r_k, lnw, lnb):
    ch = slice(g * 256, (g + 1) * 256)
    mu_r, mu_k, mu_v = mu[0:512][ch], mu[512:1024][ch], mu[1024:1536][ch]
    mu9 = np.stack([mu_r[0:128], mu_r[128:256], mu_k[0:128], mu_k[128:256], mu_v[0:128], mu_v[128:256],
                    mu[1536:1664], mu[1664:1792], mu[1792:1920]], 1).astype(np.float32)
    w2d = np.ascontiguousarray(w2[:, :, ch].reshape(128, 256)); a2d = np.ascontiguousarray(a2[:, :, ch].reshape(128, 256))
    g2d = np.ascontiguousarray(g2[:, ch])
    cols = []
    for dr in range(2):
        for hp in range(2):
            cols.append(w0[dr][ch][hp * 128:(hp + 1) * 128])
    for dr in range(2):
        for hp in range(2):
            cols.append(a0[dr][ch][hp * 128:(hp + 1) * 128])
    rk = r_k.reshape(-1)[ch]
    for vv in (k_k[ch], k_a[ch], rk, lnw[ch], lnb[ch]):
        cols.append(vv[0:128]); cols.append(vv[128:256])
    cols.append(np.zeros(128)); cols.append(np.zeros(128))
    vec = np.ascontiguousarray(np.stack(cols, 1).astype(np.float32))
    m1, m2 = b2_masks()
    return dict(mu9=np.ascontiguousarray(mu9), w2d=w2d.astype(np.float32), a2d=a2d.astype(np.float32), g2d=g2d.astype(np.float32),
                vec=vec, m1d=m1, m2d=m2, cst=b2_consts())


_PROGS = {}
_RUN = [None]


def _prog(name, fn, arg):
    key = (name, arg)
    if key not in _PROGS:
        _PROGS[key] = fn(arg)[0]
    return _PROGS[key]


def _launch(nc, in_maps):
    if _RUN[0] is not None:
        return _RUN[0](nc, in_maps)
    res = run_bass_kernel_spmd(nc, in_maps, core_ids=list(range(len(in_maps))))
    return res.results


def _in_perm():
    cols = []
    for g in range(2):
        for base in (0, 512, 1024):
            cols += list(range(base + g * 256, base + (g + 1) * 256))
        for base in (1536, 1536 + 512, 1536 + 1024):
            cols += list(range(base + g * 256, base + (g + 1) * 256))
    cols += list(range(1536 + 1536, 1536 + 1920))
    return np.array(cols)


def kernel_unfused(x, c, w_ada, b_ada, norm1, norm2, w_in, w_out, lam_q1, lam_k1, lam_q2, lam_k2,
           subln_w, tshift_mu, decay_w0, decay_w2, icl_a0, icl_a2, gate_g2, k_k, k_a, r_k,
           lnx_w, lnx_b, w_up, w_down, norm_f):
    f32 = np.float32
    A_ = lambda a: np.ascontiguousarray(np.asarray(a, f32))
    x = np.asarray(x, f32)
    B, S, Dm = x.shape
    T = S // 2
    L = np.asarray(w_in).shape[0]
    ncores = 2 * B
    ncA = _prog("A", build_A, T); ncB1 = _prog("B1", build_B1, S); ncB2 = _prog("B2", build_B2, S); ncC = _prog("C", build_C, T)
    xT = [np.ascontiguousarray(x[cid // 2, (cid % 2) * T:(cid % 2 + 1) * T, :].T) for cid in range(ncores)]
    ccol = [np.ascontiguousarray(np.repeat(colv(np.asarray(c, f32)[b])[:, :, None], 2, axis=2)) for b in range(B)]
    perm = _in_perm()
    cosT, sinT = rope_tables(S)
    prot = rot_lhsT().astype(bf)
    identb = np.eye(128, dtype=f32).astype(bf)
    outT = None
    for l in range(L):
        lam_init = f32(0.8 - 0.6 * np.exp(-0.3 * l))
        wa = np.asarray(w_ada[l], f32); ba = np.asarray(b_ada[l], f32)
        winp = np.ascontiguousarray(np.asarray(w_in[l], f32)[:, perm])
        wadaA = np.ascontiguousarray(wa[:, 0:2048]); badaA = colv(ba[0:2048]); n1 = colv(np.asarray(norm1[l], f32))
        resA = _launch(ncA, [dict(xT=xT[cid], ccol=ccol[cid // 2], wada=wadaA, bada=badaA, n1=n1, win=winp) for cid in range(ncores)])
        ZT = [np.asarray(r["ZT"]) for r in resA]
        feats = []
        for cid in range(ncores):
            b, g = cid // 2, cid % 2
            parts = []
            for j in range(2):
                z = ZT[2 * b + j]
                parts.append(np.concatenate([z[g * 1536:(g + 1) * 1536], z[3072:3456]], axis=0))
            feats.append(np.ascontiguousarray(np.concatenate(parts, axis=1)))
        lam4 = np.stack([np.asarray(v[l], f32) for v in (lam_q1, lam_k1, lam_q2, lam_k2)], 0)
        lamv = np.ascontiguousarray(np.broadcast_to(lam4[None], (128, 4, 64)))
        laminit = np.full((128, 1), lam_init, f32)
        sw = np.ascontiguousarray(np.broadcast_to(np.asarray(subln_w[l], f32)[None], (128, 128)))
        resB1 = _launch(ncB1, [dict(FEAT=feats[cid], cosT=cosT, sinT=sinT, prot=prot, ident=identb, lamv=lamv, laminit=laminit, sw=sw)
                               for cid in range(ncores)])
        b2in = [b2_inputs(g, np.asarray(tshift_mu[l], f32), np.asarray(decay_w0[l], f32), np.asarray(decay_w2[l], f32),
                          np.asarray(icl_a0[l], f32), np.asarray(icl_a2[l], f32), np.asarray(gate_g2[l], f32),
                          np.asarray(k_k[l], f32), np.asarray(k_a[l], f32), np.asarray(r_k[l], f32),
                          np.asarray(lnx_w[l], f32), np.asarray(lnx_b[l], f32)) for g in range(2)]
        resB2 = _launch(ncB2, [dict(FEAT=feats[cid], **b2in[cid % 2]) for cid in range(ncores)])
        ATT = [np.asarray(r["ATT"]) for r in resB1]
        REC = [np.asarray(r["REC"]) for r in resB2]
        wadaC = np.ascontiguousarray(wa[:, 2048:6144]); badaC = colv(ba[2048:6144])
        n2 = colv(np.asarray(norm2[l], f32)); nf = colv(np.asarray(norm_f, f32))
        wo = A_(w_out[l]); wu = A_(w_up[l]); wd = A_(w_down[l])
        mapsC = []
        for cid in range(ncores):
            b, j = cid // 2, cid % 2
            ts_ = slice(j * T, (j + 1) * T)
            cat = np.ascontiguousarray(np.concatenate([ATT[2 * b][:, ts_], ATT[2 * b + 1][:, ts_], REC[2 * b][:, ts_], REC[2 * b + 1][:, ts_]], axis=0))
            mapsC.append(dict(xT=xT[cid], cat=cat, ccol=ccol[b], wada=wadaC, bada=badaC, n2=n2, nf=nf, wout=wo, wup=wu, wdn=wd))
        resC = _launch(ncC, mapsC)
        xT = [np.ascontiguousarray(np.asarray(r["XTo"], f32)) for r in resC]
        outT = [np.asarray(r["OUTT"], f32) for r in resC]
    out = np.empty((B, S, Dm), f32)
    for cid in range(ncores):
        b, j = cid // 2, cid % 2
        out[b, j * T:(j + 1) * T, :] = outT[cid].T
    return out


PAIRS = [[0, 1], [2, 3], [4, 5], [6, 7]]


def build_fused(S, L=2):
    nc = bass.Bass("TRN2", target_bir_lowering=False)
    I = "ExternalInput"
    g = {}
    g["x0"] = dram(nc, "x0", [D, S], F32, I)
    g["ccol"] = dram(nc, "ccol", [128, 8, 2], F32, I)
    for nm, shp, dt in (("cosT", [128, S], F32), ("sinT", [128, S], F32), ("prot", [128, 128], BF16), ("ident", [128, 128], BF16),
                        ("m1d", [2, 128, 256], F32), ("m2d", [2, 128, 128], F32), ("cst", [128, 448], F32), ("nf", [128, 8], F32)):
        g[nm] = dram(nc, nm, shp, dt, I)
    per = []
    for l in range(L):
        p = {}
        for nm, shp, dt in (("wadaA", [D, 2048], F32), ("badaA", [128, 16], F32), ("n1", [128, 8], F32), ("win", [D, 1920], F32),
                            ("lamv", [128, 4, 64], F32), ("laminit", [128, 1], F32), ("sw", [128, 128], F32),
                            ("mu9", [128, 9], F32), ("w2d", [128, 256], F32), ("a2d", [128, 256], F32), ("g2d", [128, 256], F32),
                            ("vec", [128, 20], F32),
                            ("wadaC", [D, 4096], F32), ("badaC", [128, 32], F32), ("n2", [128, 8], F32),
                            ("wout", [D, D], F32), ("wup", [D, 4096], F32), ("wdn", [4096, D], F32)):
            p[nm] = dram(nc, f"{nm}{l}", shp, dt, I)
        per.append(p)
    OUTT = dram(nc, "OUTT", [D, S], F32, "ExternalOutput")
    XTs = dram(nc, "XTs", [D, S], F32, "Internal")
    FEAT = dram(nc, "FEATs", [1920, S], BF16, "Internal")
    YF = dram(nc, "YFs", [256, S], F32, "Internal")
    wupB = dram(nc, "wupBs", [D, 4096], BF16, "Internal")
    CW = min(1024, S)
    NCH = S // CW
    CATg = [nc.dram_tensor(f"CATg{k}", [512, CW], BF16).ap() for k in range(NCH)]
    CATALL = [nc.dram_tensor(f"CATALL{k}", [1024, CW], BF16).ap() for k in range(NCH)]
    catg_att = Chunked([a[0:256, :] for a in CATg], CW)
    catg_rec = Chunked([a[256:512, :] for a in CATg], CW)
    catall = Chunked(CATALL, CW)
    cx = Ctx(nc)
    P = cx.P
    with cx.st:
        cx.alloc_psum()
        for l in range(L):
            p = per[l]
            emit_A(cx, dict(xT=(g["x0"] if l == 0 else XTs), ccol=g["ccol"], wada=p["wadaA"], bada=p["badaA"], n1=p["n1"],
                            win=p["win"], ZT=FEAT), S, 15, f"L{l}A_")
            emit_B1(cx, dict(FEAT=FEAT, cosT=g["cosT"], sinT=g["sinT"], prot=g["prot"], ident=g["ident"], lamv=p["lamv"],
                             laminit=p["laminit"], sw=p["sw"], ATT=catg_att), S, f"L{l}B1_")
            emit_B2(cx, dict(FEAT=FEAT, mu9=p["mu9"], w2d=p["w2d"], a2d=p["a2d"], g2d=p["g2d"], vec=p["vec"], m1d=g["m1d"],
                             m2d=g["m2d"], cst=g["cst"], REC=catg_rec, YF=YF), S, f"L{l}B2_")
            P.fence()
            for k in range(NCH):
                P.coll(lambda e, k=k: e.collective_compute("AllGather", ALU.bypass, PAIRS, ins=[CATg[k].opt()], outs=[CATALL[k].opt()]),
                       reads=["ATT", "REC"], writes=["CATALL"])
            P.fence()
            emit_C(cx, dict(xT=(g["x0"] if l == 0 else XTs), cat=catall, ccol=g["ccol"], wada=p["wadaC"], bada=p["badaC"],
                            n2=p["n2"], nf=g["nf"], wout=p["wout"], wup=p["wup"], wdn=p["wdn"], XTo=XTs, OUTT=OUTT, wupB=wupB),
                   S, f"L{l}C_", final=(l == L - 1))
        cnt = P.emit(final_wait_keys=["OUTT"])
    return nc, cnt


def kernel(x, c, w_ada, b_ada, norm1, norm2, w_in, w_out, lam_q1, lam_k1, lam_q2, lam_k2,
                 subln_w, tshift_mu, decay_w0, decay_w2, icl_a0, icl_a2, gate_g2, k_k, k_a, r_k,
                 lnx_w, lnx_b, w_up, w_down, norm_f):
    f32 = np.float32
    A_ = lambda a: np.ascontiguousarray(np.asarray(a, f32))
    x = np.asarray(x, f32)
    B, S, Dm = x.shape
    L = np.asarray(w_in).shape[0]
    ncores = 2 * B
    ncF = _prog("F", lambda s_: build_fused(s_, L), S)
    cosT, sinT = rope_tables(S)
    m1, m2 = b2_masks()
    shared = dict(cosT=cosT, sinT=sinT, prot=rot_lhsT().astype(bf), ident=np.eye(128, dtype=f32).astype(bf),
                  m1d=m1, m2d=m2, cst=b2_consts(), nf=colv(np.asarray(norm_f, f32)))
    perm = _in_perm()
    wo_rows = np.concatenate([np.arange(0, 256), np.arange(512, 768), np.arange(256, 512), np.arange(768, 1024)])
    lay = []
    for l in range(L):
        wa = np.asarray(w_ada[l], f32); ba = np.asarray(b_ada[l], f32)
        winp = np.asarray(w_in[l], f32)[:, perm]
        lam4 = np.stack([np.asarray(v[l], f32) for v in (lam_q1, lam_k1, lam_q2, lam_k2)], 0)
        com = dict(wadaA=np.ascontiguousarray(wa[:, 0:2048]), badaA=colv(ba[0:2048]), n1=colv(np.asarray(norm1[l], f32)),
                   lamv=np.ascontiguousarray(np.broadcast_to(lam4[None], (128, 4, 64))),
                   laminit=np.full((128, 1), f32(0.8 - 0.6 * np.exp(-0.3 * l)), f32),
                   sw=np.ascontiguousarray(np.broadcast_to(np.asarray(subln_w[l], f32)[None], (128, 128))),
                   wadaC=np.ascontiguousarray(wa[:, 2048:6144]), badaC=colv(ba[2048:6144]), n2=colv(np.asarray(norm2[l], f32)),
                   wout=np.ascontiguousarray(np.asarray(w_out[l], f32)[wo_rows, :]), wup=A_(w_up[l]), wdn=A_(w_down[l]))
        pg = []
        for g_ in range(2):
            dd = dict(com)
            dd["win"] = np.ascontiguousarray(np.concatenate([winp[:, g_ * 1536:(g_ + 1) * 1536], winp[:, 3072:3456]], axis=1))
            b2 = b2_inputs(g_, np.asarray(tshift_mu[l], f32), np.asarray(decay_w0[l], f32), np.asarray(decay_w2[l], f32),
                           np.asarray(icl_a0[l], f32), np.asarray(icl_a2[l], f32), np.asarray(gate_g2[l], f32),
                           np.asarray(k_k[l], f32), np.asarray(k_a[l], f32), np.asarray(r_k[l], f32),
                           np.asarray(lnx_w[l], f32), np.asarray(lnx_b[l], f32))
            for k_ in ("mu9", "w2d", "a2d", "g2d", "vec"):
                dd[k_] = b2[k_]
            pg.append(dd)
        lay.append(pg)
    xTb = [np.ascontiguousarray(x[b].T) for b in range(B)]
    in_maps = []
    for cid in range(ncores):
        b, g_ = cid // 2, cid % 2
        m = dict(shared)
        m["x0"] = xTb[b]
        m["ccol"] = np.ascontiguousarray(np.repeat(colv(np.asarray(c, f32)[b])[:, :, None], 2, axis=2))
        for l in range(L):
            for k_, v_ in lay[l][g_].items():
                m[f"{k_}{l}"] = v_
        in_maps.append(m)
    res = _launch(ncF, in_maps)
    out = np.empty((B, S, Dm), f32)
    T = S // 2
    for cid in range(ncores):
        b, j = cid // 2, cid % 2
        out[b, j * T:(j + 1) * T, :] = np.asarray(res[cid]["OUTT"], f32)[:, j * T:(j + 1) * T].T
    return out
```

```python
import numpy as np
import concourse.bass as bass
import concourse.mybir as mybir

F32 = mybir.dt.float32
BF16 = mybir.dt.bfloat16
I32 = mybir.dt.int32
AF = mybir.ActivationFunctionType
ALU = mybir.AluOpType
AX = mybir.AxisListType

EPOCH = 20000
NDMASEM = 24


class Prog:
    ENGS = ("pe", "dve", "act", "pool", "sp")

    def __init__(self, nc):
        self.nc = nc
        self.ins = []
        self.lastw = {}
        self.readers = {}
        self.ndma = 0
        self.fence_deps = set()
        self.fence_pending = set()
        self.last_eng = {}
        self.dma_since = []

    def fence(self):
        self.fence_deps = set(self.last_eng.values()) | set(self.dma_since)
        self.fence_pending = set(self.ENGS)
        self.dma_since = []

    def _add(self, eng, fn, reads, writes, dma):
        i = len(self.ins)
        excl = [k for k in reads if (isinstance(k, tuple) and k[0] == "ps") or k == "pbt"]
        writes = list(writes) + [k for k in excl if k not in writes]
        deps = set()
        for k in list(reads) + list(writes):
            if k in self.lastw:
                deps.add(self.lastw[k])
        for k in writes:
            for r in self.readers.get(k, ()):
                deps.add(r)
        if eng in self.fence_pending:
            deps |= self.fence_deps
            self.fence_pending.discard(eng)
        deps.discard(i)
        if dma:
            self.dma_since.append(i)
        else:
            self.last_eng[eng] = i
        for k in writes:
            self.lastw[k] = i
            self.readers[k] = []
        for k in reads:
            self.readers.setdefault(k, []).append(i)
        self.ins.append(dict(eng=eng, fn=fn, deps=deps, dma=dma))
        return i

    def op(self, eng, fn, reads=(), writes=()):
        return self._add(eng, fn, reads, writes, False)

    def dma(self, eng, fn, reads=(), writes=()):
        return self._add(eng, fn, reads, writes, True)

    def coll(self, fn, reads=(), writes=()):
        i = self._add("pool", fn, reads, writes, True)
        self.ins[i]["own"] = True
        return i

    def mm(self, out, lhsT, rhs, start=True, stop=True, reads=(), writes=(), **kw):
        return self.op("pe", lambda e: e.matmul(out, lhsT, rhs, start=start, stop=stop, **kw), reads, writes)

    def transpose(self, out, in_, ident, reads=(), writes=()):
        return self.op("pe", lambda e: e.transpose(out, in_, ident), reads, writes)

    def act(self, out, in_, func, reads=(), writes=(), eng="act", **kw):
        return self.op(eng, lambda e: e.activation(out, in_, func, **kw), reads, writes)

    def tt(self, eng, out, in0, in1, op, reads=(), writes=()):
        return self.op(eng, lambda e: e.tensor_tensor(out, in0, in1, op), reads, writes)

    def ts(self, eng, out, in0, s1, s2, op0, op1=None, reads=(), writes=(), **kw):
        if op1 is None:
            return self.op(eng, lambda e: e.tensor_scalar(out, in0, s1, None, op0, **kw), reads, writes)
        return self.op(eng, lambda e: e.tensor_scalar(out, in0, s1, s2, op0, op1, **kw), reads, writes)

    def stt(self, out, in0, scalar, in1, op0, op1, reads=(), writes=(), **kw):
        return self.op("dve", lambda e: e.scalar_tensor_tensor(out, in0, scalar, in1, op0, op1, **kw), reads, writes)

    def copy(self, eng, out, in_, reads=(), writes=()):
        if eng == "act":
            return self.op(eng, lambda e: e.copy(out, in_), reads, writes)
        return self.op(eng, lambda e: e.tensor_copy(out, in_), reads, writes)

    def memset(self, eng, ap, val, writes=()):
        return self.op(eng, lambda e: e.memset(ap, val), (), writes)

    def load(self, eng, out, in_, reads=(), writes=(), **kw):
        return self.dma(eng, lambda e: e.dma_start(out=out, in_=in_, **kw), reads, writes)

    def emit(self, final_wait_keys=()):
        nc = self.nc
        ins = self.ins
        fin_deps = set()
        for k in final_wait_keys:
            if k in self.lastw:
                fin_deps.add(self.lastw[k])
        needed = set()
        for r in ins:
            for d in r["deps"]:
                if ins[d]["eng"] == "pe" and r["eng"] == "pe" and not ins[d]["dma"] and not r["dma"]:
                    continue
                needed.add(d)
        needed |= fin_deps
        cnt = {e: 0 for e in self.ENGS}
        dmacnt = [0] * NDMASEM
        ndma = 0
        nown = 0
        for i, r in enumerate(ins):
            if r["dma"] and r.get("own"):
                r["dsem"] = NDMASEM + nown
                nown += 1
                r["dprev"] = 0
                r["dtarget"] = 1
            elif r["dma"]:
                s = ndma % NDMASEM
                ndma += 1
                r["dsem"] = s
                r["dprev"] = dmacnt[s]
                dmacnt[s] += 16
                r["dtarget"] = dmacnt[s]
            elif i in needed:
                cnt[r["eng"]] += 1
                r["ms"] = cnt[r["eng"]]
        nep = {e: (cnt[e] // EPOCH) + 1 for e in self.ENGS}
        import contextlib
        st = contextlib.ExitStack()
        csem = {e: [st.enter_context(nc.semaphore(f"c_{e}_{k}")) for k in range(nep[e])] for e in self.ENGS if e != "sp"}
        dsem = [st.enter_context(nc.semaphore(f"d_{k}")) for k in range(NDMASEM + nown)]
        per = {e: [] for e in self.ENGS}
        for i, r in enumerate(ins):
            per[r["eng"]].append(i)
        block = st.enter_context(nc.Block())

        def waits_for(i, known):
            r = ins[i]
            out = []
            deps = set(r["deps"])
            for d in sorted(deps):
                dr = ins[d]
                if dr["dma"]:
                    key = ("d", dr["dsem"])
                    val = dr["dtarget"]
                    sem = dsem[dr["dsem"]]
                else:
                    if dr["eng"] == "pe" and r["eng"] == "pe" and not r["dma"]:
                        continue
                    m = dr["ms"]
                    ep = (m - 1) // EPOCH
                    key = ("c", dr["eng"], ep)
                    val = m - ep * EPOCH
                    sem = csem[dr["eng"]][ep]
                    if any(k2[0] == "c" and k2[1] == dr["eng"] and k2[2] > ep for k2 in known):
                        continue
                if known.get(key, 0) >= val:
                    continue
                known[key] = val
                out.append((sem, val))
            if r["dma"]:
                key = ("d", r["dsem"])
                if r["dprev"] > 0 and known.get(key, 0) < r["dprev"]:
                    known[key] = r["dprev"]
                    out.append((dsem[r["dsem"]], r["dprev"]))
            return out

        def body(engname):
            def f(eng):
                known = {}
                for i in per[engname]:
                    r = ins[i]
                    for sem, val in waits_for(i, known):
                        eng.wait_ge(sem, val)
                    inst = r["fn"](eng)
                    if r["dma"] and r.get("own"):
                        inst.then_inc(dsem[r["dsem"]])
                    elif r["dma"]:
                        inst.then_inc(dsem[r["dsem"]], 16)
                    elif "ms" in r:
                        ep = (r["ms"] - 1) // EPOCH
                        inst.then_inc(csem[engname][ep], 1)
                if engname == "sp":
                    for d in sorted(fin_deps):
                        dr = ins[d]
                        if dr["dma"]:
                            eng.wait_ge(dsem[dr["dsem"]], dr["dtarget"])
                        else:
                            ep = (dr["ms"] - 1) // EPOCH
                            eng.wait_ge(csem[dr["eng"]][ep], dr["ms"] - ep * EPOCH)
            return f

        block.tensor(body("pe"))
        block.vector(body("dve"))
        block.scalar(body("act"))
        block.gpsimd(body("pool"))
        block.sync(body("sp"))
        st.close()
        return {e: len(per[e]) for e in self.ENGS}

import contextlib
from concourse.bass_utils import run_bass_kernel_spmd
import ml_dtypes

D = 1024
DECAY_C = 0.6065306597126334


class Ctx:
    def __init__(self, nc):
        self.nc = nc
        self.st = contextlib.ExitStack()
        self.P = Prog(nc)
        self.alt = 0
        self.cur = self.st
        self.pfx = ""
        self.banks = None
        self.pbt = None

    def alloc_psum(self):
        self.banks = [self.ps(f"bank{i}", [128, 512]) for i in range(7)]
        self.pbt = self.ps("pbt", [128, 1024], BF16)

    @contextlib.contextmanager
    def scope(self, pfx):
        self.P.fence()
        old = (self.cur, self.pfx)
        with contextlib.ExitStack() as stk:
            self.cur, self.pfx = stk, pfx
            yield
            self.P.fence()
        self.cur, self.pfx = old

    def sb(self, name, shape, dt=F32):
        return self.cur.enter_context(self.nc.sbuf_tensor(self.pfx + name, shape, dt))

    def ps(self, name, shape, dt=F32):
        return self.st.enter_context(self.nc.psum_tensor(name, shape, dt))

    def evac_eng(self):
        self.alt ^= 1
        return "dve" if self.alt else "act"


def dram(nc, name, shape, dt, kind):
    return nc.dram_tensor(name, shape, dt, kind=kind).ap()


def bk(i):
    return [("ps", i)]


class Chunked:
    def __init__(self, aps, W):
        self.aps, self.W = aps, W

    def sl(self, r0, r1, t0, t1):
        k = t0 // self.W
        assert (t1 - 1) // self.W == k
        return self.aps[k][r0:r1, t0 - k * self.W:t1 - k * self.W]

    def tile_pkt(self, t0, t1):
        k = t0 // self.W
        assert (t1 - 1) // self.W == k
        return self.aps[k].rearrange("(k p) t -> p k t", p=128)[:, :, t0 - k * self.W:t1 - k * self.W]


def as_chunked(x, S):
    return x if isinstance(x, Chunked) else Chunked([x], S)


def emit_mod(cx, wada, bada, sc, noc, wa, pm, mod, tag):
    P = cx.P
    wv = wada.rearrange("(k p) c -> p k c", p=128)
    for oc in range(noc):
        b = oc % 2
        P.load("sp", wa[:, b, :, :], wv[:, :, oc * 128:(oc + 1) * 128],
               writes=[(tag + "wa", b)])
        for k in range(8):
            P.mm(pm[:, oc, :], wa[:, b, k, :], sc[:, k, :], start=(k == 0), stop=(k == 7),
                 reads=[(tag + "wa", b), tag + "sc"], writes=bk(6))
    P.tt("dve", mod[:, 0:noc], pm[:, 0:noc, 0], bada, ALU.add, reads=bk(6) + [tag + "bada"], writes=[tag + "mod"])


def emit_rstd(cx, xt, nk, ncol, sqb, onesD, psr_ap, psr_keys, rs, rstd, xkey, tag, eps):
    P = cx.P
    for k in range(nk):
        b = k % 2
        P.act(sqb[:, b, :], xt[:, k, :], AF.Square, reads=[xkey], writes=[(tag + "sq", b)])
        P.mm(psr_ap, onesD[:], sqb[:, b, :], start=(k == 0), stop=(k == nk - 1),
             reads=[(tag + "sq", b), "onesD"], writes=psr_keys)
    P.act(rs[:], psr_ap, AF.Sqrt, bias=eps, reads=psr_keys, writes=[tag + "rs"])
    P.op("dve", lambda e: e.reciprocal(rstd[:], rs[:]), reads=[tag + "rs"], writes=[tag + "rstd"])


def emit_A(cx, d, T, noc, pfx):
    nc = cx.nc
    P = cx.P
    xT, ccol, wada, bada, n1, win, ZT = (d[k] for k in ("xT", "ccol", "wada", "bada", "n1", "win", "ZT"))
    NT = T // 512
    gs = 9 if noc == 27 else 5
    npc = 3 if noc == 27 else 2
    pw = noc * 128 // npc
    with cx.scope(pfx):
        banks = cx.banks
        pm = banks[6][:, 0:64].rearrange("p (o t) -> p o t", t=2)
        sc = cx.sb("sc", [128, 8, 2]); wa = cx.sb("wa", [128, 2, 8, 128]); mod = cx.sb("mod", [128, 16])
        badas = cx.sb("badas", [128, 16]); n1s = cx.sb("n1s", [128, 8]); gp = cx.sb("gp", [128, 8])
        onesD = cx.sb("onesD", [128, 128]); Wb = cx.sb("Wb", [128, 8, noc * 128], BF16)
        wtmp = cx.sb("wtmp", [128, 2, pw]); xt = cx.sb("xt", [128, 2, 8, 512])
        sqb = cx.sb("sqb", [128, 2, 512]); rs = cx.sb("rs", [128, 512]); rstd = cx.sb("rstd", [128, 512])
        tmp = cx.sb("tmp", [128, 2, 512]); hb = cx.sb("hb", [128, 8, 512], BF16)
        zst = cx.sb("zst", [128, 2, gs, 512], BF16)
        P.load("sp", sc[:], ccol, writes=["Asc"])
        P.load("sp", badas[:], bada, writes=["Abada"])
        P.load("sp", n1s[:], n1, writes=["n1s"])
        P.memset("pool", onesD[:], 1.0 / D, writes=["onesD"])
        P.act(sc[:], sc[:], AF.Silu, reads=["Asc"], writes=["Asc"])
        emit_mod(cx, wada, badas[:], sc, 16, wa, pm, mod, "A")
        P.stt(gp[:], mod[:, 8:16], 1.0, n1s[:], ALU.add, ALU.mult, reads=["Amod", "n1s"], writes=["gp"])
        wv = win.rearrange("(k p) c -> p k c", p=128)
        i = 0
        for k in range(8):
            for pc in range(npc):
                b = i % 2
                P.load("sp", wtmp[:, b, :], wv[:, k, pc * pw:(pc + 1) * pw], writes=[("wtmp", b)])
                P.copy(cx.evac_eng(), Wb[:, k, pc * pw:(pc + 1) * pw], wtmp[:, b, :], reads=[("wtmp", b)], writes=["Wb"])
                i += 1
        xv = xT.rearrange("(k p) t -> p k t", p=128)
        zv = ZT.rearrange("(o p) t -> p o t", p=128)
        bi = 0
        for it in range(NT):
            xb = it % 2
            cols = slice(it * 512, (it + 1) * 512)
            P.load("sp", xt[:, xb, :, :], xv[:, :, cols], writes=[("xt", xb)])
            emit_rstd(cx, xt[:, xb], 8, 512, sqb, onesD, banks[6][:], bk(6), rs, rstd, ("xt", xb), "A", 1e-6)
            for k in range(8):
                b = k % 2
                P.tt("dve", tmp[:, b, :], xt[:, xb, k, :], rstd[:], ALU.mult, reads=[("xt", xb), "Arstd"], writes=[("tmp", b)])
                P.ts("pool", hb[:, k, :], tmp[:, b, :], gp[:, k:k + 1], mod[:, k:k + 1], ALU.mult, ALU.add,
                     reads=[("tmp", b), "gp", "Amod"], writes=["hb"])
            for oc in range(noc):
                bnk = bi % 6
                bi += 1
                for k in range(8):
                    P.mm(banks[bnk][:], Wb[:, k, oc * 128:(oc + 1) * 128], hb[:, k, :], start=(k == 0), stop=(k == 7),
                         reads=["Wb", "hb"], writes=bk(bnk))
                g = (it * 3 + oc // gs) % 2
                P.copy(cx.evac_eng(), zst[:, g, oc % gs, :], banks[bnk][:], reads=bk(bnk), writes=[("zst", g)])
                if oc % gs == gs - 1:
                    P.load("sp", zv[:, oc - gs + 1:oc + 1, cols], zst[:, g, :, :], reads=[("zst", g)], writes=["ZT"])


def build_A(T, noc=27):
    nc = bass.Bass("TRN2", target_bir_lowering=False)
    d = dict(xT=dram(nc, "xT", [D, T], F32, "ExternalInput"), ccol=dram(nc, "ccol", [128, 8, 2], F32, "ExternalInput"),
             wada=dram(nc, "wada", [D, 2048], F32, "ExternalInput"), bada=dram(nc, "bada", [128, 16], F32, "ExternalInput"),
             n1=dram(nc, "n1", [128, 8], F32, "ExternalInput"), win=dram(nc, "win", [D, noc * 128], F32, "ExternalInput"),
             ZT=dram(nc, "ZT", [noc * 128, T], BF16, "ExternalOutput"))
    cx = Ctx(nc)
    with cx.st:
        cx.alloc_psum()
        emit_A(cx, d, T, noc, "A_")
        cnt = cx.P.emit(final_wait_keys=["ZT"])
    return nc, cnt


def emit_C(cx, d, T, pfx, final=True):
    nc = cx.nc
    P = cx.P
    TC = 256
    xT, cat, ccol, wada, bada, n2, nf, wout, wup, wdn, XTo, wupB = (d[k] for k in (
        "xT", "cat", "ccol", "wada", "bada", "n2", "nf", "wout", "wup", "wdn", "XTo", "wupB"))
    OUTT = d.get("OUTT")
    NT = T // TC
    with cx.scope(pfx):
        banks = cx.banks
        pm = banks[6][:, 0:64].rearrange("p (o t) -> p o t", t=2)
        sc = cx.sb("sc", [128, 8, 2]); wa = cx.sb("wa", [128, 2, 8, 128]); mod = cx.sb("mod", [128, 32])
        badas = cx.sb("badas", [128, 32]); n2s = cx.sb("n2s", [128, 8]); nfs = cx.sb("nfs", [128, 8]); gp = cx.sb("gp", [128, 8])
        onesD = cx.sb("onesD", [128, 128])
        Wo = cx.sb("Wo", [128, 8, 1024], BF16); Wd = cx.sb("Wd", [128, 32, 1024], BF16)
        wups = cx.sb("wups", [128, 2, 8, 1024], BF16)
        wtmp = cx.sb("wtmp", [128, 2, 1024]); wcb = cx.sb("wcb", [128, 2, 1024], BF16)
        xt = cx.sb("xt", [128, 8, TC]); ct = cx.sb("ct", [128, 8, TC], BF16)
        sqb = cx.sb("sqb", [128, 2, TC]); rs = cx.sb("rs", [128, TC]); rstd = cx.sb("rstd", [128, TC])
        tmp = cx.sb("tmp", [128, 2, TC]); hb = cx.sb("hb", [128, 8, TC], BF16)
        u = cx.sb("u", [128, 32, TC], BF16); ost = cx.sb("ost", [128, 8, TC])
        P.load("sp", sc[:], ccol, writes=["Csc"])
        P.load("sp", badas[:], bada, writes=["Cbada"])
        P.load("sp", n2s[:], n2, writes=["n2s"])
        P.load("sp", nfs[:], nf, writes=["nfs"])
        P.memset("pool", onesD[:], 1.0 / D, writes=["onesD"])
        P.act(sc[:], sc[:], AF.Silu, reads=["Csc"], writes=["Csc"])
        emit_mod(cx, wada, badas[:], sc, 32, wa, pm, mod, "C")
        P.stt(gp[:], mod[:, 16:24], 1.0, n2s[:], ALU.add, ALU.mult, reads=["Cmod", "n2s"], writes=["gp"])
        i = 0
        wov = wout.rearrange("(k p) c -> p k c", p=128)
        for k in range(8):
            b = i % 2
            P.load("sp", wtmp[:, b, :], wov[:, k, :], writes=[("wtmp", b)])
            P.copy(cx.evac_eng(), Wo[:, k, :], wtmp[:, b, :], reads=[("wtmp", b)], writes=["Wo"])
            i += 1
        wdv = wdn.rearrange("(k p) c -> p k c", p=128)
        for k in range(32):
            b = i % 2
            P.load("sp", wtmp[:, b, :], wdv[:, k, :], writes=[("wtmp", b)])
            P.copy(cx.evac_eng(), Wd[:, k, :], wtmp[:, b, :], reads=[("wtmp", b)], writes=["Wd"])
            i += 1
        wuv = wup.rearrange("(k p) c -> p k c", p=128)
        wubv = wupB.rearrange("(k p) c -> p k c", p=128)
        for k in range(8):
            for pc in range(4):
                b = i % 2
                P.load("sp", wtmp[:, b, :], wuv[:, k, pc * 1024:(pc + 1) * 1024], writes=[("wtmp", b)])
                P.copy(cx.evac_eng(), wcb[:, b, :], wtmp[:, b, :], reads=[("wtmp", b)], writes=[("wcb", b)])
                P.load("sp", wubv[:, k, pc * 1024:(pc + 1) * 1024], wcb[:, b, :], reads=[("wcb", b)], writes=["wupB"])
                i += 1
        xv = xT.rearrange("(k p) t -> p k t", p=128)
        catc = as_chunked(cat, T)
        xov = XTo.rearrange("(k p) t -> p k t", p=128)
        ouv = OUTT.rearrange("(k p) t -> p k t", p=128) if final else None
        bi = 0
        wi = 0
        for it in range(NT):
            cols = slice(it * TC, (it + 1) * TC)
            P.load("sp", xt[:], xv[:, :, cols], reads=[("XT", it)], writes=["xt"])
            P.load("sp", ct[:], catc.tile_pkt(it * TC, (it + 1) * TC), writes=["ct"])
            for oc in range(8):
                bnk = bi % 6; bi += 1
                pa = banks[bnk][:, 0:TC]
                for k in range(8):
                    P.mm(pa, Wo[:, k, oc * 128:(oc + 1) * 128], ct[:, k, :], start=(k == 0), stop=(k == 7),
                         reads=["Wo", "ct"], writes=bk(bnk))
                P.stt(xt[:, oc, :], pa, mod[:, oc:oc + 1], xt[:, oc, :], ALU.mult, ALU.add,
                      reads=bk(bnk) + ["Cmod", "xt"], writes=["xt"])
            emit_rstd(cx, xt, 8, TC, sqb, onesD, banks[6][:, 0:TC], bk(6), rs, rstd, "xt", "C", 1e-6)
            for k in range(8):
                b = k % 2
                P.tt("dve", tmp[:, b, :], xt[:, k, :], rstd[:], ALU.mult, reads=["xt", "Crstd"], writes=[("tmp", b)])
                P.ts("pool", hb[:, k, :], tmp[:, b, :], gp[:, k:k + 1], mod[:, 8 + k:9 + k], ALU.mult, ALU.add,
                     reads=[("tmp", b), "gp", "Cmod"], writes=["hb"])
            for pc in range(4):
                wb_ = wi % 2; wi += 1
                P.load("sp", wups[:, wb_, :, :], wubv[:, :, pc * 1024:(pc + 1) * 1024], reads=["wupB"], writes=[("wups", wb_)])
                for f in range(8):
                    fc = pc * 8 + f
                    bnk = bi % 6; bi += 1
                    pa = banks[bnk][:, 0:TC]
                    for k in range(8):
                        P.mm(pa, wups[:, wb_, k, f * 128:(f + 1) * 128], hb[:, k, :], start=(k == 0), stop=(k == 7),
                             reads=[("wups", wb_), "hb"], writes=bk(bnk))
                    b = fc % 2
                    P.act(tmp[:, b, :], pa, AF.Relu, reads=bk(bnk), writes=[("tmp", b)])
                    P.tt("pool" if fc % 2 else "dve", u[:, fc, :], tmp[:, b, :], tmp[:, b, :], ALU.mult, reads=[("tmp", b)], writes=["u"])
            for oc in range(8):
                bnk = bi % 6; bi += 1
                pa = banks[bnk][:, 0:TC]
                for fc in range(32):
                    P.mm(pa, Wd[:, fc, oc * 128:(oc + 1) * 128], u[:, fc, :], start=(fc == 0), stop=(fc == 31),
                         reads=["Wd", "u"], writes=bk(bnk))
                P.stt(xt[:, oc, :], pa, mod[:, 24 + oc:25 + oc], xt[:, oc, :], ALU.mult, ALU.add,
                      reads=bk(bnk) + ["Cmod", "xt"], writes=["xt"])
            P.load("sp", xov[:, :, cols], xt[:], reads=["xt"], writes=["XTo", ("XT", it)])
            if not final:
                continue
            emit_rstd(cx, xt, 8, TC, sqb, onesD, banks[6][:, 0:TC], bk(6), rs, rstd, "xt", "C", 1e-6)
            for k in range(8):
                P.stt(ost[:, k, :], xt[:, k, :], nfs[:, k:k + 1], rstd[:], ALU.mult, ALU.mult,
                      reads=["xt", "nfs", "Crstd"], writes=["ost"])
            P.load("sp", ouv[:, :, cols], ost[:], reads=["ost"], writes=["OUTT"])


def build_C(T):
    nc = bass.Bass("TRN2", target_bir_lowering=False)
    I = "ExternalInput"
    d = dict(xT=dram(nc, "xT", [D, T], F32, I), cat=dram(nc, "cat", [D, T], BF16, I), ccol=dram(nc, "ccol", [128, 8, 2], F32, I),
             wada=dram(nc, "wada", [D, 4096], F32, I), bada=dram(nc, "bada", [128, 32], F32, I), n2=dram(nc, "n2", [128, 8], F32, I),
             nf=dram(nc, "nf", [128, 8], F32, I), wout=dram(nc, "wout", [D, D], F32, I), wup=dram(nc, "wup", [D, 4096], F32, I),
             wdn=dram(nc, "wdn", [4096, D], F32, I), XTo=dram(nc, "XTo", [D, T], F32, "ExternalOutput"),
             OUTT=dram(nc, "OUTT", [D, T], F32, "ExternalOutput"), wupB=dram(nc, "wupB", [D, 4096], BF16, "Internal"))
    cx = Ctx(nc)
    with cx.st:
        cx.alloc_psum()
        emit_C(cx, d, T, "C_", final=True)
        cnt = cx.P.emit(final_wait_keys=["XTo", "OUTT"])
    return nc, cnt


def emit_B1(cx, d, S, pfx):
    nc = cx.nc
    P = cx.P
    FEAT, cosT, sinT, prot, identd, lamv, laminit, sw, ATT = (d[k] for k in (
        "FEAT", "cosT", "sinT", "prot", "ident", "lamv", "laminit", "sw", "ATT"))
    NT = S // 512
    NK = S // 128
    with cx.scope(pfx):
        banks = cx.banks
        pbt = cx.pbt
        QT = cx.sb("QT", [128, 2, S], BF16); KT = cx.sb("KT", [128, 2, S], BF16)
        Vtm = cx.sb("Vtm", [128, 2, NK, 129], BF16)
        raw = cx.sb("raw", [128, 2, 512], BF16); cs_ = cx.sb("cs", [128, 2, 512]); sn_ = cx.sb("sn", [128, 2, 512])
        t1 = cx.sb("t1", [128, 2, 512]); t2 = cx.sb("t2", [128, 2, 512])
        protb = cx.sb("protb", [128, 128], BF16); identb = cx.sb("identb", [128, 128], BF16)
        lamvs = cx.sb("lamvs", [128, 4, 64]); lt = cx.sb("lt", [128, 2, 64]); ls = cx.sb("ls", [128, 2]); le = cx.sb("le", [128, 2])
        lis = cx.sb("lis", [128, 1]); nlam = cx.sb("nlam", [128, 1]); oml = cx.sb("oml", [128, 1])
        sws = cx.sb("sws", [128, 128])
        pT = cx.sb("pT", [128, 3, 512], BF16)
        Osb = cx.sb("Osb", [128, 2, 4, 129])
        rs2 = cx.sb("rs2", [128, 2]); nl1 = cx.sb("nl1", [128, 1]); of = cx.sb("of", [128, 128]); of2 = cx.sb("of2", [128, 128])
        junk = cx.sb("junk", [128, 128]); ssq = cx.sb("ssq", [128, 1]); rr = cx.sb("rr", [128, 1]); rinv = cx.sb("rinv", [128, 1])
        attb = cx.sb("attb", [128, 128], BF16); attT = cx.sb("attT", [128, 2, 512], BF16)
        P.load("sp", protb[:], prot, writes=["protb"])
        P.load("sp", identb[:], identd, writes=["identb"])
        P.load("sp", lamvs[:], lamv, writes=["lamvs"])
        P.load("sp", lis[:], laminit, writes=["lis"])
        P.load("sp", sws[:], sw, writes=["sws"])
        P.tt("dve", lt[:, 0, :], lamvs[:, 0, :], lamvs[:, 1, :], ALU.mult, reads=["lamvs"], writes=["lt"])
        P.tt("dve", lt[:, 1, :], lamvs[:, 2, :], lamvs[:, 3, :], ALU.mult, reads=["lamvs"], writes=["lt"])
        P.op("dve", lambda e: e.tensor_reduce(ls[:], lt[:], AX.X, ALU.add), reads=["lt"], writes=["ls"])
        P.act(le[:], ls[:], AF.Exp, reads=["ls"], writes=["le"])
        P.tt("dve", nlam[:], le[:, 1:2], le[:, 0:1], ALU.subtract, reads=["le"], writes=["nlam"])
        P.tt("dve", nlam[:], nlam[:], lis[:], ALU.subtract, reads=["nlam", "lis"], writes=["nlam"])
        P.ts("dve", oml[:], lis[:], -1.0, 1.0, ALU.mult, ALU.add, reads=["lis"], writes=["oml"])
        P.ts("dve", sws[:], sws[:], oml[:, 0:1], None, ALU.mult, reads=["sws", "oml"], writes=["sws"])
        P.memset("pool", Vtm[:], 1.0, writes=["Vtm"])
        ri = 0
        for it in range(NT):
            cols = slice(it * 512, (it + 1) * 512)
            cb = it % 2
            P.load("sp", cs_[:, cb, :], cosT[:, cols], writes=[("cs", cb)])
            P.load("sp", sn_[:, cb, :], sinT[:, cols], writes=[("sn", cb)])
            for h in range(2):
                for wq, (dst, row0) in enumerate(((QT, h * 128), (KT, 256 + h * 128))):
                    b = ri % 2; ri += 1
                    P.load("sp", raw[:, b, :], FEAT[row0:row0 + 128, cols], writes=[("raw", b)])
                    P.mm(banks[6][:], protb[:], raw[:, b, :], reads=["protb", ("raw", b)], writes=bk(6))
                    P.tt("dve", t1[:, b, :], banks[6][:], sn_[:, cb, :], ALU.mult, reads=bk(6) + [("sn", cb)], writes=[("t1", b)])
                    P.tt("pool", t2[:, b, :], raw[:, b, :], cs_[:, cb, :], ALU.mult, reads=[("raw", b), ("cs", cb)], writes=[("t2", b)])
                    P.tt("dve", dst[:, h, cols], t1[:, b, :], t2[:, b, :], ALU.add, reads=[("t1", b), ("t2", b)],
                         writes=["QT" if wq == 0 else "KT"])
                b = ri % 2; ri += 1
                P.load("sp", raw[:, b, :], FEAT[512 + h * 128:640 + h * 128, cols], writes=[("raw", b)])
                for j in range(4):
                    kt = it * 4 + j
                    sl = (kt % 8)
                    P.transpose(pbt[:, sl * 128:(sl + 1) * 128], raw[:, b, j * 128:(j + 1) * 128], identb[:],
                                reads=[("raw", b), "identb"], writes=["pbt"])
                    P.copy(cx.evac_eng(), Vtm[:, h, kt, 0:128], pbt[:, sl * 128:(sl + 1) * 128], reads=["pbt"], writes=["Vtm"])
        steps = [(h, qt, c, kt) for h in range(2) for qt in range(NT) for c in range(2) for kt in range(NK)]
        ti = 0

        def emit_qk(i):
            h, qt, c, kt = steps[i]
            cs = slice(c * 64, (c + 1) * 64)
            P.mm(banks[i % 2][:], KT[cs, h, kt * 128:(kt + 1) * 128], QT[cs, h, qt * 512:(qt + 1) * 512],
                 reads=["KT", "QT"], writes=bk(i % 2))

        emit_qk(0)
        for i, (h, qt, c, kt) in enumerate(steps):
            qcols = slice(qt * 512, (qt + 1) * 512)
            if i + 1 < len(steps):
                emit_qk(i + 1)
            pb = i % 3
            P.act(pT[:, pb, :], banks[i % 2][:], AF.Exp, scale=0.125, reads=bk(i % 2), writes=[("pT", pb)])
            for qs in range(4):
                P.mm(banks[2 + qs][:, 0:129], pT[:, pb, qs * 128:(qs + 1) * 128], Vtm[:, h, kt, :],
                     start=(kt == 0), stop=(kt == NK - 1), reads=[("pT", pb), "Vtm"], writes=bk(2 + qs))
            if kt != NK - 1:
                continue
            for qs in range(4):
                P.copy("dve" if qs % 2 == 0 else "act", Osb[:, c, qs, :], banks[2 + qs][:, 0:129], reads=bk(2 + qs), writes=[("Osb", c)])
            if c != 1:
                continue
            ab = (h * NT + qt) % 2
            for qs in range(4):
                P.op("dve", lambda e, qs=qs: e.reciprocal(rs2[:], Osb[:, :, qs, 128]), reads=[("Osb", 0), ("Osb", 1)], writes=["rs2"])
                P.tt("dve", nl1[:], rs2[:, 1:2], nlam[:], ALU.mult, reads=["rs2", "nlam"], writes=["nl1"])
                P.ts("dve", of[:], Osb[:, 0, qs, 0:128], rs2[:, 0:1], None, ALU.mult, reads=[("Osb", 0), "rs2"], writes=["of"])
                P.stt(of2[:], Osb[:, 1, qs, 0:128], nl1[:, 0:1], of[:], ALU.mult, ALU.add, reads=[("Osb", 1), "nl1", "of"], writes=["of2"])
                P.act(junk[:], of2[:], AF.Square, accum_out=ssq[:], reads=["of2"], writes=["junk", "ssq"])
                P.act(rr[:], ssq[:], AF.Sqrt, scale=1.0 / 128, bias=1e-5, reads=["ssq"], writes=["rr"])
                P.op("dve", lambda e: e.reciprocal(rinv[:], rr[:]), reads=["rr"], writes=["rinv"])
                P.stt(attb[:], of2[:], rinv[:, 0:1], sws[:], ALU.mult, ALU.mult, reads=["of2", "rinv", "sws"], writes=["attb"])
                sl = ti % 8; ti += 1
                P.transpose(pbt[:, sl * 128:(sl + 1) * 128], attb[:], identb[:], reads=["attb", "identb"], writes=["pbt"])
                P.copy("dve", attT[:, ab, qs * 128:(qs + 1) * 128], pbt[:, sl * 128:(sl + 1) * 128],
                       reads=["pbt"], writes=[("attT", ab)])
            P.load("sp", as_chunked(ATT, S).sl(h * 128, (h + 1) * 128, qt * 512, (qt + 1) * 512), attT[:, ab, :],
                   reads=[("attT", ab)], writes=["ATT"])


def build_B1(S):
    nc = bass.Bass("TRN2", target_bir_lowering=False)
    I = "ExternalInput"
    d = dict(FEAT=dram(nc, "FEAT", [1920, S], BF16, I), cosT=dram(nc, "cosT", [128, S], F32, I), sinT=dram(nc, "sinT", [128, S], F32, I),
             prot=dram(nc, "prot", [128, 128], BF16, I), ident=dram(nc, "ident", [128, 128], BF16, I),
             lamv=dram(nc, "lamv", [128, 4, 64], F32, I), laminit=dram(nc, "laminit", [128, 1], F32, I),
             sw=dram(nc, "sw", [128, 128], F32, I), ATT=dram(nc, "ATT", [256, S], BF16, "ExternalOutput"))
    cx = Ctx(nc)
    with cx.st:
        cx.alloc_psum()
        emit_B1(cx, d, S, "B1_")
        cnt = cx.P.emit(final_wait_keys=["ATT"])
    return nc, cnt


def emit_B2(cx, d, S, pfx, dbg=0):
    nc = cx.nc
    P = cx.P
    FEAT, mu9, w2d, a2d, g2d, vec, m1d, m2d, cst, REC, YF = (d[k] for k in (
        "FEAT", "mu9", "w2d", "a2d", "g2d", "vec", "m1d", "m2d", "cst", "REC", "YF"))
    NT = S // 512
    C_ = DECAY_C
    with cx.scope(pfx):
        banks = cx.banks
        names = ["nb0", "nb1", "zo0", "zo1", "zr", "zk", "zv", "zd", "za", "zg", "sgm", "aa", "aa0", "kkr", "sq", "nrm",
                 "kkn", "tk", "tk0", "kd", "bb", "csf", "cse", "E1", "E2", "E3", "bt", "kt", "Bh", "Kh", "msk",
                 "ytile", "yf", "yc", "tmpb", "yn"]
        W = {n: cx.sb("w_" + n, [128, 512]) for n in names}
        ar = cx.sb("ar", [128, 2, 512])
        raw = {n: cx.sb("raw_" + n, [128, 514], BF16) for n in ["r", "k", "v", "d", "a", "g"]}
        mus = cx.sb("mus", [128, 9]); omm = cx.sb("omm", [128, 9]); hm = cx.sb("hm", [128, 9])
        w2s = cx.sb("w2s", [128, 256]); a2s = cx.sb("a2s", [128, 256]); g2s = cx.sb("g2s", [128, 256])
        vecs = cx.sb("vecs", [128, 20]); omka = cx.sb("omka", [128, 2])
        m1s = cx.sb("m1s", [128, 2, 256]); m2s = cx.sb("m2s", [128, 2, 128]); csts = cx.sb("csts", [128, 448])
        wtot = cx.sb("wtot", [128, 8]); iwtot = cx.sb("iwtot", [128, 8])
        X = [[[cx.sb(f"X{b}{h}{p}", [128, 128]) for p in range(2)] for h in range(2)] for b in range(2)]
        Vpad = [[cx.sb(f"Vpad{b}{h}", [128, 128]) for h in range(2)] for b in range(2)]
        Bhtm = [cx.sb(f"Bhtm{b}", [128, 128]) for b in range(2)]; Khtm = [cx.sb(f"Khtm{b}", [128, 128]) for b in range(2)]
        LRs = [[cx.sb(f"LRs{b}{h}", [128, 256]) for h in range(2)] for b in range(2)]
        KRs = [[cx.sb(f"KRs{b}{h}", [128, 256]) for h in range(2)] for b in range(2)]
        P0s = [[cx.sb(f"P0s{b}{h}", [128, 128]) for h in range(2)] for b in range(2)]
        LP = [[[cx.sb(f"LP{b}{h}{p}", [128, 256]) for p in range(2)] for h in range(2)] for b in range(2)]
        RpT = [cx.sb(f"RpT{b}", [128, 128]) for b in range(2)]
        ST = cx.sb("ST", [128, 64]); MTs = cx.sb("MTs", [128, 64]); Nns = cx.sb("Nns", [128, 64])
        recb = cx.sb("recb", [128, 2, 512], BF16)
        blk1 = csts[:, 0:128]; blkm = csts[:, 128:256]; identf = csts[:, 256:384]; ident64 = csts[:, 384:448]
        ctr = dict(f=0, h=0, q=0)

        def full():
            i = ctr["f"] % 2; ctr["f"] += 1
            return banks[i][:], bk(i)

        def half():
            b = 2 + ctr["q"] % 5; ctr["q"] += 1
            return banks[b][:, 0:256], bk(b)

        def quarter():
            b = 2 + ctr["q"] % 5; ctr["q"] += 1
            return banks[b][:, 0:128], bk(b)

        for (t_, d_, k_) in ((mus, mu9, "mus"), (w2s, w2d, "w2s"), (a2s, a2d, "a2s"), (g2s, g2d, "g2s"), (vecs, vec, "vecs"), (csts, cst, "csts")):
            P.load("sp", t_[:], d_, writes=[k_])
        for dr in range(2):
            P.load("sp", m1s[:, dr, :], m1d[dr], writes=["m1s"])
            P.load("sp", m2s[:, dr, :], m2d[dr], writes=["m2s"])
        P.ts("dve", omm[:], mus[:], -1.0, 1.0, ALU.mult, ALU.add, reads=["mus"], writes=["omm"])
        P.ts("dve", hm[:], mus[:], 0.5, None, ALU.mult, reads=["mus"], writes=["hm"])
        P.ts("dve", omka[:], vecs[:, 10:12], -1.0, 1.0, ALU.mult, ALU.add, reads=["vecs"], writes=["omka"])
        P.memset("pool", W["msk"][:], 1.0, writes=["msk"])
        P.memset("pool", W["msk"][:].rearrange("p (c t) -> p c t", t=64)[:, :, 0:1], 0.0, writes=["msk"])
        for b_ in range(2):
            for h in range(2):
                P.memset("pool", Vpad[b_][h][:], 0.0, writes=[("Vpad", b_, h)])
        mi = [0]

        def mix(nm, col, dst):
            i = mi[0] % 2; mi[0] += 1
            rb = raw[nm]
            nb, zo = W[f"nb{i}"], W[f"zo{i}"]
            P.tt("pool", nb[:], rb[:, 0:512], rb[:, 2:514], ALU.add, reads=[("raw", nm)], writes=[f"nb{i}"])
            P.act(zo[:], rb[:, 1:513], AF.Identity, scale=omm[:, col:col + 1], reads=[("raw", nm), "omm"], writes=[f"zo{i}"])
            P.stt(W[dst][:], nb[:], hm[:, col:col + 1], zo[:], ALU.mult, ALU.add, reads=[f"nb{i}", f"zo{i}", "hm"], writes=[dst])

        def tt(eng, o, a, b, op):
            P.tt(eng, W[o][:], W[a][:], W[b][:], op, reads=[a, b], writes=[o])

        for hp in range(2):
            hcols = slice(hp * 128, (hp + 1) * 128)
            rows = dict(r=768 + hp * 128, k=1024 + hp * 128, v=1280 + hp * 128, d=1536, a=1664, g=1792)
            mucol = dict(r=hp, k=2 + hp, v=4 + hp, d=6, a=7, g=8)
            for dr in range(2):
                drs = slice(dr * 64, (dr + 1) * 64)
                P.memset("pool", ST[:], 0.0, writes=[("ST", 0), ("ST", 1)])
                tiles = range(NT) if dr == 0 else range(NT - 1, -1, -1)
                for it in tiles:
                    t0 = it * 512
                    cols = slice(t0, t0 + 512)
                    for nm in ["r", "k", "v", "d", "a"] + (["g"] if dr == 1 else []):
                        rb = raw[nm]
                        lo, hi = max(t0 - 1, 0), min(t0 + 513, S)
                        dlo = lo - (t0 - 1)
                        if t0 == 0:
                            P.memset("pool", rb[:, 0:1], 0.0, writes=[("raw", nm)])
                        if t0 + 512 == S:
                            P.memset("pool", rb[:, 513:514], 0.0, writes=[("raw", nm)])
                        P.load("sp", rb[:, dlo:dlo + (hi - lo)], FEAT[rows[nm]:rows[nm] + 128, lo:hi], writes=[("raw", nm)])
                    mix("r", mucol["r"], "zr"); mix("k", mucol["k"], "zk"); mix("v", mucol["v"], "zv")
                    mix("d", mucol["d"], "zd"); mix("a", mucol["a"], "za")
                    P.act(W["zd"][:], W["zd"][:], AF.Tanh, reads=["zd"], writes=["zd"])
                    pa, pk = full()
                    P.mm(pa, w2s[drs, hcols], W["zd"][drs, :], reads=["w2s", "zd"], writes=pk)
                    P.act(W["sgm"][:], pa, AF.Sigmoid, bias=vecs[:, dr * 2 + hp:dr * 2 + hp + 1], reads=pk + ["vecs"], writes=["sgm"])
                    pa, pk = full()
                    P.mm(pa, a2s[drs, hcols], W["za"][drs, :], reads=["a2s", "za"], writes=pk)
                    P.act(W["aa"][:], pa, AF.Sigmoid, bias=vecs[:, 4 + dr * 2 + hp:5 + dr * 2 + hp], reads=pk + ["vecs"], writes=["aa"])
                    P.ts("dve", W["kkr"][:], W["zk"][:], vecs[:, 8 + hp:9 + hp], None, ALU.mult, reads=["zk", "vecs"], writes=["kkr"])
                    tt("pool", "sq", "kkr", "kkr", ALU.mult)
                    pa, pk = full()
                    P.mm(pa, blk1, W["sq"][:], reads=["csts", "sq"], writes=pk)
                    P.act(W["nrm"][:], pa, AF.Sqrt, reads=pk, writes=["nrm"])
                    P.ts("dve", W["nrm"][:], W["nrm"][:], 1e-12, None, ALU.max, reads=["nrm"], writes=["nrm"])
                    P.op("dve", lambda e: e.reciprocal(W["nrm"][:], W["nrm"][:]), reads=["nrm"], writes=["nrm"])
                    tt("dve", "kkn", "kkr", "nrm", ALU.mult)
                    P.ts("pool", W["tk"][:], W["aa"][:], vecs[:, 10 + hp:11 + hp], omka[:, hp:hp + 1], ALU.mult, ALU.add,
                         reads=["aa", "vecs", "omka"], writes=["tk"])
                    tt("pool", "kd", "tk", "zk", ALU.mult)
                    tt("pool", "bb", "kkn", "aa", ALU.mult)
                    P.op("dve", lambda e: e.tensor_tensor_scan(W["csf"][:], W["msk"][:], W["sgm"][:], 0.0, ALU.mult, ALU.add),
                         reads=["msk", "sgm"], writes=["csf"])
                    tt("pool", "cse", "csf", "sgm", ALU.subtract)
                    cends = W["csf"][:].rearrange("p (c t) -> p c t", t=64)[:, :, 63]
                    P.act(wtot[:], cends, AF.Exp, scale=-C_, reads=["csf"], writes=["wtot"])
                    wtb = wtot[:].unsqueeze(2).to_broadcast([128, 8, 64])
                    v3 = lambda ap: ap.rearrange("p (c t) -> p c t", t=64)
                    at_, rt_ = ar[:, 0, :], ar[:, 1, :]
                    if dr == 0:
                        P.act(W["E1"][:], W["cse"][:], AF.Exp, scale=-C_, reads=["cse"], writes=["E1"])
                        P.act(W["E2"][:], W["csf"][:], AF.Exp, scale=-C_, reads=["csf"], writes=["E2"])
                        P.act(W["E3"][:], W["csf"][:], AF.Exp, scale=C_, reads=["csf"], writes=["E3"])
                        P.stt(at_, W["kkn"][:], -1.0, W["E1"][:], ALU.mult, ALU.mult, reads=["kkn", "E1"], writes=["ar"])
                        P.tt("pool", rt_, W["zr"][:], W["E2"][:], ALU.mult, reads=["zr", "E2"], writes=["ar"])
                        tt("dve", "bt", "bb", "E3", ALU.mult)
                        tt("pool", "kt", "kd", "E3", ALU.mult)
                        P.tt("dve", v3(W["Bh"][:]), v3(W["bt"][:]), wtb, ALU.mult, reads=["bt", "wtot"], writes=["Bh"])
                        P.tt("pool", v3(W["Kh"][:]), v3(W["kt"][:]), wtb, ALU.mult, reads=["kt", "wtot"], writes=["Kh"])
                    else:
                        P.act(iwtot[:], cends, AF.Exp, scale=C_, reads=["csf"], writes=["iwtot"])
                        iwb = iwtot[:].unsqueeze(2).to_broadcast([128, 8, 64])
                        P.act(W["E1"][:], W["csf"][:], AF.Exp, scale=C_, reads=["csf"], writes=["E1"])
                        P.act(W["E2"][:], W["cse"][:], AF.Exp, scale=C_, reads=["cse"], writes=["E2"])
                        P.act(W["E3"][:], W["cse"][:], AF.Exp, scale=-C_, reads=["cse"], writes=["E3"])
                        P.stt(at_, W["kkn"][:], -1.0, W["E1"][:], ALU.mult, ALU.mult, reads=["kkn", "E1"], writes=["ar"])
                        P.tt("dve", v3(at_), v3(at_), wtb, ALU.mult, reads=["ar", "wtot"], writes=["ar"])
                        P.tt("pool", rt_, W["zr"][:], W["E2"][:], ALU.mult, reads=["zr", "E2", "ar"], writes=["ar"])
                        P.tt("pool", v3(rt_), v3(rt_), wtb, ALU.mult, reads=["ar", "wtot"], writes=["ar"])
                        tt("dve", "Bh", "bb", "E3", ALU.mult)
                        tt("pool", "Kh", "kd", "E3", ALU.mult)
                        P.tt("dve", v3(W["bt"][:]), v3(W["Bh"][:]), iwb, ALU.mult, reads=["Bh", "iwtot"], writes=["bt"])
                        P.tt("pool", v3(W["kt"][:]), v3(W["Kh"][:]), iwb, ALU.mult, reads=["Kh", "iwtot"], writes=["kt"])
                    blocks = list(range(4)) if dr == 0 else list(range(3, -1, -1))
                    HH = (0, 1)
                    hsl = [slice(hh * 64, (hh + 1) * 64) for hh in HH]
                    UN = [(bp, hh) for bp in (0, 1) for hh in HH]
                    for pi in range(2):
                        bpair = (blocks[2 * pi], blocks[2 * pi + 1])
                        cbs = [slice(b_ * 128, (b_ + 1) * 128) for b_ in bpair]
                        for bp in (0, 1):
                            cb = cbs[bp]
                            qa, qk = quarter()
                            P.mm(qa, ar[:, 0, cb], identf, reads=["ar", "csts"], writes=qk)
                            for hh in HH:
                                P.copy(cx.evac_eng(), X[bp][hh][0][:, 0:64], qa[:, hh * 64:(hh + 1) * 64], reads=qk, writes=[("X", bp, hh, 0)])
                            qa, qk = quarter()
                            P.mm(qa, W["zv"][:, cb], identf, reads=["zv", "csts"], writes=qk)
                            for hh in HH:
                                P.copy(cx.evac_eng(), Vpad[bp][hh][:, 64:128], qa[:, hh * 64:(hh + 1) * 64], reads=qk, writes=[("Vpad", bp, hh)])
                            qa, qk = quarter()
                            P.mm(qa, W["Bh"][:, cb], identf, reads=["Bh", "csts"], writes=qk)
                            P.copy(cx.evac_eng(), Bhtm[bp][:], qa, reads=qk, writes=[("Bhtm", bp)])
                            qa, qk = quarter()
                            P.mm(qa, W["Kh"][:, cb], identf, reads=["Kh", "csts"], writes=qk)
                            P.copy(cx.evac_eng(), Khtm[bp][:], qa, reads=qk, writes=[("Khtm", bp)])
                        pend = {}
                        for u in UN:
                            bp, hh = u
                            ha, hk = half()
                            P.mm(ha, W["bt"][hsl[hh], cbs[bp]], ar[hsl[hh], :, cbs[bp]], reads=["bt", "ar"], writes=hk)
                            pend[u] = (ha, hk)
                        for u in UN:
                            bp, hh = u
                            ha, hk = pend[u]
                            P.tt("dve", LRs[bp][hh][:], ha, m1s[:, dr, :], ALU.mult, reads=hk + ["m1s"], writes=[("LRs", bp, hh)])
                        for u in UN:
                            bp, hh = u
                            ha, hk = half()
                            P.mm(ha, W["kt"][hsl[hh], cbs[bp]], ar[hsl[hh], :, cbs[bp]], reads=["kt", "ar"], writes=hk)
                            pend[u] = (ha, hk)
                        for u in UN:
                            bp, hh = u
                            ha, hk = pend[u]
                            P.tt("dve", KRs[bp][hh][:], ha, m1s[:, dr, :], ALU.mult, reads=hk + ["m1s"], writes=[("KRs", bp, hh)])
                        for u in UN:
                            bp, hh = u
                            qa, qk = quarter()
                            P.mm(qa, ar[hsl[hh], 0, cbs[bp]], W["bt"][hsl[hh], cbs[bp]], reads=["bt", "ar"], writes=qk)
                            pend[u] = (qa, qk)
                        for u in UN:
                            bp, hh = u
                            qa, qk = pend[u]
                            P.tt("dve", P0s[bp][hh][:], qa, m2s[:, dr, :], ALU.mult, reads=qk + ["m2s"], writes=[("P0s", bp, hh)])
                        for u in UN:
                            bp, hh = u
                            qa, qk = quarter()
                            P.mm(qa[:, 0:64], KRs[bp][hh][:, 0:128], Vpad[bp][hh][:, 64:128], reads=[("KRs", bp, hh), ("Vpad", bp, hh)], writes=qk)
                            pend[u] = (qa, qk)
                        for u in UN:
                            bp, hh = u
                            qa, qk = pend[u]
                            P.copy("act", X[bp][hh][0][:, 64:128], qa[:, 0:64], reads=qk, writes=[("X", bp, hh, 0)])
                        L_ap = {u: LRs[u[0]][u[1]][:, 0:128] for u in UN}
                        P_ap = {u: P0s[u[0]][u[1]][:] for u in UN}
                        Lk = {u: [("LRs",) + u] for u in UN}
                        Pk = {u: [("P0s",) + u] for u in UN}
                        xc = 0
                        for lvl in range(6):
                            for u in UN:
                                bp, hh = u
                                qa, qk = quarter()
                                P.mm(qa, L_ap[u], X[bp][hh][xc][:], reads=Lk[u] + [("X", bp, hh, xc)], writes=qk)
                                pend[u] = (qa, qk)
                            for u in UN:
                                bp, hh = u
                                qa, qk = pend[u]
                                P.tt("dve", X[bp][hh][1 - xc][:], qa, X[bp][hh][xc][:], ALU.add, reads=qk + [("X", bp, hh, xc)],
                                     writes=[("X", bp, hh, 1 - xc)])
                            xc = 1 - xc
                            if lvl < 5:
                                par = lvl % 2
                                for u in UN:
                                    qa, qk = quarter()
                                    P.mm(qa, P_ap[u], L_ap[u], reads=Lk[u] + Pk[u], writes=qk)
                                    pend[u] = (qa, qk)
                                for u in UN:
                                    bp, hh = u
                                    qa, qk = pend[u]
                                    P.copy("act", LP[bp][hh][par][:, 0:128], qa, reads=qk, writes=[("LPa", bp, hh, par)])
                                for u in UN:
                                    qa, qk = quarter()
                                    P.mm(qa, L_ap[u], P_ap[u], reads=Lk[u] + Pk[u], writes=qk)
                                    pend[u] = (qa, qk)
                                for u in UN:
                                    bp, hh = u
                                    qa, qk = pend[u]
                                    P.copy("act" if hh == 0 else "dve", LP[bp][hh][par][:, 128:256], qa, reads=qk, writes=[("LPb", bp, hh, par)])
                                for u in UN:
                                    bp, hh = u
                                    L_ap[u], P_ap[u] = LP[bp][hh][par][:, 0:128], LP[bp][hh][par][:, 128:256]
                                    Lk[u] = [("LPa", bp, hh, par)]
                                    Pk[u] = [("LPb", bp, hh, par)]
                        Xf = {u: X[u[0]][u[1]][xc] for u in UN}
                        Xfk = {u: [("X", u[0], u[1], xc)] for u in UN}
                        for u in UN:
                            bp, hh = u
                            ga, gk = quarter()
                            P.mm(ga, Xf[u][:], LRs[bp][hh][:, 128:256], start=True, stop=False, reads=Xfk[u] + [("LRs", bp, hh)], writes=gk)
                            P.mm(ga, Vpad[bp][hh][:], KRs[bp][hh][:, 128:256], start=False, stop=True, reads=[("Vpad", bp, hh), ("KRs", bp, hh)], writes=gk)
                            pend[u] = (ga, gk)
                        for u in UN:
                            bp, hh = u
                            ga, gk = pend[u]
                            P.tt("dve", RpT[bp][hsl[hh], :], ga[0:64, :], ar[hsl[hh], 1, cbs[bp]], ALU.add, reads=gk + ["ar"], writes=[("RpT", bp, hh)])
                            P.copy("act", W["ytile"][hsl[hh], cbs[bp]], ga[64:128, :], reads=gk, writes=[("ytile", hh)])
                        for bp in (0, 1):
                            bidx = bpair[bp]
                            for c in ((0, 1) if dr == 0 else (1, 0)):
                                cc = slice(c * 64, (c + 1) * 64)
                                ycols = slice(bidx * 128 + c * 64, bidx * 128 + (c + 1) * 64)
                                wcol = bidx * 2 + c
                                pd = []
                                for hh in HH:
                                    hs = hsl[hh]
                                    qa, qk = quarter()
                                    P.mm(qa[hs, 0:64], ST[hs, :], RpT[bp][hs, cc], reads=[("ST", hh), ("RpT", bp, hh)], writes=qk)
                                    qb, qkb = quarter()
                                    u = (bp, hh)
                                    P.mm(qb[hs, 0:64], Xf[u][cc, 0:64], Bhtm[bp][cc, hs], reads=Xfk[u] + [("Bhtm", bp)], writes=qkb)
                                    P.mm(qb[hs, 64:128], Bhtm[bp][cc, hs], Xf[u][cc, 64:128], start=True, stop=False, reads=Xfk[u] + [("Bhtm", bp)], writes=qkb)
                                    P.mm(qb[hs, 64:128], Khtm[bp][cc, hs], Vpad[bp][hh][cc, 64:128], start=False, stop=True,
                                         reads=[("Khtm", bp), ("Vpad", bp, hh)], writes=qkb)
                                    pd.append((qa, qk, qb, qkb))
                                for hh in HH:
                                    hs = hsl[hh]
                                    qa, qk, qb, qkb = pd[hh]
                                    P.tt("dve", W["ytile"][hs, ycols], qa[hs, 0:64], W["ytile"][hs, ycols], ALU.add,
                                         reads=qk + [("ytile", hh)], writes=[("ytile", hh)])
                                    P.stt(MTs[hs, :], ident64[hs, :], wtot[hs, wcol:wcol + 1], qb[hs, 0:64], ALU.mult, ALU.add,
                                          reads=qkb + ["csts", "wtot"], writes=[("MTs", hh)])
                                    P.copy("act", Nns[hs, :], qb[hs, 64:128], reads=qkb, writes=[("Nns", hh)])
                                pd = []
                                for hh in HH:
                                    hs = hsl[hh]
                                    qa, qk = quarter()
                                    P.mm(qa[hs, 0:64], MTs[hs, :], ST[hs, :], reads=[("MTs", hh), ("ST", hh)], writes=qk)
                                    pd.append((qa, qk))
                                for hh in HH:
                                    hs = hsl[hh]
                                    qa, qk = pd[hh]
                                    P.tt("dve", ST[hs, :], qa[hs, 0:64], Nns[hs, :], ALU.add, reads=qk + [("Nns", hh)], writes=[("ST", hh)])
                    if dr == 0:
                        P.load("sp", YF[hp * 128:(hp + 1) * 128, cols], W["ytile"][:], reads=[("ytile", 0), ("ytile", 1)], writes=["YF"])
                    else:
                        P.load("sp", W["yf"][:], YF[hp * 128:(hp + 1) * 128, cols], reads=["YF"], writes=["yf"])
                        P.tt("pool", W["yf"][:], W["ytile"][:], W["yf"][:], ALU.add, reads=[("ytile", 0), ("ytile", 1), "yf"], writes=["yf"])
                        pa, pk = full()
                        P.mm(pa, blkm, W["yf"][:], reads=["csts", "yf"], writes=pk)
                        P.tt("dve", W["yc"][:], W["yf"][:], pa, ALU.subtract, reads=pk + ["yf"], writes=["yc"])
                        tt("pool", "sq", "yc", "yc", ALU.mult)
                        pa, pk = full()
                        P.mm(pa, blkm, W["sq"][:], reads=["csts", "sq"], writes=pk)
                        P.act(W["nrm"][:], pa, AF.Sqrt, bias=64e-5, reads=pk, writes=["nrm"])
                        P.op("dve", lambda e: e.reciprocal(W["nrm"][:], W["nrm"][:]), reads=["nrm"], writes=["nrm"])
                        tt("pool", "yn", "yc", "nrm", ALU.mult)
                        P.ts("pool", W["yn"][:], W["yn"][:], vecs[:, 14 + hp:15 + hp], vecs[:, 16 + hp:17 + hp], ALU.mult, ALU.add,
                             reads=["yn", "vecs"], writes=["yn"])
                        pa, pk = full()
                        P.mm(pa, a2s[0:64, hcols], W["za"][0:64, :], reads=["a2s", "za"], writes=pk)
                        P.act(W["aa0"][:], pa, AF.Sigmoid, bias=vecs[:, 4 + hp:5 + hp], reads=pk + ["vecs"], writes=["aa0"])
                        P.ts("pool", W["tk0"][:], W["aa0"][:], vecs[:, 10 + hp:11 + hp], omka[:, hp:hp + 1], ALU.mult, ALU.add,
                             reads=["aa0", "vecs", "omka"], writes=["tk0"])
                        tt("pool", "tk0", "tk0", "tk", ALU.add)
                        tt("pool", "tk0", "tk0", "zk", ALU.mult)
                        P.stt(W["tmpb"][:], W["zr"][:], vecs[:, 12 + hp:13 + hp], W["tk0"][:], ALU.mult, ALU.mult,
                              reads=["zr", "vecs", "tk0"], writes=["tmpb"])
                        pa, pk = full()
                        P.mm(pa, blk1, W["tmpb"][:], reads=["csts", "tmpb"], writes=pk)
                        P.tt("dve", W["tmpb"][:], pa, W["zv"][:], ALU.mult, reads=pk + ["zv", "tmpb"], writes=["tmpb"])
                        tt("pool", "yn", "yn", "tmpb", ALU.add)
                        mix("g", mucol["g"], "zg")
                        P.act(W["zg"][:], W["zg"][:], AF.Sigmoid, reads=["zg"], writes=["zg"])
                        pa, pk = full()
                        P.mm(pa, g2s[:, hcols], W["zg"][:], reads=["g2s", "zg"], writes=pk)
                        rb_ = it % 2
                        P.tt("dve", recb[:, rb_, :], pa, W["yn"][:], ALU.mult, reads=pk + ["yn"], writes=[("recb", rb_)])
                        P.load("sp", as_chunked(REC, S).sl(hp * 128, (hp + 1) * 128, t0, t0 + 512), recb[:, rb_, :],
                               reads=[("recb", rb_)], writes=["REC"])


def build_B2(S, dbg=0):
    nc = bass.Bass("TRN2", target_bir_lowering=False)
    I = "ExternalInput"
    d = dict(FEAT=dram(nc, "FEAT", [1920, S], BF16, I), mu9=dram(nc, "mu9", [128, 9], F32, I), w2d=dram(nc, "w2d", [128, 256], F32, I),
             a2d=dram(nc, "a2d", [128, 256], F32, I), g2d=dram(nc, "g2d", [128, 256], F32, I), vec=dram(nc, "vec", [128, 20], F32, I),
             m1d=dram(nc, "m1d", [2, 128, 256], F32, I), m2d=dram(nc, "m2d", [2, 128, 128], F32, I), cst=dram(nc, "cst", [128, 448], F32, I),
             REC=dram(nc, "REC", [256, S], BF16, "ExternalOutput"), YF=dram(nc, "YF", [256, S], F32, "Internal"))
    cx = Ctx(nc)
    with cx.st:
        cx.alloc_psum()
        emit_B2(cx, d, S, "B2_", dbg)
        cnt = cx.P.emit(final_wait_keys=["REC"])
    return nc, cnt


bf = ml_dtypes.bfloat16
def rope_tables(S):
    inv = (1.0 / (np.float32(10000.0) ** (np.arange(0, 64, 2, dtype=np.float32) / np.float32(64)))).astype(np.float32)
    ang = (np.arange(S, dtype=np.float32)[:, None] * inv[None, :]).astype(np.float32)
    cos = np.cos(ang).astype(np.float32); sin = np.sin(ang).astype(np.float32)
    idx = np.arange(128) % 32
    return np.ascontiguousarray(cos[:, idx].T), np.ascontiguousarray(sin[:, idx].T)
def rot_lhsT():
    Pm = np.zeros((128, 128), np.float32)
    for p in range(128):
        if (p % 64) < 32: Pm[p, p + 32] = -1.0
        else: Pm[p, p - 32] = 1.0
    return np.ascontiguousarray(Pm.T)

def colv(v):
    return np.ascontiguousarray(np.asarray(v, np.float32).reshape(-1, 128).T)

def b2_masks():
    t = np.arange(128)[:, None]; j = np.arange(128)[None, :]
    same = (t // 64) == (j // 64)
    m1 = np.zeros((2, 128, 256), np.float32); m2 = np.zeros((2, 128, 128), np.float32)
    for dr in range(2):
        strict = same & ((j < t) if dr == 0 else (j > t))
        incl = same & ((j <= t) if dr == 0 else (j >= t))
        m2[dr] = strict
        m1[dr][:, 0:128] = strict.T
        m1[dr][:, 128:256] = incl.T
    return m1, m2

def b2_consts():
    blk1 = np.kron(np.eye(2), np.ones((64, 64))).astype(np.float32)
    ident64 = np.concatenate([np.eye(64), np.eye(64)], 0).astype(np.float32)
    return np.ascontiguousarray(np.concatenate([blk1, blk1 / 64.0, np.eye(128, dtype=np.float32), ident64], 1).astype(np.float32))

def b2_inputs(g, mu, w0, w2, a0, a2, g2, k_k, k_a, r_k, lnw, lnb):
    ch = slice(g * 256, (g + 1) * 256)
    mu_r, mu_k, mu_v = mu[0:512][ch], mu[512:1024][ch], mu[1024:1536][ch]
    mu9 = np.stack([mu_r[0:128], mu_r[128:256], mu_k[0:128], mu_k[128:256], mu_v[0:128], mu_v[128:256],
                    mu[1536:1664], mu[1664:1792], mu[1792:1920]], 1).astype(np.float32)
    w2d = np.ascontiguousarray(w2[:, :, ch].reshape(128, 256)); a2d = np.ascontiguousarray(a2[:, :, ch].reshape(128, 256))
    g2d = np.ascontiguousarray(g2[:, ch])
    cols = []
    for dr in range(2):
        for hp in range(2):
            cols.append(w0[dr][ch][hp * 128:(hp + 1) * 128])
    for dr in range(2):
        for hp in range(2):
            cols.append(a0[dr][ch][hp * 128:(hp + 1) * 128])
    rk = r_k.reshape(-1)[ch]
    for vv in (k_k[ch], k_a[ch], rk, lnw[ch], lnb[ch]):
        cols.append(vv[0:128]); cols.append(vv[128:256])
    cols.append(np.zeros(128)); cols.append(np.zeros(128))
    vec = np.ascontiguousarray(np.stack(cols, 1).astype(np.float32))
    m1, m2 = b2_masks()
    return dict(mu9=np.ascontiguousarray(mu9), w2d=w2d.astype(np.float32), a2d=a2d.astype(np.float32), g2d=g2d.astype(np.float32),
                vec=vec, m1d=m1, m2d=m2, cst=b2_consts())


_PROGS = {}
_RUN = [None]


def _prog(name, fn, arg):
    key = (name, arg)
    if key not in _PROGS:
        _PROGS[key] = fn(arg)[0]
    return _PROGS[key]


def _launch(nc, in_maps):
    if _RUN[0] is not None:
        return _RUN[0](nc, in_maps)
    res = run_bass_kernel_spmd(nc, in_maps, core_ids=list(range(len(in_maps))))
    return res.results


def _in_perm():
    cols = []
    for g in range(2):
        for base in (0, 512, 1024):
            cols += list(range(base + g * 256, base + (g + 1) * 256))
        for base in (1536, 1536 + 512, 1536 + 1024):
            cols += list(range(base + g * 256, base + (g + 1) * 256))
    cols += list(range(1536 + 1536, 1536 + 1920))
    return np.array(cols)


def kernel_unfused(x, c, w_ada, b_ada, norm1, norm2, w_in, w_out, lam_q1, lam_k1, lam_q2, lam_k2,
           subln_w, tshift_mu, decay_w0, decay_w2, icl_a0, icl_a2, gate_g2, k_k, k_a, r_k,
           lnx_w, lnx_b, w_up, w_down, norm_f):
    f32 = np.float32
    A_ = lambda a: np.ascontiguousarray(np.asarray(a, f32))
    x = np.asarray(x, f32)
    B, S, Dm = x.shape
    T = S // 2
    L = np.asarray(w_in).shape[0]
    ncores = 2 * B
    ncA = _prog("A", build_A, T); ncB1 = _prog("B1", build_B1, S); ncB2 = _prog("B2", build_B2, S); ncC = _prog("C", build_C, T)
    xT = [np.ascontiguousarray(x[cid // 2, (cid % 2) * T:(cid % 2 + 1) * T, :].T) for cid in range(ncores)]
    ccol = [np.ascontiguousarray(np.repeat(colv(np.asarray(c, f32)[b])[:, :, None], 2, axis=2)) for b in range(B)]
    perm = _in_perm()
    cosT, sinT = rope_tables(S)
    prot = rot_lhsT().astype(bf)
    identb = np.eye(128, dtype=f32).astype(bf)
    outT = None
    for l in range(L):
        lam_init = f32(0.8 - 0.6 * np.exp(-0.3 * l))
        wa = np.asarray(w_ada[l], f32); ba = np.asarray(b_ada[l], f32)
        winp = np.ascontiguousarray(np.asarray(w_in[l], f32)[:, perm])
        wadaA = np.ascontiguousarray(wa[:, 0:2048]); badaA = colv(ba[0:2048]); n1 = colv(np.asarray(norm1[l], f32))
        resA = _launch(ncA, [dict(xT=xT[cid], ccol=ccol[cid // 2], wada=wadaA, bada=badaA, n1=n1, win=winp) for cid in range(ncores)])
        ZT = [np.asarray(r["ZT"]) for r in resA]
        feats = []
        for cid in range(ncores):
            b, g = cid // 2, cid % 2
            parts = []
            for j in range(2):
                z = ZT[2 * b + j]
                parts.append(np.concatenate([z[g * 1536:(g + 1) * 1536], z[3072:3456]], axis=0))
            feats.append(np.ascontiguousarray(np.concatenate(parts, axis=1)))
        lam4 = np.stack([np.asarray(v[l], f32) for v in (lam_q1, lam_k1, lam_q2, lam_k2)], 0)
        lamv = np.ascontiguousarray(np.broadcast_to(lam4[None], (128, 4, 64)))
        laminit = np.full((128, 1), lam_init, f32)
        sw = np.ascontiguousarray(np.broadcast_to(np.asarray(subln_w[l], f32)[None], (128, 128)))
        resB1 = _launch(ncB1, [dict(FEAT=feats[cid], cosT=cosT, sinT=sinT, prot=prot, ident=identb, lamv=lamv, laminit=laminit, sw=sw)
                               for cid in range(ncores)])
        b2in = [b2_inputs(g, np.asarray(tshift_mu[l], f32), np.asarray(decay_w0[l], f32), np.asarray(decay_w2[l], f32),
                          np.asarray(icl_a0[l], f32), np.asarray(icl_a2[l], f32), np.asarray(gate_g2[l], f32),
                          np.asarray(k_k[l], f32), np.asarray(k_a[l], f32), np.asarray(r_k[l], f32),
                          np.asarray(lnx_w[l], f32), np.asarray(lnx_b[l], f32)) for g in range(2)]
        resB2 = _launch(ncB2, [dict(FEAT=feats[cid], **b2in[cid % 2]) for cid in range(ncores)])
        ATT = [np.asarray(r["ATT"]) for r in resB1]
        REC = [np.asarray(r["REC"]) for r in resB2]
        wadaC = np.ascontiguousarray(wa[:, 2048:6144]); badaC = colv(ba[2048:6144])
        n2 = colv(np.asarray(norm2[l], f32)); nf = colv(np.asarray(norm_f, f32))
        wo = A_(w_out[l]); wu = A_(w_up[l]); wd = A_(w_down[l])
        mapsC = []
        for cid in range(ncores):
            b, j = cid // 2, cid % 2
            ts_ = slice(j * T, (j + 1) * T)
            cat = np.ascontiguousarray(np.concatenate([ATT[2 * b][:, ts_], ATT[2 * b + 1][:, ts_], REC[2 * b][:, ts_], REC[2 * b + 1][:, ts_]], axis=0))
            mapsC.append(dict(xT=xT[cid], cat=cat, ccol=ccol[b], wada=wadaC, bada=badaC, n2=n2, nf=nf, wout=wo, wup=wu, wdn=wd))
        resC = _launch(ncC, mapsC)
        xT = [np.ascontiguousarray(np.asarray(r["XTo"], f32)) for r in resC]
        outT = [np.asarray(r["OUTT"], f32) for r in resC]
    out = np.empty((B, S, Dm), f32)
    for cid in range(ncores):
        b, j = cid // 2, cid % 2
        out[b, j * T:(j + 1) * T, :] = outT[cid].T
    return out


PAIRS = [[0, 1], [2, 3], [4, 5], [6, 7]]


def build_fused(S, L=2):
    nc = bass.Bass("TRN2", target_bir_lowering=False)
    I = "ExternalInput"
    g = {}
    g["x0"] = dram(nc, "x0", [D, S], F32, I)
    g["ccol"] = dram(nc, "ccol", [128, 8, 2], F32, I)
    for nm, shp, dt in (("cosT", [128, S], F32), ("sinT", [128, S], F32), ("prot", [128, 128], BF16), ("ident", [128, 128], BF16),
                        ("m1d", [2, 128, 256], F32), ("m2d", [2, 128, 128], F32), ("cst", [128, 448], F32), ("nf", [128, 8], F32)):
        g[nm] = dram(nc, nm, shp, dt, I)
    per = []
    for l in range(L):
        p = {}
        for nm, shp, dt in (("wadaA", [D, 2048], F32), ("badaA", [128, 16], F32), ("n1", [128, 8], F32), ("win", [D, 1920], F32),
                            ("lamv", [128, 4, 64], F32), ("laminit", [128, 1], F32), ("sw", [128, 128], F32),
                            ("mu9", [128, 9], F32), ("w2d", [128, 256], F32), ("a2d", [128, 256], F32), ("g2d", [128, 256], F32),
                            ("vec", [128, 20], F32),
                            ("wadaC", [D, 4096], F32), ("badaC", [128, 32], F32), ("n2", [128, 8], F32),
                            ("wout", [D, D], F32), ("wup", [D, 4096], F32), ("wdn", [4096, D], F32)):
            p[nm] = dram(nc, f"{nm}{l}", shp, dt, I)
        per.append(p)
    OUTT = dram(nc, "OUTT", [D, S], F32, "ExternalOutput")
    XTs = dram(nc, "XTs", [D, S], F32, "Internal")
    FEAT = dram(nc, "FEATs", [1920, S], BF16, "Internal")
    YF = dram(nc, "YFs", [256, S], F32, "Internal")
    wupB = dram(nc, "wupBs", [D, 4096], BF16, "Internal")
    CW = min(1024, S)
    NCH = S // CW
    CATg = [nc.dram_tensor(f"CATg{k}", [512, CW], BF16).ap() for k in range(NCH)]
    CATALL = [nc.dram_tensor(f"CATALL{k}", [1024, CW], BF16).ap() for k in range(NCH)]
    catg_att = Chunked([a[0:256, :] for a in CATg], CW)
    catg_rec = Chunked([a[256:512, :] for a in CATg], CW)
    catall = Chunked(CATALL, CW)
    cx = Ctx(nc)
    P = cx.P
    with cx.st:
        cx.alloc_psum()
        for l in range(L):
            p = per[l]
            emit_A(cx, dict(xT=(g["x0"] if l == 0 else XTs), ccol=g["ccol"], wada=p["wadaA"], bada=p["badaA"], n1=p["n1"],
                            win=p["win"], ZT=FEAT), S, 15, f"L{l}A_")
            emit_B1(cx, dict(FEAT=FEAT, cosT=g["cosT"], sinT=g["sinT"], prot=g["prot"], ident=g["ident"], lamv=p["lamv"],
                             laminit=p["laminit"], sw=p["sw"], ATT=catg_att), S, f"L{l}B1_")
            emit_B2(cx, dict(FEAT=FEAT, mu9=p["mu9"], w2d=p["w2d"], a2d=p["a2d"], g2d=p["g2d"], vec=p["vec"], m1d=g["m1d"],
                             m2d=g["m2d"], cst=g["cst"], REC=catg_rec, YF=YF), S, f"L{l}B2_")
            P.fence()
            for k in range(NCH):
                P.coll(lambda e, k=k: e.collective_compute("AllGather", ALU.bypass, PAIRS, ins=[CATg[k].opt()], outs=[CATALL[k].opt()]),
                       reads=["ATT", "REC"], writes=["CATALL"])
            P.fence()
            emit_C(cx, dict(xT=(g["x0"] if l == 0 else XTs), cat=catall, ccol=g["ccol"], wada=p["wadaC"], bada=p["badaC"],
                            n2=p["n2"], nf=g["nf"], wout=p["wout"], wup=p["wup"], wdn=p["wdn"], XTo=XTs, OUTT=OUTT, wupB=wupB),
                   S, f"L{l}C_", final=(l == L - 1))
        cnt = P.emit(final_wait_keys=["OUTT"])
    return nc, cnt


def kernel(x, c, w_ada, b_ada, norm1, norm2, w_in, w_out, lam_q1, lam_k1, lam_q2, lam_k2,
                 subln_w, tshift_mu, decay_w0, decay_w2, icl_a0, icl_a2, gate_g2, k_k, k_a, r_k,
                 lnx_w, lnx_b, w_up, w_down, norm_f):
    f32 = np.float32
    A_ = lambda a: np.ascontiguousarray(np.asarray(a, f32))
    x = np.asarray(x, f32)
    B, S, Dm = x.shape
    L = np.asarray(w_in).shape[0]
    ncores = 2 * B
    ncF = _prog("F", lambda s_: build_fused(s_, L), S)
    cosT, sinT = rope_tables(S)
    m1, m2 = b2_masks()
    shared = dict(cosT=cosT, sinT=sinT, prot=rot_lhsT().astype(bf), ident=np.eye(128, dtype=f32).astype(bf),
                  m1d=m1, m2d=m2, cst=b2_consts(), nf=colv(np.asarray(norm_f, f32)))
    perm = _in_perm()
    wo_rows = np.concatenate([np.arange(0, 256), np.arange(512, 768), np.arange(256, 512), np.arange(768, 1024)])
    lay = []
    for l in range(L):
        wa = np.asarray(w_ada[l], f32); ba = np.asarray(b_ada[l], f32)
        winp = np.asarray(w_in[l], f32)[:, perm]
        lam4 = np.stack([np.asarray(v[l], f32) for v in (lam_q1, lam_k1, lam_q2, lam_k2)], 0)
        com = dict(wadaA=np.ascontiguousarray(wa[:, 0:2048]), badaA=colv(ba[0:2048]), n1=colv(np.asarray(norm1[l], f32)),
                   lamv=np.ascontiguousarray(np.broadcast_to(lam4[None], (128, 4, 64))),
                   laminit=np.full((128, 1), f32(0.8 - 0.6 * np.exp(-0.3 * l)), f32),
                   sw=np.ascontiguousarray(np.broadcast_to(np.asarray(subln_w[l], f32)[None], (128, 128))),
                   wadaC=np.ascontiguousarray(wa[:, 2048:6144]), badaC=colv(ba[2048:6144]), n2=colv(np.asarray(norm2[l], f32)),
                   wout=np.ascontiguousarray(np.asarray(w_out[l], f32)[wo_rows, :]), wup=A_(w_up[l]), wdn=A_(w_down[l]))
        pg = []
        for g_ in range(2):
            dd = dict(com)
            dd["win"] = np.ascontiguousarray(np.concatenate([winp[:, g_ * 1536:(g_ + 1) * 1536], winp[:, 3072:3456]], axis=1))
            b2 = b2_inputs(g_, np.asarray(tshift_mu[l], f32), np.asarray(decay_w0[l], f32), np.asarray(decay_w2[l], f32),
                           np.asarray(icl_a0[l], f32), np.asarray(icl_a2[l], f32), np.asarray(gate_g2[l], f32),
                           np.asarray(k_k[l], f32), np.asarray(k_a[l], f32), np.asarray(r_k[l], f32),
                           np.asarray(lnx_w[l], f32), np.asarray(lnx_b[l], f32))
            for k_ in ("mu9", "w2d", "a2d", "g2d", "vec"):
                dd[k_] = b2[k_]
            pg.append(dd)
        lay.append(pg)
    xTb = [np.ascontiguousarray(x[b].T) for b in range(B)]
    in_maps = []
    for cid in range(ncores):
        b, g_ = cid // 2, cid % 2
        m = dict(shared)
        m["x0"] = xTb[b]
        m["ccol"] = np.ascontiguousarray(np.repeat(colv(np.asarray(c, f32)[b])[:, :, None], 2, axis=2))
        for l in range(L):
            for k_, v_ in lay[l][g_].items():
                m[f"{k_}{l}"] = v_
        in_maps.append(m)
    res = _launch(ncF, in_maps)
    out = np.empty((B, S, Dm), f32)
    T = S // 2
    for cid in range(ncores):
        b, j = cid // 2, cid % 2
        out[b, j * T:(j + 1) * T, :] = np.asarray(res[cid]["OUTT"], f32)[:, j * T:(j + 1) * T].T
    return out
```
